# Optimizing a Trainium2 kernel written in Bass

```python
import jax, jax.numpy as jnp
from jax import lax
import numpy as np

D_MODEL = 1024
BATCH = 8
SEQ = 2048
DEPTH = 2
DEC_BATCH = 128
DEC_SEQ = 8
PAST_LEN = 16384
PAGE_SIZE = 128

D_PLE = 256
D_FF = 2816
GROUP_W = D_MODEL // 4
N_HEADS_PER_GROUP = 4
HEAD_DIM = GROUP_W // N_HEADS_PER_GROUP
MIX_IN_W = 8 * GROUP_W
CONV_A_WIDTH = 31
POOL_WINDOWS = (2, 4, 8, 16)
POOL_STATE = max(POOL_WINDOWS) - 1
CHUNK = 128
SHORT_CONV_WIDTH = 3
EPS = 1e-6

kernel_name = "hybrid_conv_pool_sgu_shortconv_decoder_step"


def _rms_norm(x, g):
    xf = x.astype(jnp.float32)
    y = xf * lax.rsqrt(jnp.mean(xf * xf, axis=-1, keepdims=True) + EPS)
    return (y * g.astype(jnp.float32)).astype(x.dtype)


def _head_layer_norm(x, g, b):
    shp = x.shape
    xf = x.astype(jnp.float32).reshape(shp[:-1] + (N_HEADS_PER_GROUP, HEAD_DIM))
    mu = jnp.mean(xf, axis=-1, keepdims=True)
    xc = xf - mu
    var = jnp.mean(xc * xc, axis=-1, keepdims=True)
    y = (xc * lax.rsqrt(var + EPS)).reshape(shp)
    return (y * g.astype(jnp.float32) + b.astype(jnp.float32)).astype(x.dtype)


def _swiglu(x, w_gate, w_up, w_down):
    return (jax.nn.silu(x @ w_gate) * (x @ w_up)) @ w_down


def _causal_depthwise_conv(x, prev, w):
    xp = jnp.concatenate([prev.astype(x.dtype), x], axis=1)
    y = lax.conv_general_dilated(xp, w[:, None, :].astype(x.dtype), window_strides=(1,),
                                 padding='VALID', dimension_numbers=('NWC', 'WIO', 'NWC'),
                                 feature_group_count=x.shape[-1])
    return y, xp[:, -(w.shape[0] - 1):]


def _conv_module(a_val, a_gate, prev, w, b, ng, nb):
    h = a_val * jax.nn.sigmoid(a_gate)
    c, new_state = _causal_depthwise_conv(h, prev, w)
    c = _head_layer_norm(c + b, ng, nb)
    return jax.nn.silu(c), new_state


def _multiscale_pool(z, prev, start, w_pool, scale):
    B_, L, _ = z.shape
    zp = jnp.concatenate([prev.astype(z.dtype), z], axis=1)
    cs = jnp.pad(jnp.cumsum(zp.astype(jnp.float32), axis=1), ((0, 0), (1, 0), (0, 0)))
    pos = start + jnp.arange(L)
    means = []
    for g, win in enumerate(POOL_WINDOWS):
        sl = slice(g * HEAD_DIM, (g + 1) * HEAD_DIM)
        hi = cs[:, POOL_STATE + 1:POOL_STATE + 1 + L, sl]
        lo = cs[:, POOL_STATE + 1 - win:POOL_STATE + 1 - win + L, sl]
        cnt = jnp.minimum(pos + 1, win).astype(jnp.float32)[None, :, None]
        means.append((hi - lo) / cnt)
    mean = jnp.stack(means, axis=2)
    d = (mean - z.astype(jnp.float32).reshape(B_, L, N_HEADS_PER_GROUP, HEAD_DIM)).astype(z.dtype)
    y = jnp.einsum('blgc,gcd->blgd', d, w_pool).reshape(B_, L, GROUP_W)
    return y * scale, zp[:, -POOL_STATE:]


def _spatial_gating(u, v, ng, nb, w_s, b_s):
    v = _head_layer_norm(v, ng, nb)
    B_, L, C = v.shape
    cs = min(L, CHUNK)
    Lp = -(-L // cs) * cs
    vb = jnp.pad(v, ((0, 0), (0, Lp - L), (0, 0))).reshape(B_, Lp // cs, cs, N_HEADS_PER_GROUP, HEAD_DIM)
    w = jnp.tril(w_s[:, :cs, :cs])
    s = jnp.einsum('hts,bnshc->bnthc', w, vb) + b_s[:, :cs].T[None, None, :, :, None]
    s = s.reshape(B_, Lp, C)[:, :L]
    return u * s, v


def _short_gated_conv(b_gate, c_gate, h, prev, w):
    q = c_gate * h
    y, new_state = _causal_depthwise_conv(q, prev, w)
    return b_gate * y, new_state


def _trunk(x, p, st_a, st_pool, st_sc, start, wt):
    B_ = x.shape[0]
    new_a, new_pool, new_sc, new_v = [], [], [], []
    for i in range(DEPTH):
        h = _rms_norm(x, wt['norm_ffn1'][i])
        x = x + 0.5 * _swiglu(h, wt['w_ffn1_gate'][i], wt['w_ffn1_up'][i], wt['w_ffn1_down'][i])

        h = _rms_norm(x, wt['norm_mix'][i])
        z = h @ wt['w_mix_in'][i]
        a_val, a_gate, zb, u, v, b_gate, c_gate, d_in = jnp.split(z, 8, axis=-1)
        prev_a = jnp.zeros((B_, CONV_A_WIDTH - 1, GROUP_W), x.dtype) if st_a is None else st_a[i]
        prev_p = jnp.zeros((B_, POOL_STATE, GROUP_W), x.dtype) if st_pool is None else st_pool[i]
        prev_s = jnp.zeros((B_, SHORT_CONV_WIDTH - 1, GROUP_W), x.dtype) if st_sc is None else st_sc[i]

        y_a, s_a = _conv_module(a_val, a_gate, prev_a, wt['conv_a_w'][i], wt['conv_a_b'][i],
                                wt['norm_a_g'][i], wt['norm_a_b'][i])
        y_b, s_p = _multiscale_pool(zb, prev_p, start, wt['pool_w'][i], wt['pool_scale'][i])
        y_c, v_n = _spatial_gating(jax.nn.gelu(u), jax.nn.gelu(v), wt['sgu_norm_g'][i],
                                   wt['sgu_norm_b'][i], wt['sgu_w'][i], wt['sgu_b'][i])
        y_d, s_s = _short_gated_conv(b_gate, c_gate, d_in, prev_s, wt['short_conv_w'][i])
        x = x + jnp.concatenate([y_a, y_b, y_c, y_d], axis=-1) @ wt['w_mix_out'][i]

        h = _rms_norm(x, wt['norm_ffn2'][i])
        x = x + 0.5 * _swiglu(h, wt['w_ffn2_gate'][i], wt['w_ffn2_up'][i], wt['w_ffn2_down'][i])

        gate = jax.nn.sigmoid(_rms_norm(x, wt['norm_ple'][i]) @ wt['w_ple_gate'][i])
        x = x + gate * (p[i] @ wt['w_ple_proj'][i])

        new_a.append(s_a); new_pool.append(s_p); new_sc.append(s_s); new_v.append(v_n)
    y = _rms_norm(x, wt['norm_final'])
    return y, jnp.stack(new_a), jnp.stack(new_pool), jnp.stack(new_sc), jnp.stack(new_v)


def setup_inputs(seed: int = 0) -> dict:
    key = jax.random.key(seed)
    ks = iter(jax.random.split(key, 40))

    def nrm(shape, scale):
        return scale * jax.random.normal(next(ks), shape, jnp.float32)

    def gain(shape):
        return 1.0 + nrm(shape, 0.01)

    L = DEPTH
    return {
        'x_prompt': nrm((BATCH, SEQ, D_MODEL), 1.0),
        'x_sample': nrm((DEC_BATCH, DEC_SEQ, D_MODEL), 1.0),
        'p_prompt': nrm((DEPTH, BATCH, SEQ, D_PLE), 1.0),
        'p_sample': nrm((DEPTH, DEC_BATCH, DEC_SEQ, D_PLE), 1.0),
        'state_conv_a': nrm((DEPTH, DEC_BATCH, CONV_A_WIDTH - 1, GROUP_W), 0.5),
        'state_pool': nrm((DEPTH, DEC_BATCH, POOL_STATE, GROUP_W), 1.0),
        'state_short_conv': nrm((DEPTH, DEC_BATCH, SHORT_CONV_WIDTH - 1, GROUP_W), 0.5),
        'norm_ffn1': gain((L, D_MODEL)),
        'w_ffn1_gate': nrm((L, D_MODEL, D_FF), D_MODEL ** -0.5),
        'w_ffn1_up': nrm((L, D_MODEL, D_FF), D_MODEL ** -0.5),
        'w_ffn1_down': nrm((L, D_FF, D_MODEL), D_FF ** -0.5),
        'norm_mix': gain((L, D_MODEL)),
        'w_mix_in': nrm((L, D_MODEL, MIX_IN_W), D_MODEL ** -0.5),
        'conv_a_w': nrm((L, CONV_A_WIDTH, GROUP_W), CONV_A_WIDTH ** -0.5),
        'conv_a_b': nrm((L, GROUP_W), 0.02),
        'norm_a_g': gain((L, GROUP_W)),
        'norm_a_b': nrm((L, GROUP_W), 0.02),
        'pool_w': nrm((L, N_HEADS_PER_GROUP, HEAD_DIM, HEAD_DIM), HEAD_DIM ** -0.5),
        'pool_scale': 1.0 + nrm((L, GROUP_W), 0.1),
        'sgu_norm_g': gain((L, GROUP_W)),
        'sgu_norm_b': nrm((L, GROUP_W), 0.02),
        'sgu_w': nrm((L, N_HEADS_PER_GROUP, CHUNK, CHUNK), CHUNK ** -0.5),
        'sgu_b': 1.0 + nrm((L, N_HEADS_PER_GROUP, CHUNK), 0.02),
        'short_conv_w': nrm((L, SHORT_CONV_WIDTH, GROUP_W), SHORT_CONV_WIDTH ** -0.5),
        'w_mix_out': nrm((L, D_MODEL, D_MODEL), D_MODEL ** -0.5),
        'norm_ffn2': gain((L, D_MODEL)),
        'w_ffn2_gate': nrm((L, D_MODEL, D_FF), D_MODEL ** -0.5),
        'w_ffn2_up': nrm((L, D_MODEL, D_FF), D_MODEL ** -0.5),
        'w_ffn2_down': nrm((L, D_FF, D_MODEL), D_FF ** -0.5),
        'norm_ple': gain((L, D_MODEL)),
        'w_ple_gate': nrm((L, D_MODEL, D_MODEL), D_MODEL ** -0.5),
        'w_ple_proj': nrm((L, D_PLE, D_MODEL), D_PLE ** -0.5),
        'norm_final': gain((D_MODEL,)),
    }


def reference(x_prompt, x_sample, p_prompt, p_sample, state_conv_a, state_pool, state_short_conv,
              norm_ffn1, w_ffn1_gate, w_ffn1_up, w_ffn1_down, norm_mix, w_mix_in,
              conv_a_w, conv_a_b, norm_a_g, norm_a_b, pool_w, pool_scale,
              sgu_norm_g, sgu_norm_b, sgu_w, sgu_b, short_conv_w, w_mix_out,
              norm_ffn2, w_ffn2_gate, w_ffn2_up, w_ffn2_down,
              norm_ple, w_ple_gate, w_ple_proj, norm_final):
    wt = dict(norm_ffn1=norm_ffn1, w_ffn1_gate=w_ffn1_gate, w_ffn1_up=w_ffn1_up, w_ffn1_down=w_ffn1_down,
              norm_mix=norm_mix, w_mix_in=w_mix_in, conv_a_w=conv_a_w, conv_a_b=conv_a_b,
              norm_a_g=norm_a_g, norm_a_b=norm_a_b, pool_w=pool_w, pool_scale=pool_scale,
              sgu_norm_g=sgu_norm_g, sgu_norm_b=sgu_norm_b, sgu_w=sgu_w, sgu_b=sgu_b,
              short_conv_w=short_conv_w, w_mix_out=w_mix_out, norm_ffn2=norm_ffn2,
              w_ffn2_gate=w_ffn2_gate, w_ffn2_up=w_ffn2_up, w_ffn2_down=w_ffn2_down,
              norm_ple=norm_ple, w_ple_gate=w_ple_gate, w_ple_proj=w_ple_proj, norm_final=norm_final)
    y_prompt, conv_a_prompt, pool_prompt, short_conv_prompt, _ = _trunk(
        x_prompt, p_prompt, None, None, None, 0, wt)
    y_sample, conv_a_sample, pool_sample, short_conv_sample, chunk_v_sample = _trunk(
        x_sample, p_sample, state_conv_a, state_pool, state_short_conv, PAST_LEN, wt)
    return (y_prompt, y_sample, conv_a_prompt, conv_a_sample, pool_prompt, pool_sample,
            short_conv_prompt, short_conv_sample, chunk_v_sample)
```

```python
import numpy as np
from contextlib import ExitStack
import concourse.bass as bass
import concourse.mybir as mybir
from concourse.bass_utils import run_bass_kernel_spmd

F32 = mybir.dt.float32
BF16 = mybir.dt.bfloat16
AF = mybir.ActivationFunctionType
ALU = mybir.AluOpType

NCORES = 8
D = 1024
KC = 8
FF = 2816
FC = 22
HALF = 11
TMAX = 1152
EPS = 1e-6
L = 2

WEIGHT_NAMES = ['norm_ffn1', 'w_ffn1_gate', 'w_ffn1_up', 'w_ffn1_down', 'norm_mix', 'w_mix_in',
                'conv_a_w', 'conv_a_b', 'norm_a_g', 'norm_a_b', 'pool_w', 'pool_scale',
                'sgu_norm_g', 'sgu_norm_b', 'sgu_w', 'sgu_b', 'short_conv_w', 'w_mix_out',
                'norm_ffn2', 'w_ffn2_gate', 'w_ffn2_up', 'w_ffn2_down',
                'norm_ple', 'w_ple_gate', 'w_ple_proj', 'norm_final']
WEIGHT_SHAPES = {
    'norm_ffn1': [2, 1024], 'w_ffn1_gate': [2, 1024, 2816], 'w_ffn1_up': [2, 1024, 2816],
    'w_ffn1_down': [2, 2816, 1024], 'norm_mix': [2, 1024], 'w_mix_in': [2, 1024, 2048],
    'conv_a_w': [2, 31, 256], 'conv_a_b': [2, 256], 'norm_a_g': [2, 256], 'norm_a_b': [2, 256],
    'pool_w': [2, 4, 64, 64], 'pool_scale': [2, 256], 'sgu_norm_g': [2, 256], 'sgu_norm_b': [2, 256],
    'sgu_w': [2, 4, 128, 128], 'sgu_b': [2, 4, 128], 'short_conv_w': [2, 3, 256],
    'w_mix_out': [2, 1024, 1024], 'norm_ffn2': [2, 1024], 'w_ffn2_gate': [2, 1024, 2816],
    'w_ffn2_up': [2, 1024, 2816], 'w_ffn2_down': [2, 2816, 1024], 'norm_ple': [2, 1024],
    'w_ple_gate': [2, 1024, 1024], 'w_ple_proj': [2, 256, 1024], 'norm_final': [1024],
}
CORE_IN_SHAPES = {
    'xp': [2048, 1024], 'xs': [128, 1024], 'pp': [2, 2048, 256], 'psm': [2, 128, 256],
    'sta': [2, 480, 256], 'stp': [2, 240, 256], 'sts': [2, 32, 256],
}
CORE_OUT_SHAPES = {
    'y_p': [2048, 1024], 'y_s': [128, 1024], 'ca_p': [2, 30, 256], 'ca_s': [2, 480, 256],
    'po_p': [2, 15, 256], 'po_s': [2, 240, 256], 'sc_p': [2, 2, 256], 'sc_s': [2, 32, 256],
    'cv_s': [2, 128, 256],
}


class Prod:
    def __init__(self, h, name):
        self.h = h
        self.count = 0
        self.name = name


class EngW:
    def __init__(self, name, eng, prod):
        self.name = name
        self.eng = eng
        self.prod = prod
        self.waited = {}


class Res:
    __slots__ = ('name', 'w', 'r', 'gen')

    def __init__(self, name):
        self.name = name
        self.w = None
        self.r = {}


class GenRes:
    __slots__ = ('base', 'gen')

    def __init__(self, base):
        self.base = base
        base.gen = getattr(base, 'gen', 0) + 1
        self.gen = base.gen

    def _chk(self):
        assert self.gen == self.base.gen, f"stale ring buffer use: {self.base.name}"
        return self.base

    @property
    def w(self):
        return self._chk().w

    @w.setter
    def w(self, v):
        self._chk().w = v

    @property
    def r(self):
        return self._chk().r

    @r.setter
    def r(self, v):
        self._chk().r = v


class Sched:
    def __init__(self, nc, es):
        self.nc = nc
        self.es = es
        self.nsem = 0
        mk = lambda n, e: EngW(n, e, self.new_prod(n))
        self.pe = mk('pe', nc.tensor)
        self.act = mk('act', nc.scalar)
        self.dve = mk('dve', nc.vector)
        self.pool = mk('pool', nc.gpsimd)
        self.sp = mk('sp', nc.sync)
        self.out_marks = []

    def new_prod(self, name):
        h = self.es.enter_context(self.nc.semaphore(name + str(self.nsem)))
        self.nsem += 1
        return Prod(h, name)

    def _deps(self, ew, reads, writes):
        deps = {}

        def add(p, c):
            if deps.get(p, 0) < c:
                deps[p] = c
        for r in reads:
            if r.w is not None:
                add(*r.w)
        for w in writes:
            if w.w is not None:
                add(*w.w)
            for p, c in w.r.items():
                if p is ew.prod and ew.name == 'pe':
                    continue
                add(p, c)
        for p, c in deps.items():
            if p is ew.prod and ew.name == 'pe':
                continue
            if ew.waited.get(p, 0) < c:
                ew.eng.wait_ge(p.h, c)
                ew.waited[p] = c
                self.nwaits = getattr(self, 'nwaits', 0) + 1
                if p is ew.prod:
                    self.nself = getattr(self, 'nself', 0) + 1

    def _mark(self, mark, reads, writes):
        p, c = mark
        for r in reads:
            if r.r.get(p, 0) < c:
                r.r[p] = c
        for w in writes:
            w.w = mark
            w.r = {}

    def op(self, ew, reads, writes, fn):
        self._deps(ew, reads, writes)
        inst = fn(ew.eng)
        ew.prod.count += 1
        inst.then_inc(ew.prod.h, 1)
        self._mark((ew.prod, ew.prod.count), reads, writes)

    def dma(self, qew, sem, items, is_output=False):
        allr, allw = [], []
        for reads, writes, fn in items:
            allr += reads
            allw += writes
        self._deps(qew, allr, allw)
        if sem.count and qew.waited.get(sem, 0) < sem.count:
            qew.eng.wait_ge(sem.h, sem.count)
            qew.waited[sem] = sem.count
        for reads, writes, fn in items:
            inst = fn(qew.eng)
            sem.count += 16
            inst.then_inc(sem.h, 16)
        self._mark((sem, sem.count), allr, allw)
        if is_output:
            self.out_marks.append(sem)


def build_program():
    nc = bass.Bass("TRN2", target_bir_lowering=False)
    din = {}
    for n, s in CORE_IN_SHAPES.items():
        din[n] = nc.dram_tensor(n, s, F32, kind="ExternalInput").ap()
    for n in WEIGHT_NAMES:
        din[n] = nc.dram_tensor(n, WEIGHT_SHAPES[n], F32, kind="ExternalInput").ap()
    dout = {}
    for n, s in CORE_OUT_SHAPES.items():
        dout[n] = nc.dram_tensor(n, s, F32, kind="ExternalOutput").ap()

    with ExitStack() as es:
        S = Sched(nc, es)
        pe, act, dve, pool, sp = S.pe, S.act, S.dve, S.pool, S.sp

        def sb(name, shape, dt):
            return es.enter_context(nc.sbuf_tensor(name, shape, dt))

        X = sb("X", [128, KC, TMAX], F32)
        Hf = sb("Hf", [128, KC * TMAX // 2], F32)
        H = Hf[:].bitcast(BF16).rearrange("p (c t) -> p c t", c=KC)
        YFB = Hf[:, 0:KC * 512].rearrange("p (c t) -> p c t", c=KC)
        RAf = sb("RAf", [128, 8 * TMAX // 2], F32)
        RA = RAf[:].bitcast(BF16).rearrange("p (s t) -> p s t", s=8)
        RBf = sb("RBf", [128, 3 * TMAX // 2], F32)
        RB = RBf[:].bitcast(BF16).rearrange("p (s t) -> p s t", s=3)
        ZW = 15 + 1024
        ZB = RBf[:, 0:ZW]
        ZBS = RBf[:, ZW:ZW + 16 * 23].rearrange("p (s t) -> p s t", s=16)
        P1 = sb("P1", [128, ZW + 16 * 23], F32)
        P2 = sb("P2", [128, ZW + 16 * 23], F32)
        HA = sb("HA", [128, 2, 30 + 1024], BF16)
        HAS = sb("HAS", [128, 2, 16, 38], BF16)
        Q = sb("Q", [128, 2, 2 + 1024], BF16)
        QS = sb("QS", [128, 2, 16, 10], BF16)
        DB = sb("DB", [128, TMAX], BF16)
        VTf = sb("VTf", [128, 9 * 128], F32)
        VT = VTf[:].bitcast(BF16).rearrange("p (a c) -> p a c", a=9)
        PT = sb("PT", [128, 2, TMAX], BF16)
        WA = [sb(f"WA{i}", [128, 2048], BF16) for i in range(4)]
        WD = [sb(f"WD{i}", [128, HALF * 256], BF16) for i in range(2)]
        XS = [sb(f"XS{i}", [128, 1024], F32) for i in range(2)]
        OST = [sb(f"OST{i}", [128, 256], F32) for i in range(3)]
        NTMP = 8
        TMP = [sb(f"TMP{i}", [128, 512], F32) for i in range(NTMP)]
        SQf = sb("SQf", [128, KC * 256], F32)
        SQ = SQf[:].bitcast(BF16).rearrange("p (c t) -> p c t", c=KC)
        TMPX = TMP + [SQf[:, i * 512:(i + 1) * 512] for i in range(4)]
        DIAGf = sb("DIAGf", [128, 31 * 64], F32)
        DIAG = DIAGf[:].bitcast(BF16).rearrange("p (k c) -> p k c", k=31)
        DIAG2f = sb("DIAG2f", [128, 31 * 64], F32)
        DIAG2 = DIAG2f[:].bitcast(BF16).rearrange("p (k c) -> p k c", k=31)
        DIAGc = [DIAG, DIAG2]
        DIAGQ = sb("DIAGQ", [128, 3, 128], BF16)
        IDENT = sb("IDENT", [128, 128], F32)
        ONES = sb("ONES", [128, 128], BF16)
        BD = sb("BD", [128, 128], F32)
        IMBD = sb("IMBD", [128, 128], F32)
        TRIL = sb("TRIL", [128, 128], F32)
        MASKS = sb("MASKS", [128, 128], F32)
        EPSV = sb("EPSV", [128, 1], F32)
        DUMV = sb("DUMV", [128, 1], F32)
        G1024 = sb("G1024", [128, KC, 16], F32)
        P256 = sb("P256", [128, 2, 80], F32)
        WT = sb("WT", [128, L, 4, 128], BF16)
        WTS = sb("WTS", [128, L, 4, 128], BF16)
        BSP = sb("BSP", [128, L, 2, 128], F32)
        BSS = sb("BSS", [128, L, 2, 128], F32)
        BDP = sb("BDP", [128, L, 2, 128], BF16)
        CORR = sb("CORR", [128, 2, 16], F32)
        RS8 = sb("RS8", [128, L, 4, 8], F32)
        HALOA = sb("HALOA", [128, L, 2, 30], BF16)
        HALOB = sb("HALOB", [128, L, 2, 15], F32)
        HALOQ = sb("HALOQ", [128, L, 2, 2], BF16)
        STA = sb("STA", [128, 2, 32], F32)
        STAS = sb("STAS", [128, 2, 128], F32)
        STB = sb("STB", [128, 2, 16], F32)
        STBS = sb("STBS", [128, 2, 128], F32)
        STQ = sb("STQ", [128, 2, 2], F32)
        STQS = sb("STQS", [128, 2, 32], F32)
        CORT = sb("CORT", [128, 16], F32)
        PS = es.enter_context(nc.psum_tensor("PS", [128, 8, 512], F32))

        NB = 3
        XR = [[Res(f"X{c}_{b}") for b in range(NB)] for c in range(KC)]
        HR = [[Res(f"H{c}_{b}") for b in range(NB)] for c in range(KC)]
        RR = [[Res(f"R{j}_{b}") for b in range(NB)] for j in range(HALF)]
        RBall = [RR[j][b] for j in (8, 9, 10) for b in range(NB)]
        PSR = [Res(f"PS{i}") for i in range(8)]
        YFB2 = RAf[:, 0:KC * 512].rearrange("p (c t) -> p c t", c=KC)
        YFBs = [(YFB, [HR[c][b_] for c in range(KC) for b_ in range(NB)]),
                (YFB2, [RR[j][b_] for j in range(8) for b_ in range(NB)])]
        TMPR = [Res(f"TMP{i}") for i in range(NTMP + 4)]
        SQR = TMPR[NTMP:NTMP + 4]
        WAR_ = [Res(f"WA{i}") for i in range(4)]
        WDR = [Res(f"WD{i}") for i in range(2)]
        XSR = [Res(f"XS{i}") for i in range(2)]
        OSTR = [Res(f"OST{i}") for i in range(3)]
        rSQ, rDIAG, rDIAGQ = Res("SQ"), Res("DIAG"), Res("DIAGQ")
        rDK = [Res(f"DK{k}") for k in range(31)]
        rDK2 = [Res(f"DK2_{k}") for k in range(31)]
        rDKc = [rDK, rDK2]
        rDQK = [Res(f"DQK{k}") for k in range(3)]
        rCONST = Res("CONST")
        rC2 = Res("CONST2")
        rBDP, rBS = Res("BDP"), Res("BS")
        rDUM = Res("DUM")
        rP1, rP2, rDB, rVT, rPT = Res("P1"), Res("P2"), Res("DB"), Res("VT"), Res("PT")
        rHA = [[Res(f"HA{c}_{b}") for b in range(NB)] for c in range(2)]
        rHAh = [Res("HAh0"), Res("HAh1")]
        rHASh = Res("HASh")
        rQ = [[Res(f"Q{c}_{b}") for b in range(NB)] for c in range(2)]
        rQh = [Res("Qh0"), Res("Qh1")]
        rQSh = Res("QSh")
        rZBh, rZBSh = Res("ZBh"), Res("ZBSh")
        rVTc = [[Res(f"VT{c}_{b}") for b in range(NB)] for c in range(2)]
        rHALOA, rHALOB, rHALOQ = Res("HALOA"), Res("HALOB"), Res("HALOQ")
        rSTA, rSTAS, rSTB, rSTBS, rSTQ, rSTQS = (Res("STA"), Res("STAS"), Res("STB"), Res("STBS"),
                                                 Res("STQ"), Res("STQS"))
        rCORT = Res("CORT")

        state = {'bank': 0, 'tmp': 0, 'ost': 0, 'xs': 0}

        pinned = set()

        def bank(pin=False):
            i = state['bank']
            while i in pinned:
                i = (i + 1) % 8
            state['bank'] = (i + 1) % 8
            if pin:
                pinned.add(i)
            return PS[:, i, :], GenRes(PSR[i])

        def unpin(ps_res):
            pinned.discard(PSR.index(ps_res.base))

        tpinned = set()

        def tmp(pin=False, ext=False):
            nring = NTMP + 4 if ext else NTMP
            i = state['tmp'] % nring
            while i in tpinned:
                i = (i + 1) % nring
            state['tmp'] = (i + 1) % nring
            if pin:
                tpinned.add(i)
            return TMPX[i], GenRes(TMPR[i])

        def unpin_t(t_res):
            tpinned.discard(TMPR.index(t_res.base))

        sem_w = {('A', i): S.new_prod(f"wA{i}") for i in range(4)}
        sem_w.update({('D', i): S.new_prod(f"wD{i}") for i in range(2)})
        sem_xs = [S.new_prod(f"xs{i}") for i in range(2)]
        sem_ost = [S.new_prod(f"ost{i}") for i in range(3)]
        sem_misc = S.new_prod("misc")
        sem_d2d = S.new_prod("d2d")
        sem_setup = S.new_prod("setup")
        sem_setup2 = S.new_prod("setup2")
        sem_misc2 = S.new_prod("misc2")

        def pool_op(reads, writes, fn):
            S.op(pool, reads, writes, fn)

        def dve_op(reads, writes, fn):
            S.op(dve, reads, writes, fn)

        pool_op([], [rCONST], lambda e: e.memset(IDENT[:], 0.0))
        pool_op([rCONST], [rCONST], lambda e: e.affine_select(
            out=IDENT[:], in_=IDENT[:], pattern=[[-1, 128]], compare_op=ALU.not_equal, fill=1.0,
            base=0, channel_multiplier=1))
        pool_op([], [rCONST], lambda e: e.memset(ONES[:], 1.0 / 1024.0))
        pool_op([], [rCONST], lambda e: e.memset(EPSV[:], EPS))

        items = []
        PSTG2 = OST[0]
        for l in range(L):
            for i, n in enumerate(['norm_ffn1', 'norm_mix', 'norm_ffn2', 'norm_ple']):
                r = l * 4 + i
                for hf in range(2):
                    items.append(([], [TMPR[hf]], lambda e, n=n, l=l, r=r, hf=hf: e.dma_start(
                        out=TMP[hf][r:r + 1, :], in_=din[n][l:l + 1, hf * 512:(hf + 1) * 512])))
        for hf in range(2):
            items.append(([], [TMPR[hf]], lambda e, hf=hf: e.dma_start(
                out=TMP[hf][8:9, :], in_=din['norm_final'].rearrange("(a n) -> a n", a=1)[:, hf * 512:(hf + 1) * 512])))
        for l in range(L):
            b = l * 40
            items.append(([], [OSTR[0]], lambda e, l=l, b=b: e.dma_start(out=PSTG2[b:b + 31, 0:256], in_=din['conv_a_w'][l])))
            for i, n in enumerate(['conv_a_b', 'norm_a_g', 'norm_a_b', 'pool_scale', 'sgu_norm_g', 'sgu_norm_b']):
                items.append(([], [OSTR[0]], lambda e, n=n, l=l, r=b + 31 + i: e.dma_start(
                    out=PSTG2[r:r + 1, 0:256], in_=din[n][l:l + 1, :])))
            items.append(([], [OSTR[0]], lambda e, l=l, b=b: e.dma_start(out=PSTG2[b + 37:b + 40, 0:256], in_=din['short_conv_w'][l])))
        S.dma(act, sem_setup, items)
        for half in range(2):
            pb, pr = bank()
            S.op(pe, [TMPR[half], rCONST], [pr], lambda e, half=half, pb=pb: [
                e.transpose(out=pb[:, j * 16:j * 16 + 9], in_=TMP[half][0:9, j * 128:(j + 1) * 128],
                            identity=IDENT[0:9, 0:9]) for j in range(4)][-1])
            S.op(dve, [pr], [rCONST], lambda e, half=half, pb=pb: e.tensor_copy(
                out=G1024[:, half * 4:half * 4 + 4, 0:9], in_=pb[:, 0:64].rearrange("p (c r) -> p c r", c=4)[:, :, 0:9]))
        pb, pr = bank()
        S.op(pe, [OSTR[0], rCONST], [pr], lambda e, pb=pb: [
            e.transpose(out=pb[:, cc * 80:cc * 80 + 80], in_=PSTG2[0:80, cc * 128:(cc + 1) * 128],
                        identity=IDENT[0:80, 0:80]) for cc in range(2)][-1])
        S.op(dve, [pr], [rCONST], lambda e, pb=pb: e.tensor_copy(
            out=P256[:], in_=pb[:, 0:160].rearrange("p (c r) -> p c r", c=2)))

        def late_setup():
            dve_op([], [rC2], lambda e: e.memset(BD[:], 1.0 / 64.0))
            dve_op([rCONST, rC2, rBDP, rBS], [rC2], lambda e: e.memset(BD[0:64, 64:128], 0.0))
            dve_op([rCONST, rC2, rBDP, rBS], [rC2], lambda e: e.memset(BD[64:128, 0:64], 0.0))
            dve_op([rCONST, rC2, rBDP, rBS], [rC2], lambda e: e.tensor_tensor(out=IMBD[:], in0=IDENT[:], in1=BD[:], op=ALU.subtract))
            dve_op([], [rC2], lambda e: e.memset(TRIL[:], 1.0))
            dve_op([], [rC2], lambda e: e.memset(MASKS[:], 1.0))
            pool_op([rCONST, rC2, rBDP, rBS], [rC2], lambda e: e.affine_select(
                out=TRIL[:], in_=TRIL[:], pattern=[[1, 128]], compare_op=ALU.is_ge, fill=0.0,
                base=0, channel_multiplier=-1))
            MS3 = MASKS[:].rearrange("p (a b) -> p a b", a=16)
            pool_op([rCONST, rC2, rBDP, rBS], [rC2], lambda e: e.affine_select(
                out=MS3, in_=MS3, pattern=[[8, 16], [1, 8]], compare_op=ALU.is_ge, fill=0.0,
                base=0, channel_multiplier=-1))
            pool_op([rCONST, rC2, rBDP, rBS], [rC2], lambda e: e.affine_select(
                out=MS3, in_=MS3, pattern=[[-8, 16], [0, 8]], compare_op=ALU.is_ge, fill=0.0,
                base=0, channel_multiplier=1))
            dve_op([], [rHAh[0], rHAh[1]], lambda e: e.memset(HA[:, :, 0:30], 0.0))
            dve_op([], [rQh[0], rQh[1]], lambda e: e.memset(Q[:, :, 0:2], 0.0))
            wins = {(0, 0): 2, (0, 1): 4, (1, 0): 8, (1, 1): 16}
            for (cc, hf), win in wins.items():
                dve_op([rCONST, rC2, rBDP, rBS], [rC2], lambda e: e.memset(CORR[hf * 64:(hf + 1) * 64, cc, :], 1.0 / win))
                for t in range(win - 1):
                    dve_op([rCONST, rC2, rBDP, rBS], [rC2], lambda e: e.memset(CORR[hf * 64:(hf + 1) * 64, cc, t:t + 1], 1.0 / (t + 1)))

        def setup_dmas():
            dve_op([], [rBDP], lambda e: e.memset(BDP[:], 0.0))
            items = []
            for l in range(L):
                for h in range(4):
                    j = l * 4 + h
                    cc, hh = h // 2, h % 2
                    items.append(([], [rP1], lambda e, l=l, h=h, j=j: e.dma_start(out=P1[:, j * 128:(j + 1) * 128], in_=din['sgu_w'][l, h])))
                    items.append(([], [rP1], lambda e, l=l, h=h, j=j: e.dma_start(
                        out=P1[:, 1024 + j * 8:1024 + j * 8 + 8], in_=din['sgu_w'][l, h, 0:8, 0:8].partition_broadcast(16))))
                    items.append(([], [rBS], lambda e, l=l, h=h, cc=cc, hh=hh: e.dma_start(
                        out=BSP[hh * 64:(hh + 1) * 64, l, cc, :], in_=din['sgu_b'][l, h:h + 1, :].broadcast_to([64, 128]))))
                    items.append(([], [rBS], lambda e, l=l, h=h, cc=cc, hh=hh: e.dma_start(
                        out=BSS[hh * 64:(hh + 1) * 64, l, cc, :].rearrange("p (a t) -> p a t", a=16),
                        in_=din['sgu_b'][l, h, 0:8].partition_broadcast(16).partition_broadcast(64))))
            S.dma(sp, sem_setup2, items)
            items = []
            for l in range(L):
                for g_ in range(4):
                    cc, hh = g_ // 2, g_ % 2
                    items.append(([], [rBDP], lambda e, l=l, g_=g_, cc=cc, hh=hh: e.dma_start(
                        out=BDP[hh * 64:(hh + 1) * 64, l, cc, hh * 64:(hh + 1) * 64], in_=din['pool_w'][l, g_])))
            S.dma(pool, sem_misc, items)
            items = []
            for l in range(L):
                items.append(([], [], lambda e, l=l: e.dma_start(
                    out=dout['ca_s'][l].rearrange("(s j) c -> s j c", s=16)[:, 0:22, :],
                    in_=din['sta'][l].rearrange("(s j) c -> s j c", s=16)[:, 8:30, :])))
                items.append(([], [], lambda e, l=l: e.dma_start(
                    out=dout['po_s'][l].rearrange("(s j) c -> s j c", s=16)[:, 0:7, :],
                    in_=din['stp'][l].rearrange("(s j) c -> s j c", s=16)[:, 8:15, :])))
            S.dma(sp, sem_d2d, items, is_output=True)

        def late_setup_sgu():
            for l in range(L):
                for h in range(4):
                    j = l * 4 + h
                    pb, pr = bank()
                    S.op(pe, [rP1, rCONST], [pr], lambda e: e.transpose(
                        out=pb[:, 0:128], in_=P1[:, j * 128:(j + 1) * 128], identity=IDENT[:]))
                    S.op(dve, [pr, rCONST, rC2, rBDP, rBS], [rC2], lambda e: e.tensor_tensor(
                        out=WT[:, l, h, :], in0=pb[:, 0:128], in1=TRIL[:], op=ALU.mult))
                    S.op(dve, [rP1], [rP2], lambda e: e.tensor_copy(
                        out=P2[:, j * 128:(j + 1) * 128].rearrange("p (a s) -> p a s", a=16),
                        in_=P1[:, 1024 + j * 8:1024 + j * 8 + 8].unsqueeze(1).broadcast_to([128, 16, 8])))
                    pb2, pr2 = bank()
                    S.op(pe, [rP2, rCONST], [pr2], lambda e: e.transpose(
                        out=pb2[:, 0:128], in_=P2[:, j * 128:(j + 1) * 128], identity=IDENT[:]))
                    S.op(dve, [pr2, rCONST, rC2, rBDP, rBS], [rC2], lambda e: e.tensor_tensor(
                        out=WTS[:, l, h, :], in0=pb2[:, 0:128], in1=MASKS[:], op=ALU.mult))

        stream = []

        def a_tile(name, l, c0, ncol, nk=KC):
            def fn(e, buf):
                src = din[name][l].rearrange("(kc p) n -> p kc n", p=128)[:, :, c0:c0 + ncol]
                return e.dma_start(out=buf[:, 0:nk * ncol].rearrange("p (k n) -> p k n", k=nk), in_=src)
            return ('A', fn)

        def d_tile(name, l, half, mp):
            def fn(e, buf):
                src = din[name][l][half * HALF * 128:(half + 1) * HALF * 128, mp * 256:(mp + 1) * 256]
                src = src.rearrange("(j p) n -> p j n", p=128)
                return e.dma_start(out=buf[:, :].rearrange("p (j n) -> p j n", j=HALF), in_=src)
            return ('D', fn)

        def ffn_tiles(l, which):
            pre = f"w_ffn{which}_"
            for half in range(2):
                for tp in range(6):
                    c0 = (half * HALF + tp * 2) * 128
                    ncol = 256 if tp < 5 else 128
                    stream.append(a_tile(pre + 'gate', l, c0, ncol))
                    stream.append(a_tile(pre + 'up', l, c0, ncol))
                for mp in range(4):
                    stream.append(d_tile(pre + 'down', l, half, mp))

        for g in range(2):
            for l in range(L):
                ffn_tiles(l, 1)
                for part in (2, 0, 1, 4, 3, 6, 7, 5):
                    stream.append(a_tile('w_mix_in', l, part * 256, 256))
                for mp in range(4):
                    stream.append(a_tile('w_mix_out', l, mp * 256, 256))
                ffn_tiles(l, 2)
                stream.append(a_tile('w_ple_proj', l, 0, 1024, nk=2))
                for mp in range(4):
                    stream.append(a_tile('w_ple_gate', l, mp * 256, 256))

        wst = {'next_load': 0, 'next_use': 0, 'cnt': {'A': 0, 'D': 0}, 'slot_of': {}, 'free': {}}
        for i in range(4):
            wst['free'][('A', i)] = True
        for i in range(2):
            wst['free'][('D', i)] = True
        nslots = {'A': 4, 'D': 2}
        bufs = {'A': WA, 'D': WD}
        wres = {'A': WAR_, 'D': WDR}

        def pump():
            while wst['next_load'] < len(stream):
                i = wst['next_load']
                kind, fn = stream[i]
                cand = [kk for kk in range(nslots[kind]) if wst['free'][(kind, kk)]]
                if not cand:
                    break
                k = cand[0]
                wst['free'][(kind, k)] = False
                wst['cnt'][kind] += 1
                wst['slot_of'][i] = (kind, k)
                buf = bufs[kind][k]
                S.dma(pool, sem_w[(kind, k)], [(wst.pop('first_reads', []), [wres[kind][k]], lambda e, fn=fn, buf=buf: fn(e, buf))])
                wst['next_load'] += 1

        def getw(expect_kind):
            i = wst['next_use']
            assert i < wst['next_load'], "weight tile not loaded yet (slot starvation)"
            kind, k = wst['slot_of'][i]
            assert kind == expect_kind
            wst['next_use'] += 1
            return bufs[kind][k], wres[kind][k], (kind, k)

        def release(slot):
            wst['free'][slot] = True
            pump()

        def mm_group(reads, writes, mms):
            def fn(e):
                inst = None
                for (o, a, b, st, sp_) in mms:
                    inst = e.matmul(o, lhsT=a, rhs=b, start=st, stop=sp_)
                return inst
            S.op(pe, reads, writes, fn)

        def wavefront(items, stages, order=None):
            n_, m_ = len(items), len(stages)
            for step in range(n_ + m_ - 1):
                for s_ in (order or range(m_)):
                    i_ = step - s_
                    if 0 <= i_ < n_:
                        stages[s_](items[i_])

        def wavefront2(itemsA_, stagesA_, itemsB_, stagesB_, lag=0):
            nA, mA, nB, mB = len(itemsA_), len(stagesA_), len(itemsB_), len(stagesB_)
            for step in range(max(nA + mA - 1, lag + nB + mB - 1)):
                for s_ in range(mA):
                    i_ = step - s_
                    if 0 <= i_ < nA:
                        stagesA_[s_](itemsA_[i_])
                for s_ in range(mB):
                    i_ = step - lag - s_
                    if 0 <= i_ < nB:
                        stagesB_[s_](itemsB_[i_])

        def m_order(last, nb):
            if last:
                return [(mm, bi) for bi in range(nb) for mm in range(2)]
            return [(mm, bi) for mm in range(2) for bi in range(nb)]

        def blocks_of(g):
            return [(0, 512), (512, 512), (1024, 128)] if g == 0 else [(0, 512), (512, 512)]

        def xr_all(bi):
            return [XR[c][bi] for c in range(KC)]

        def hr_all(bi):
            return [HR[c][bi] for c in range(KC)]

        nst = {'key': None, 'done': set(), 'stat': {}}

        def norm_stats(g, gi, bi, final):
            b0, n = blocks_of(g)[bi]
            pb, pr = bank(pin=final)
            S.op(act, xr_all(bi), SQR, lambda e: e.activation(
                out=SQ[:, :, 0:n], in_=X[:, :, b0:b0 + n], func=AF.Square))
            S.op(act, [rCONST], [rDUM], lambda e: e.activation(out=DUMV[:, 0:1], in_=EPSV[:, 0:1], func=AF.Ln))
            mm_group(SQR + [rCONST], [pr], [(pb[:, 0:n], ONES[:], SQ[:, c, 0:n], c == 0, c == KC - 1) for c in range(KC)])
            S.op(act, [pr, rCONST], [pr], lambda e: e.activation(
                out=pb[:, 0:n], in_=pb[:, 0:n], func=AF.Ln, bias=EPSV[:, 0:1], scale=1.0))
            S.op(act, [pr], [pr], lambda e: e.activation(out=pb[:, 0:n], in_=pb[:, 0:n], func=AF.Exp, scale=-0.5))
            nst['stat'][bi] = (pb, pr)

        def norm_apply(g, gi, bi, final, tile_cb=None):
            b0, n = blocks_of(g)[bi]
            pb, pr = nst['stat'][bi]
            for c in range(KC):
                if not final:
                    S.op(dve, [XR[c][bi], pr, rCONST], [HR[c][bi]], lambda e, c=c: e.scalar_tensor_tensor(
                        out=H[:, c, b0:b0 + n], in0=X[:, c, b0:b0 + n], scalar=G1024[:, c, gi:gi + 1],
                        in1=pb[:, 0:n], op0=ALU.mult, op1=ALU.mult))
                else:
                    S.op(dve, [XR[c][bi], pr, rCONST], YFBs[bi % 2][1], lambda e, c=c: e.scalar_tensor_tensor(
                        out=YFBs[bi % 2][0][:, c, 0:n], in0=X[:, c, b0:b0 + n], scalar=G1024[:, c, gi:gi + 1],
                        in1=pb[:, 0:n], op0=ALU.mult, op1=ALU.mult))
            if final:
                unpin(pr)
                tile_cb(bi, b0, n)

        def norm_hook(g, nxt):
            if nxt is None:
                return lambda bi: None
            gi, final = nxt
            key = (g, gi)

            def hook(bi):
                if nst['key'] != key:
                    nst['key'], nst['done'], nst['stat'] = key, set(), {}
                norm_stats(g, gi, bi, final)
                if not final:
                    norm_apply(g, gi, bi, final)
                nst['done'].add(bi)
            return hook

        def rmsnorm(g, gi, final=False, tile_cb=None):
            key = (g, gi)
            if nst['key'] != key:
                nst['key'], nst['done'], nst['stat'] = key, set(), {}
            nb = len(blocks_of(g))
            if final:
                for bi in range(nb):
                    if bi not in nst['done']:
                        norm_stats(g, gi, bi, final)
                for bi in range(nb):
                    norm_apply(g, gi, bi, final, tile_cb)
            else:
                for bi in range(nb):
                    if bi not in nst['done']:
                        norm_stats(g, gi, bi, final)
                        norm_apply(g, gi, bi, final)
            nst['key'] = None

        def rslot(j):
            return RA[:, j, :] if j < 8 else RB[:, j - 8, :]

        def ffn(g, l, which, mid_cb=None, nxt=None):
            blocks = blocks_of(g)
            rmsnorm(g, l * 4 + (0 if which == 1 else 2))
            for half in range(2):
                for tp in range(6):
                    ncol = 256 if tp < 5 else 128
                    wg, wgr, sg = getw('A')
                    wu, wur, su = getw('A')
                    wg3 = wg[:, 0:KC * ncol].rearrange("p (k n) -> p k n", k=KC)
                    wu3 = wu[:, 0:KC * ncol].rearrange("p (k n) -> p k n", k=KC)
                    njj = ncol // 128
                    if half == 0 and tp == 0:
                        jb_order = [(jj, bi) for bi in range(len(blocks)) for jj in range(njj)]
                    else:
                        jb_order = [(jj, bi) for jj in range(njj) for bi in range(len(blocks))]
                    for jj, bi in jb_order:
                        slot = tp * 2 + jj
                        for (b0, n) in [blocks[bi]]:
                            pg, pgr = bank()
                            pu, pur = bank()
                            if half == 0 and tp == 0 and jj == 0:
                                for kc in range(KC):
                                    mm_group([wgr, wur, HR[kc][bi]], [pgr, pur],
                                             [(pg[:, 0:n], wg3[:, kc, jj * 128:(jj + 1) * 128], H[:, kc, b0:b0 + n], kc == 0, kc == KC - 1),
                                              (pu[:, 0:n], wu3[:, kc, jj * 128:(jj + 1) * 128], H[:, kc, b0:b0 + n], kc == 0, kc == KC - 1)])
                            else:
                                mm_group([wgr, wur] + hr_all(bi), [pgr, pur],
                                         [(pg[:, 0:n], wg3[:, kc, jj * 128:(jj + 1) * 128], H[:, kc, b0:b0 + n], kc == 0, kc == KC - 1) for kc in range(KC)] +
                                         [(pu[:, 0:n], wu3[:, kc, jj * 128:(jj + 1) * 128], H[:, kc, b0:b0 + n], kc == 0, kc == KC - 1) for kc in range(KC)])
                            t1, t1r = tmp()
                            S.op(act, [pgr], [t1r], lambda e: e.activation(out=t1[:, 0:n], in_=pg[:, 0:n], func=AF.Silu))
                            S.op(dve, [t1r, pur], [RR[slot][bi]], lambda e: e.tensor_tensor(
                                out=rslot(slot)[:, b0:b0 + n], in0=pu[:, 0:n], in1=t1[:, 0:n], op=ALU.mult))
                    release(sg)
                    release(su)
                for mp in range(4):
                    wd, wdr, sd = getw('D')
                    wd3 = wd[:, :].rearrange("p (j n) -> p j n", j=HALF)
                    lastmp = (half == 1 and mp == 3)
                    hook = norm_hook(g, nxt)
                    for mm, bi in m_order(lastmp, len(blocks)):
                        m = mp * 2 + mm
                        for (b0, n) in [blocks[bi]]:
                            pd, pdr = bank()
                            mm_group([wdr] + [RR[s][bi] for s in range(HALF)], [pdr],
                                     [(pd[:, 0:n], wd3[:, s, mm * 128:(mm + 1) * 128], rslot(s)[:, b0:b0 + n], s == 0, s == HALF - 1)
                                      for s in range(HALF)])
                            S.op(dve, [pdr, XR[m][bi]], [XR[m][bi]], lambda e: e.scalar_tensor_tensor(
                                out=X[:, m, b0:b0 + n], in0=pd[:, 0:n], scalar=0.5, in1=X[:, m, b0:b0 + n],
                                op0=ALU.mult, op1=ALU.add))
                        if lastmp and mm == 1 and bi > 0:
                            hook(bi - 1)
                    if lastmp:
                        hook(len(blocks) - 1)
                    release(sd)
                if half == 0 and mid_cb is not None:
                    mid_cb()

        def z_mms(ps_ap, w3, cc, bi, b0, n):
            return [(ps_ap[:, 0:n], w3[:, kc, cc * 128:(cc + 1) * 128], H[:, kc, b0:b0 + n], kc == 0, kc == KC - 1)
                    for kc in range(KC)]

        def a3(w):
            return w[:, 0:KC * 256].rearrange("p (k n) -> p k n", k=KC)

        def out_T(srcs, src_res, nrows, dst_fn):
            pb, pr = bank()
            S.op(pe, src_res + [rCONST], [pr], lambda e: [
                e.transpose(out=pb[0:nrows, cc * 128:(cc + 1) * 128], in_=srcs[cc], identity=IDENT[:]) for cc in range(2)][-1])
            k = state['ost']
            state['ost'] = (k + 1) % 3
            S.op(act, [pr], [OSTR[k]], lambda e: e.activation(out=OST[k][0:nrows, :], in_=pb[0:nrows, 0:256], func=AF.Copy))
            S.dma(sp, sem_ost[k], [([OSTR[k]], [], lambda e: e.dma_start(out=dst_fn(), in_=OST[k][0:nrows, :]))], is_output=True)

        def load_T(src_ap, nrows, evac):
            k = state['ost']
            state['ost'] = (k + 1) % 3
            S.dma(sp, sem_ost[k], [([], [OSTR[k]], lambda e: e.dma_start(out=OST[k][0:nrows, :], in_=src_ap))])
            pb, pr = bank()
            S.op(pe, [OSTR[k], rCONST], [pr], lambda e: [
                e.transpose(out=pb[:, cc * 128:cc * 128 + nrows], in_=OST[k][0:nrows, cc * 128:(cc + 1) * 128],
                            identity=IDENT[0:nrows, 0:nrows]) for cc in range(2)][-1])
            for cc in range(2):
                ew, reads, writes, fn = evac(cc, pb[:, cc * 128:cc * 128 + nrows])
                S.op(ew, [pr] + reads, writes, fn)

        def mixer_prep(g, l):
            pbase = l * 40
            if g == 0:
                for i in range(4):
                    halo_T(DIAGf[:, i * 256:(i + 1) * 256], [rDIAG], 120, lambda cc, ps_ap, i=i: (
                        [], [rHASh], lambda e: e.activation(
                            out=HAS[:, cc, 4 * i:4 * i + 4, 0:30], in_=ps_ap.rearrange("p (s j) -> p s j", s=4), func=AF.Copy)))
                halo_T(DIAGf[:, 1024:1280], [rDIAG], 32, lambda cc, ps_ap: (
                    [], [rQSh], lambda e: e.activation(
                        out=QS[:, cc, :, 0:2], in_=ps_ap.rearrange("p (s j) -> p s j", s=16), func=AF.Copy)))
            else:
                S.op(dve, [rHALOA], [rHAh[0], rHAh[1]], lambda e: e.tensor_copy(out=HA[:, :, 0:30], in_=HALOA[:, l, :, :]))
                S.op(dve, [rHALOQ], [rQh[0], rQh[1]], lambda e: e.tensor_copy(out=Q[:, :, 0:2], in_=HALOQ[:, l, :, :]))

            for cc in (1, 0):
                for k in range(31):
                    wr = [rDKc[cc][k]] + ([rDIAG] if (k == 0 and cc == 0) else [])
                    S.op(dve, [rCONST] + ([rDIAG] if cc == 0 else []), wr, lambda e, k=k, cc=cc: e.tensor_scalar(
                        out=DIAGc[cc][:, k, :], in0=IDENT[:], scalar1=P256[:, cc, pbase + k:pbase + k + 1], scalar2=None,
                        op0=ALU.mult))


        def mixers(g, l):
            blocks = blocks_of(g)
            nbk = len(blocks)
            pbase = l * 40
            rmsnorm(g, l * 4 + 1)
            last_p = 1

            wb, wbr, sbk = getw('A')
            wb3 = a3(wb)
            inv = {(0, 0): 0.5, (0, 1): 0.25, (1, 0): 0.125, (1, 1): 0.0625}
            P1S = P1[:, ZW:].rearrange("p (s t) -> p s t", s=16)
            P2S = P2[:, ZW:].rearrange("p (s t) -> p s t", s=16)
            ZBc = [ZB, RAf[:, 2304:2304 + ZW]]
            ZBSc = [ZBS, RAf[:, 2304 + ZW:2304 + ZW + 16 * 23].rearrange("p (s t) -> p s t", s=16)]
            ZRc = [RBall, [RR[j][b_] for j in (4, 5, 6) for b_ in range(NB)]]
            DBc = [DB[:, :], RA[:, 7, :]]
            DBRc = [[rDB], [RR[7][b_] for b_ in range(NB)]]
            if g == 0:
                for i in range(2):
                    halo_T(VTf[:, i * 256:(i + 1) * 256], [rVT], 120, lambda c2, ps_ap, i=i: (
                        [], ZRc[c2], lambda e: e.activation(
                            out=ZBSc[c2][:, 8 * i:8 * i + 8, 0:15], in_=ps_ap.rearrange("p (s j) -> p s j", s=8), func=AF.Copy)))
            for cc in range(2):
                zb, zr_ = ZBc[cc], ZRc[cc]
                if g == 0:
                    S.op(dve, [], zr_, lambda e: e.memset(zb[:, 0:15], 0.0))
                else:
                    S.op(dve, [rHALOB], zr_, lambda e: e.tensor_copy(out=zb[:, 0:15], in_=HALOB[:, l, cc, :]))
            for bi, (b0, n) in enumerate(blocks):
                for cc in range(2):
                    zb, zbs, zr_ = ZBc[cc], ZBSc[cc], ZRc[cc]
                    pz, pzr = bank()
                    mm_group([wbr] + hr_all(bi), [pzr], z_mms(pz, wb3, cc, bi, b0, n))
                    if b0 < 1024:
                        S.op(act, [pzr], zr_, lambda e: e.activation(out=zb[:, 15 + b0:15 + b0 + n], in_=pz[:, 0:n], func=AF.Copy))
                    else:
                        S.op(act, [pzr], zr_, lambda e: e.activation(
                            out=zbs[:, :, 15:23], in_=pz[:, 0:128].rearrange("p (s t) -> p s t", s=16), func=AF.Copy))
            for cc in range(2):
                zb, zbs, zr_, db, dbr = ZBc[cc], ZBSc[cc], ZRc[cc], DBc[cc], DBRc[cc]
                zr = zr_

                def lvl(dst, dstS, src, srcS, shift, lo, plo, dres, sres):
                    S.op(pool, sres, dres, lambda e: e.tensor_tensor(
                        out=dst[plo:128, lo:ZW], in0=src[plo:128, lo:ZW], in1=src[plo:128, lo - shift:ZW - shift], op=ALU.add))
                    if g == 0:
                        S.op(pool, sres, dres, lambda e: e.tensor_tensor(
                            out=dstS[plo:128, :, lo:23], in0=srcS[plo:128, :, lo:23], in1=srcS[plo:128, :, lo - shift:23 - shift], op=ALU.add))
                if cc == 0:
                    lvl(P1, P1S, zb, zbs, 1, 1, 0, [rP1], zr)
                    lvl(P2, P2S, P1, P1S, 2, 3, 64, [rP2], [rP1])
                else:
                    lvl(P1, P1S, zb, zbs, 1, 1, 0, [rP1], zr)
                    lvl(P2, P2S, P1, P1S, 2, 3, 0, [rP2], [rP1])
                    lvl(P1, P1S, P2, P2S, 4, 7, 0, [rP1], [rP2])
                    lvl(P2, P2S, P1, P1S, 8, 15, 64, [rP2], [rP1])
                for hf in range(2):
                    Sb, SbS = (P1, P1S) if hf == 0 else (P2, P2S)
                    pl, ph = hf * 64, hf * 64 + 64
                    iv = inv[(cc, hf)]
                    if g == 0:
                        S.op(pool, [rP1, rP2, rCONST, rC2, rBDP, rBS], [rCORT], lambda e: e.tensor_tensor(
                            out=CORT[pl:ph, 0:15], in0=Sb[pl:ph, 15:30], in1=CORR[pl:ph, cc, 0:15], op=ALU.mult))
                    S.op(pool, [rP1, rP2], [rP1, rP2], lambda e: e.tensor_scalar(
                        out=Sb[pl:ph, 15:ZW], in0=Sb[pl:ph, 15:ZW], scalar1=iv, scalar2=0.0, op0=ALU.mult, op1=ALU.add))
                    S.op(pool, [rP1, rP2] + zr, dbr, lambda e: e.tensor_tensor(
                        out=db[pl:ph, 0:1024], in0=Sb[pl:ph, 15:ZW], in1=zb[pl:ph, 15:ZW], op=ALU.subtract))
                    if g == 0:
                        S.op(pool, [rP1, rP2], [rP1, rP2], lambda e: e.tensor_scalar(
                            out=SbS[pl:ph, :, 15:23], in0=SbS[pl:ph, :, 15:23], scalar1=iv, scalar2=0.0, op0=ALU.mult, op1=ALU.add))
                        S.op(pool, [rP1, rP2] + zr, dbr, lambda e: e.tensor_tensor(
                            out=db[pl:ph, 1024:1152].rearrange("p (s t) -> p s t", s=16), in0=SbS[pl:ph, :, 15:23],
                            in1=zbs[pl:ph, :, 15:23], op=ALU.subtract))
                        S.op(pool, [rCORT] + zr, dbr, lambda e: e.tensor_tensor(
                            out=db[pl:ph, 0:15], in0=CORT[pl:ph, 0:15], in1=zb[pl:ph, 15:30], op=ALU.subtract))
                if g == 0:
                    S.op(dve, zr, [rHALOB], lambda e: e.tensor_copy(out=HALOB[:, l, cc, :], in_=zb[:, ZW - 15:ZW]))
                    S.op(dve, zr, [rSTBS], lambda e: e.tensor_copy(
                        out=STBS[:, cc, :].rearrange("p (s t) -> p s t", s=16), in_=zbs[:, :, 15:23]))
                else:
                    S.op(dve, zr, [rSTB], lambda e: e.tensor_copy(out=STB[:, cc, 0:15], in_=zb[:, ZW - 15:ZW]))
            release(sbk)
            if g == 1:
                out_T([STB[:, 0, 0:15], STB[:, 1, 0:15]], [rSTB], 15, lambda: dout['po_p'][l])
            else:
                out_T([STBS[:, 0, :], STBS[:, 1, :]], [rSTBS], 128,
                      lambda: dout['po_s'][l].rearrange("(s j) c -> s j c", s=16)[:, 7:15, :])

            wv, wvr, sv = getw('A')
            wgt, wgtr, sgt = getw('A')
            wv3, wgt3 = a3(wv), a3(wgt)
            itemsA = [dict(cc=cc, bi=bi, b0=b0, n=n) for bi, (b0, n) in enumerate(blocks) for cc in range(2)]

            def A1(d):
                cc, bi, b0, n = d['cc'], d['bi'], d['b0'], d['n']
                d['pv'], d['pvr'] = bank(pin=True)
                d['pg'], d['pgr'] = bank()
                mm_group([wvr, wgtr] + hr_all(bi), [d['pvr'], d['pgr']],
                         z_mms(d['pv'], wv3, cc, bi, b0, n) + z_mms(d['pg'], wgt3, cc, bi, b0, n))
                d['t1'], d['t1r'] = tmp(pin=True, ext=True)
                S.op(act, [d['pgr']], [d['t1r']], lambda e: e.activation(out=d['t1'][:, 0:n], in_=d['pg'][:, 0:n], func=AF.Sigmoid))
                if d is itemsA[-1]:
                    release(sv)
                    release(sgt)

            def A2(d):
                n = d['n']
                d['t2'], d['t2r'] = tmp(pin=True, ext=True)
                S.op(dve, [d['pvr'], d['t1r']], [d['t2r']], lambda e: e.tensor_tensor(
                    out=d['t2'][:, 0:n], in0=d['pv'][:, 0:n], in1=d['t1'][:, 0:n], op=ALU.mult))
                unpin(d['pvr'])
                unpin_t(d['t1r'])

            def A3(d):
                cc, bi, b0, n = d['cc'], d['bi'], d['b0'], d['n']
                t2, t2r = d['t2'], d['t2r']
                if b0 < 1024:
                    S.op(act, [t2r], [rHA[cc][bi]], lambda e: e.activation(
                        out=HA[:, cc, 30 + b0:30 + b0 + n], in_=t2[:, 0:n], func=AF.Copy))
                    if g == 1 and bi == last_p:
                        S.op(dve, [t2r], [rSTA], lambda e: e.tensor_copy(out=STA[:, cc, 0:30], in_=t2[:, n - 30:n]))
                    if g == 0 and bi == last_p:
                        S.op(dve, [rHA[cc][bi]], [rHALOA], lambda e: e.tensor_copy(
                            out=HALOA[:, l, cc, :], in_=HA[:, cc, 1024:1054]))
                else:
                    S.op(act, [t2r], [rHA[cc][bi]], lambda e: e.activation(
                        out=HAS[:, cc, :, 30:38], in_=t2[:, 0:128].rearrange("p (s t) -> p s t", s=16), func=AF.Copy))
                    S.op(dve, [t2r], [rSTAS], lambda e: e.tensor_copy(out=STAS[:, cc, :], in_=t2[:, 0:128]))
                unpin_t(t2r)

            def A4(d):
                cc, bi, b0, n = d['cc'], d['bi'], d['b0'], d['n']
                pc, pcr = bank()
                DG, rdk = DIAGc[cc], rDKc[cc]
                if b0 < 1024:
                    rd = rdk + [rHA[cc][bi], rHAh[cc]] + ([rHA[cc][bi - 1]] if bi > 0 else [])
                    mm_group(rd, [pcr], [(pc[:, 0:n], DG[:, k, :], HA[:, cc, b0 + k:b0 + k + n], k == 0, k == 30) for k in range(31)])
                else:
                    mm_group(rdk + [rHA[cc][bi], rHASh], [pcr],
                             [(pc[:, 0:128].rearrange("p (s t) -> p s t", s=16), DG[:, k, :], HAS[:, cc, :, k:k + 8], k == 0, k == 30)
                              for k in range(31)])
                d['t3'], d['t3r'] = tmp(pin=True, ext=True)
                S.op(act, [pcr, rCONST, rC2, rBDP, rBS], [d['t3r']], lambda e: e.activation(
                    out=d['t3'][:, 0:n], in_=pc[:, 0:n], func=AF.Identity, bias=P256[:, cc, pbase + 31:pbase + 32], scale=1.0))

            def A5(d):
                n = d['n']
                d['pd'], d['pdr'] = bank(pin=True)
                mm_group([d['t3r'], rCONST, rC2, rBDP, rBS], [d['pdr']], [(d['pd'][:, 0:n], IMBD[:], d['t3'][:, 0:n], True, True)])
                d['t4'], d['t4r'] = tmp(pin=True, ext=True)
                S.op(act, [d['pdr']], [d['t4r']], lambda e: e.activation(out=d['t4'][:, 0:n], in_=d['pd'][:, 0:n], func=AF.Square))
                unpin_t(d['t3r'])

            def A6(d):
                cc, bi, b0, n = d['cc'], d['bi'], d['b0'], d['n']
                pw, pwr = bank()
                mm_group([d['t4r'], rCONST, rC2, rBDP, rBS], [pwr], [(pw[:, 0:n], BD[:], d['t4'][:, 0:n], True, True)])
                S.op(act, [pwr, rCONST, rC2, rBDP, rBS], [pwr], lambda e: e.activation(
                    out=pw[:, 0:n], in_=pw[:, 0:n], func=AF.Ln, bias=EPSV[:, 0:1], scale=1.0))
                t5, t5r = tmp(ext=True)
                S.op(act, [pwr], [t5r], lambda e: e.activation(out=t5[:, 0:n], in_=pw[:, 0:n], func=AF.Exp, scale=-0.5))
                t6, t6r = tmp(ext=True)
                S.op(dve, [d['pdr'], t5r], [t6r], lambda e: e.tensor_tensor(out=t6[:, 0:n], in0=d['pd'][:, 0:n], in1=t5[:, 0:n], op=ALU.mult))
                S.op(act, [t6r, rCONST, rC2, rBDP, rBS], [RR[cc][bi]], lambda e: e.activation(
                    out=RA[:, cc, b0:b0 + n], in_=t6[:, 0:n], func=AF.Silu,
                    bias=P256[:, cc, pbase + 33:pbase + 34], scale=P256[:, cc, pbase + 32:pbase + 33]))
                unpin(d['pdr'])
                unpin_t(d['t4r'])

            wavefront(itemsA, [A1, A2, A3, A4, A5, A6])
            wvv, wvvr, svv = getw('A')
            wuu, wuur, suu = getw('A')
            wvv3, wuu3 = a3(wvv), a3(wuu)
            pairsC = [[dict(cc=cc, bi=bi, b0=b0, n=n) for cc in range(2)] for bi, (b0, n) in enumerate(blocks)]

            def PC1(P):
                for d in P:
                    cc, bi, b0, n = d['cc'], d['bi'], d['b0'], d['n']
                    d['pv'], d['pvr'] = bank()
                    mm_group([wvvr] + hr_all(bi), [d['pvr']], z_mms(d['pv'], wvv3, cc, bi, b0, n))
                for d in P:
                    n = d['n']
                    d['t1'], d['t1r'] = tmp(pin=True, ext=True)
                    S.op(act, [d['pvr']], [d['t1r']], lambda e: e.activation(out=d['t1'][:, 0:n], in_=d['pv'][:, 0:n], func=AF.Gelu_apprx_tanh))
                if P is pairsC[-1]:
                    release(svv)

            def PC2(P):
                for d in P:
                    n = d['n']
                    d['pd'], d['pdr'] = bank(pin=True)
                    mm_group([d['t1r'], rCONST, rC2, rBDP, rBS], [d['pdr']], [(d['pd'][:, 0:n], IMBD[:], d['t1'][:, 0:n], True, True)])
                for d in P:
                    n = d['n']
                    d['t2'], d['t2r'] = tmp(pin=True, ext=True)
                    S.op(act, [d['pdr']], [d['t2r']], lambda e: e.activation(out=d['t2'][:, 0:n], in_=d['pd'][:, 0:n], func=AF.Square))
                    unpin_t(d['t1r'])
                S.op(act, [rCONST], [rDUM], lambda e: e.activation(out=DUMV[:, 0:1], in_=EPSV[:, 0:1], func=AF.Ln))

            def PC3(P):
                for d in P:
                    n = d['n']
                    d['pw'], d['pwr'] = bank()
                    mm_group([d['t2r'], rCONST, rC2, rBDP, rBS], [d['pwr']], [(d['pw'][:, 0:n], BD[:], d['t2'][:, 0:n], True, True)])
                for d in P:
                    n = d['n']
                    S.op(act, [d['pwr'], rCONST, rC2, rBDP, rBS], [d['pwr']], lambda e: e.activation(
                        out=d['pw'][:, 0:n], in_=d['pw'][:, 0:n], func=AF.Ln, bias=EPSV[:, 0:1], scale=1.0))
                for d in P:
                    n = d['n']
                    d['t3'], d['t3r'] = tmp(ext=True)
                    S.op(act, [d['pwr']], [d['t3r']], lambda e: e.activation(
                        out=d['t3'][:, 0:n], in_=d['pw'][:, 0:n], func=AF.Exp, scale=-0.5))
                S.op(act, [rCONST], [rDUM], lambda e: e.activation(out=DUMV[:, 0:1], in_=EPSV[:, 0:1], func=AF.Gelu_apprx_tanh))
                for d in P:
                    cc, n = d['cc'], d['n']
                    d['t4'], d['t4r'] = tmp(ext=True)
                    S.op(dve, [d['pdr'], d['t3r'], rCONST], [d['t4r']], lambda e: e.scalar_tensor_tensor(
                        out=d['t4'][:, 0:n], in0=d['pd'][:, 0:n], scalar=P256[:, cc, pbase + 35:pbase + 36], in1=d['t3'][:, 0:n],
                        op0=ALU.mult, op1=ALU.mult))
                for d in P:
                    cc, n = d['cc'], d['n']
                    d['t5'], d['t5r'] = tmp(pin=True, ext=True)
                    S.op(act, [d['t4r'], rCONST], [d['t5r']], lambda e: e.activation(
                        out=d['t5'][:, 0:n], in_=d['t4'][:, 0:n], func=AF.Identity, bias=P256[:, cc, pbase + 36:pbase + 37], scale=1.0))
                    unpin(d['pdr'])
                    unpin_t(d['t2r'])

            def PC4(P):
                for d in P:
                    n = d['n']
                    nt = n // 128
                    t5 = d['t5']
                    d['pt'], d['ptr'] = bank()
                    pt = d['pt']
                    S.op(pe, [d['t5r'], rCONST, rC2, rBDP, rBS], [d['ptr']], lambda e: [
                        e.transpose(out=pt[:, ti * 128:(ti + 1) * 128], in_=t5[:, ti * 128:(ti + 1) * 128], identity=IDENT[:])
                        for ti in range(nt)][-1])
                for d in P:
                    cc, bi, b0, n = d['cc'], d['bi'], d['b0'], d['n']
                    nt = n // 128
                    tt0 = b0 // 128
                    pt = d['pt']
                    S.op(dve, [d['ptr']], [rVTc[cc][bi], rVT], lambda e: e.tensor_copy(
                        out=VT[:, tt0:tt0 + nt, cc * 128:(cc + 1) * 128], in_=pt[:, 0:n].rearrange("p (a c) -> p a c", a=nt)))
                    if b0 >= 1024:
                        k = state['ost']
                        state['ost'] = (k + 1) % 3
                        S.op(act, [d['ptr']], [OSTR[k]], lambda e: e.activation(
                            out=OST[k][:, 0:128], in_=pt[:, 0:128], func=AF.Copy))
                        S.dma(sp, sem_ost[k], [([OSTR[k]], [], lambda e: e.dma_start(
                            out=dout['cv_s'][l][:, cc * 128:(cc + 1) * 128], in_=OST[k][:, 0:128]))], is_output=True)
                    unpin_t(d['t5r'])

            def PC5(P):
                for d in P:
                    cc, bi, b0, n = d['cc'], d['bi'], d['b0'], d['n']
                    nt = n // 128
                    tt0 = b0 // 128
                    d['ps'], d['psr'] = bank()
                    mms = []
                    for ti in range(nt):
                        for hh in range(2):
                            h = 2 * cc + hh
                            wt = WTS[:, l, h, :] if b0 >= 1024 else WT[:, l, h, :]
                            mms.append((d['ps'][hh * 64:(hh + 1) * 64, ti * 128:(ti + 1) * 128],
                                        VT[:, tt0 + ti, h * 64:(h + 1) * 64], wt, True, True))
                    mm_group([rVTc[cc][bi], rCONST, rC2, rBDP, rBS], [d['psr']], mms)
                    d['pu'], d['pur'] = bank()
                    mm_group([wuur] + hr_all(bi), [d['pur']], z_mms(d['pu'], wuu3, cc, bi, b0, n))
                for d in P:
                    n = d['n']
                    d['t6'], d['t6r'] = tmp(ext=True)
                    S.op(act, [d['pur']], [d['t6r']], lambda e: e.activation(out=d['t6'][:, 0:n], in_=d['pu'][:, 0:n], func=AF.Gelu_apprx_tanh))
                for d in P:
                    cc, b0, n = d['cc'], d['b0'], d['n']
                    nt = n // 128
                    d['t7'], d['t7r'] = tmp(ext=True)
                    if b0 < 1024:
                        S.op(dve, [d['psr'], rCONST, rC2, rBDP, rBS], [d['t7r']], lambda e: e.tensor_tensor(
                            out=d['t7'][:, 0:n].rearrange("p (a t) -> p a t", a=nt), in0=d['ps'][:, 0:n].rearrange("p (a t) -> p a t", a=nt),
                            in1=BSP[:, l, cc, :].unsqueeze(1).broadcast_to([128, nt, 128]), op=ALU.add))
                    else:
                        S.op(dve, [d['psr'], rCONST, rC2, rBDP, rBS], [d['t7r']], lambda e: e.tensor_tensor(
                            out=d['t7'][:, 0:n], in0=d['ps'][:, 0:n], in1=BSS[:, l, cc, :], op=ALU.add))
                for d in P:
                    cc, bi, b0, n = d['cc'], d['bi'], d['b0'], d['n']
                    S.op(dve, [d['t6r'], d['t7r']], [RR[4 + cc][bi]], lambda e: e.tensor_tensor(
                        out=RA[:, 4 + cc, b0:b0 + n], in0=d['t6'][:, 0:n], in1=d['t7'][:, 0:n], op=ALU.mult))

            wavefront(pairsC, [PC1, PC2, PC3, PC4, PC5])
            release(suu)
            if g == 1:
                out_T([STA[:, 0, 0:30], STA[:, 1, 0:30]], [rSTA], 30, lambda: dout['ca_p'][l])
            else:
                out_T([STAS[:, 0, :], STAS[:, 1, :]], [rSTAS], 128,
                      lambda: dout['ca_s'][l].rearrange("(s j) c -> s j c", s=16)[:, 22:30, :])

            for cc in range(2):
                db, dbr = DBc[cc], DBRc[cc]
                for bi, (b0, n) in enumerate(blocks):
                    pp_, ppr = bank()
                    mm_group(dbr + [rCONST, rC2, rBDP, rBS], [ppr], [(pp_[:, 0:n], BDP[:, l, cc, :], db[:, b0:b0 + n], True, True)])
                    S.op(act, [ppr, rCONST, rC2, rBDP, rBS], [RR[2 + cc][bi]], lambda e: e.activation(
                        out=RA[:, 2 + cc, b0:b0 + n], in_=pp_[:, 0:n], func=AF.Identity,
                        scale=P256[:, cc, pbase + 34:pbase + 35]))


            wc, wcr, sc_ = getw('A')
            wdi, wdir, sdi = getw('A')
            wc3, wdi3 = a3(wc), a3(wdi)
            itemsD = [dict(cc=cc, bi=bi, b0=b0, n=n) for cc in range(2) for bi, (b0, n) in enumerate(blocks)]

            def D1(d):
                cc, bi, b0, n = d['cc'], d['bi'], d['b0'], d['n']
                pc, pcr = bank()
                d['pdn'], d['pdnr'] = bank()
                mm_group([wcr, wdir] + hr_all(bi), [pcr, d['pdnr']], z_mms(pc, wc3, cc, bi, b0, n) + z_mms(d['pdn'], wdi3, cc, bi, b0, n))
                d['t1'], d['t1r'] = tmp()
                S.op(act, [pcr], [d['t1r']], lambda e: e.activation(out=d['t1'][:, 0:n], in_=pc[:, 0:n], func=AF.Copy))

            def D2(d):
                n = d['n']
                d['t2'], d['t2r'] = tmp()
                S.op(dve, [d['pdnr'], d['t1r']], [d['t2r']], lambda e: e.tensor_tensor(
                    out=d['t2'][:, 0:n], in0=d['pdn'][:, 0:n], in1=d['t1'][:, 0:n], op=ALU.mult))

            def D3(d):
                cc, bi, b0, n = d['cc'], d['bi'], d['b0'], d['n']
                t2, t2r = d['t2'], d['t2r']
                if b0 < 1024:
                    S.op(act, [t2r], [rQ[cc][bi]], lambda e: e.activation(out=Q[:, cc, 2 + b0:2 + b0 + n], in_=t2[:, 0:n], func=AF.Copy))
                    if g == 1 and bi == last_p:
                        S.op(dve, [t2r], [rSTQ], lambda e: e.tensor_copy(out=STQ[:, cc, 0:2], in_=t2[:, n - 2:n]))
                    if g == 0 and bi == last_p:
                        S.op(dve, [rQ[cc][bi]], [rHALOQ], lambda e: e.tensor_copy(out=HALOQ[:, l, cc, :], in_=Q[:, cc, 1024:1026]))
                else:
                    S.op(act, [t2r], [rQ[cc][bi]], lambda e: e.activation(
                        out=QS[:, cc, :, 2:10], in_=t2[:, 0:128].rearrange("p (s t) -> p s t", s=16), func=AF.Copy))
                    S.op(dve, [t2r], [rSTQS], lambda e: e.tensor_copy(
                        out=STQS[:, cc, :].rearrange("p (s j) -> p s j", s=16),
                        in_=t2[:, 0:128].rearrange("p (s t) -> p s t", s=16)[:, :, 6:8]))

            wavefront(itemsD, [D1, D2, D3])
            release(sc_)
            release(sdi)
            wbg, wbgr, sbg = getw('A')
            wbg3 = a3(wbg)
            itemsD2 = [dict(cc=cc, bi=bi, b0=b0, n=n) for cc in range(2) for bi, (b0, n) in enumerate(blocks)]

            def E1(d):
                cc, bi, b0, n = d['cc'], d['bi'], d['b0'], d['n']
                if bi == 0:
                    for k in range(3):
                        S.op(dve, [rCONST], [rDQK[k]], lambda e, k=k: e.tensor_scalar(
                            out=DIAGQ[:, k, :], in0=IDENT[:], scalar1=P256[:, cc, pbase + 37 + k:pbase + 38 + k], scalar2=None,
                            op0=ALU.mult))
                pb_, pbr = bank()
                d['py'], d['pyr'] = bank()
                py = d['py']
                if b0 < 1024:
                    rd = rDQK + [rQ[cc][bi], rQh[cc]] + ([rQ[cc][bi - 1]] if bi > 0 else [])
                    cm = [(py[:, 0:n], DIAGQ[:, k, :], Q[:, cc, b0 + k:b0 + k + n], k == 0, k == 2) for k in range(3)]
                else:
                    rd = rDQK + [rQ[cc][bi], rQSh]
                    cm = [(py[:, 0:128].rearrange("p (s t) -> p s t", s=16), DIAGQ[:, k, :], QS[:, cc, :, k:k + 8], k == 0, k == 2)
                          for k in range(3)]
                mm_group([wbgr] + hr_all(bi) + rd, [pbr, d['pyr']], z_mms(pb_, wbg3, cc, bi, b0, n) + cm)
                d['t1'], d['t1r'] = tmp()
                S.op(act, [pbr], [d['t1r']], lambda e: e.activation(out=d['t1'][:, 0:n], in_=pb_[:, 0:n], func=AF.Copy))

            def E2(d):
                cc, bi, b0, n = d['cc'], d['bi'], d['b0'], d['n']
                S.op(dve, [d['pyr'], d['t1r']], [RR[6 + cc][bi]], lambda e: e.tensor_tensor(
                    out=RA[:, 6 + cc, b0:b0 + n], in0=d['py'][:, 0:n], in1=d['t1'][:, 0:n], op=ALU.mult))

            wavefront(itemsD2, [E1, E2])
            release(sbg)
            if g == 1:
                out_T([STQ[:, 0, 0:2], STQ[:, 1, 0:2]], [rSTQ], 2, lambda: dout['sc_p'][l])
            else:
                out_T([STQS[:, 0, :], STQS[:, 1, :]], [rSTQS], 32, lambda: dout['sc_s'][l])

            for mp in range(4):
                wo, wor, so = getw('A')
                wo3 = a3(wo)
                hook = norm_hook(g, (l * 4 + 2, False))
                for mm, bi in m_order(mp == 3, len(blocks)):
                    m = mp * 2 + mm
                    for (b0, n) in [blocks[bi]]:
                        po, por = bank()
                        mm_group([wor] + [RR[kc][bi] for kc in range(KC)], [por],
                                 [(po[:, 0:n], wo3[:, kc, mm * 128:(mm + 1) * 128], RA[:, kc, b0:b0 + n], kc == 0, kc == KC - 1)
                                  for kc in range(KC)])
                        S.op(dve, [por, XR[m][bi]], [XR[m][bi]], lambda e: e.tensor_tensor(
                            out=X[:, m, b0:b0 + n], in0=po[:, 0:n], in1=X[:, m, b0:b0 + n], op=ALU.add))
                    if mp == 3 and mm == 1 and bi > 0:
                        hook(bi - 1)
                if mp == 3:
                    hook(len(blocks) - 1)
                release(so)

        RAall = [RR[j][b_] for j in range(8) for b_ in range(NB)]
        xstage = [(P1[:, 0:1024], [rP1], S.new_prod("xst0")),
                  (P2[:, 0:1024], [rP2], S.new_prod("xst1")),
                  (RBf[:, 0:1024], RBall, S.new_prod("xst2")),
                  (VTf[:, 0:1024], [rVT] + rVTc[0] + rVTc[1], S.new_prod("xst3")),
                  (DIAGf[:, 0:1024], [rDIAG] + rDK, S.new_prod("xst4")),
                  (DIAG2f[:, 0:1024], rDK2, S.new_prod("xst5"))]
        xseq = [0, 1, 2, 3, 4, 5, 0, 1, 2]

        def x_src(g, tt):
            p0 = g * 1024
            if tt < 8:
                return din['xp'][p0 + tt * 128:p0 + (tt + 1) * 128, :]
            return din['xs'][:, :]

        def x_dma(g, tt, after=None):
            ap_, res_, sem_ = xstage[xseq[tt]]
            S.dma(sp, sem_, [(after or [], res_, lambda e: e.dma_start(out=ap_, in_=x_src(g, tt)))])

        def x_load(g, pre_done):
            ntile = 9 if g == 0 else 8
            if not pre_done:
                for tt in range(4):
                    x_dma(g, tt)
                pump()
                for tt in range(4, 6):
                    x_dma(g, tt, after=list(xstage[0][1]))
            for tt in range(ntile):
                ap_, res_, sem_ = xstage[xseq[tt]]
                for q in range(2):
                    pb, pr = bank()
                    S.op(pe, res_ + [rCONST], [pr], lambda e: [
                        e.transpose(out=pb[:, j * 128:(j + 1) * 128], in_=ap_[:, (q * 4 + j) * 128:(q * 4 + j + 1) * 128],
                                    identity=IDENT[:]) for j in range(4)][-1])
                    dres = [XR[c][min(tt // 4, 2)] for c in range(q * 4, q * 4 + 4)]
                    dst = X[:, q * 4:q * 4 + 4, tt * 128:(tt + 1) * 128]
                    if (tt + q) % 2 == 0:
                        S.op(act, [pr], dres, lambda e: e.activation(
                            out=dst, in_=pb[:, 0:512].rearrange("p (c t) -> p c t", c=4), func=AF.Copy))
                    else:
                        S.op(dve, [pr], dres, lambda e: e.tensor_copy(
                            out=dst, in_=pb[:, 0:512].rearrange("p (c t) -> p c t", c=4)))
                if tt + 6 < ntile:
                    x_dma(g, tt + 6)

        pt_st = {}

        def pt_dma(g, l):
            p0 = g * 1024
            for k in range(2):
                S.dma(sp, sem_xs[k], [([], [XSR[k]], lambda e: e.dma_start(
                    out=XS[k][:, :].rearrange("p (t c) -> p t c", t=4),
                    in_=din['pp'][l, p0 + k * 512:p0 + (k + 1) * 512, :].rearrange("(t p) c -> p t c", p=128)))])
            if g == 0:
                k = state['ost']
                state['ost'] = (k + 1) % 3
                pt_st['k'] = k
                S.dma(sp, sem_ost[k], [([], [OSTR[k]], lambda e: e.dma_start(out=OST[k][:, :], in_=din['psm'][l]))])

        def pt_transposes(g, l):
            ntile = 9 if g == 0 else 8
            for tt in range(ntile):
                if tt < 8:
                    src, sres = XS[tt // 4][:, (tt % 4) * 256:(tt % 4 + 1) * 256], [XSR[tt // 4]]
                else:
                    src, sres = OST[pt_st['k']][:, :], [OSTR[pt_st['k']]]
                pb, pr = bank()
                S.op(pe, sres + [rCONST], [pr], lambda e: [
                    e.transpose(out=pb[:, j * 128:(j + 1) * 128], in_=src[:, j * 128:(j + 1) * 128], identity=IDENT[:])
                    for j in range(2)][-1])
                if tt % 2 == 0:
                    S.op(act, [pr], [rPT], lambda e: e.activation(
                        out=PT[:, 0:2, tt * 128:(tt + 1) * 128], in_=pb[:, 0:256].rearrange("p (c t) -> p c t", c=2), func=AF.Copy))
                else:
                    S.op(dve, [pr], [rPT], lambda e: e.tensor_copy(
                        out=PT[:, 0:2, tt * 128:(tt + 1) * 128], in_=pb[:, 0:256].rearrange("p (c t) -> p c t", c=2)))

        def halo_dma(l):
            items = []
            for i in range(4):
                items.append(([], [rDIAG] + rDK, lambda e, i=i: e.dma_start(out=DIAGf[0:120, i * 256:(i + 1) * 256], in_=din['sta'][l, i * 120:(i + 1) * 120, :])))
            items.append(([], [rDIAG] + rDK, lambda e: e.dma_start(out=DIAGf[0:32, 1024:1280], in_=din['sts'][l])))
            for i in range(2):
                items.append(([], [rVT] + rVTc[0] + rVTc[1], lambda e, i=i: e.dma_start(out=VTf[0:120, i * 256:(i + 1) * 256], in_=din['stp'][l, i * 120:(i + 1) * 120, :])))
            S.dma(sp, sem_misc2, items)

        def halo_T(src, sres, nrows, evac):
            pb, pr = bank()
            S.op(pe, sres + [rCONST], [pr], lambda e: [
                e.transpose(out=pb[:, cc * 128:cc * 128 + nrows], in_=src[0:nrows, cc * 128:(cc + 1) * 128],
                            identity=IDENT[0:nrows, 0:nrows]) for cc in range(2)][-1])
            for cc in range(2):
                reads, writes, fn = evac(cc, pb[:, cc * 128:cc * 128 + nrows])
                S.op(act, [pr] + reads, writes, fn)

        def ple(g, l, nxt=None):
            blocks = blocks_of(g)
            rmsnorm(g, l * 4 + 3)
            wp, wpr, spp = getw('A')
            wp3 = wp[:, 0:2048].rearrange("p (k n) -> p k n", k=2)
            for mp in range(4):
                wgt, wgtr, sgt = getw('A')
                wgt3 = a3(wgt)
                hook = norm_hook(g, nxt)
                for mm, bi in m_order(mp in (0, 3), len(blocks)):
                    m = mp * 2 + mm
                    for (b0, n) in [blocks[bi]]:
                        pg, pgr = bank()
                        pq, pqr = bank()
                        mm_group([wgtr, wpr, rPT] + hr_all(bi), [pgr, pqr],
                                 [(pg[:, 0:n], wgt3[:, kc, mm * 128:(mm + 1) * 128], H[:, kc, b0:b0 + n], kc == 0, kc == KC - 1) for kc in range(KC)] +
                                 [(pq[:, 0:n], wp3[:, k2, m * 128:(m + 1) * 128], PT[:, k2, b0:b0 + n], k2 == 0, k2 == 1) for k2 in range(2)])
                        t1, t1r = tmp()
                        S.op(act, [pgr], [t1r], lambda e: e.activation(out=t1[:, 0:n], in_=pg[:, 0:n], func=AF.Sigmoid))
                        t2, t2r = tmp()
                        S.op(dve, [pqr, t1r], [t2r], lambda e: e.tensor_tensor(out=t2[:, 0:n], in0=pq[:, 0:n], in1=t1[:, 0:n], op=ALU.mult))
                        S.op(dve, [t2r, XR[m][bi]], [XR[m][bi]], lambda e: e.tensor_tensor(
                            out=X[:, m, b0:b0 + n], in0=t2[:, 0:n], in1=X[:, m, b0:b0 + n], op=ALU.add))
                    if mp == 3 and mm == 1 and bi > 0:
                        hook(bi - 1)
                if mp == 3:
                    hook(len(blocks) - 1)
                release(sgt)
            release(spp)

        def final_store(g):
            p0 = g * 1024

            def store_tiles(bi, b0, n):
                for ti in range(n // 128):
                    tt = b0 // 128 + ti
                    k = state['xs']
                    state['xs'] = (k + 1) % 2
                    for q in range(2):
                        pb, pr = bank()
                        S.op(pe, YFBs[bi % 2][1] + [rCONST], [pr], lambda e: [
                            e.transpose(out=pb[:, j * 128:(j + 1) * 128], in_=YFBs[bi % 2][0][:, q * 4 + j, ti * 128:(ti + 1) * 128],
                                        identity=IDENT[:]) for j in range(4)][-1])
                        if q == 0:
                            S.op(act, [pr], [XSR[k]], lambda e: e.activation(out=XS[k][:, 0:512], in_=pb[:, 0:512], func=AF.Copy))
                        else:
                            S.op(dve, [pr], [XSR[k]], lambda e: e.tensor_copy(out=XS[k][:, 512:1024], in_=pb[:, 0:512]))
                    if tt < 8:
                        dst = dout['y_p'][p0 + tt * 128:p0 + (tt + 1) * 128, :]
                    else:
                        dst = dout['y_s'][:, :]
                    S.dma(sp, sem_xs[k], [([XSR[k]], [], lambda e: e.dma_start(out=dst, in_=XS[k][:, :]))], is_output=True)
            rmsnorm(g, 8, final=True, tile_cb=store_tiles)

        for g in range(2):
            x_load(g, pre_done=(g == 1))
            if g == 0:
                setup_dmas()
            for l in range(L):
                pt_dma(g, l)
                if g == 0:
                    halo_dma(l)
                def mid(g=g, l=l):
                    if g == 0 and l == 0:
                        late_setup()
                        late_setup_sgu()
                    pt_transposes(g, l)
                    mixer_prep(g, l)
                ffn(g, l, 1, mid_cb=mid, nxt=(l * 4 + 1, False))
                mixers(g, l)
                ffn(g, l, 2, nxt=(l * 4 + 3, False))
                if g == 0 and l == L - 1:
                    for tt in range(6):
                        x_dma(1, tt)
                ple(g, l, nxt=((l + 1) * 4, False) if l < L - 1 else (8, True))
            final_store(g)

        assert wst['next_use'] == len(stream), (wst['next_use'], len(stream))
        done = set()
        for sem in S.out_marks:
            if id(sem) in done:
                continue
            done.add(id(sem))
            sp.eng.wait_ge(sem.h, sem.count)
    return nc


_CACHE = {}


def kernel(**inputs):
    f32 = lambda a: np.ascontiguousarray(np.asarray(a, dtype=np.float32))
    x_prompt = f32(inputs['x_prompt'])
    x_sample = f32(inputs['x_sample'])
    p_prompt = f32(inputs['p_prompt'])
    p_sample = f32(inputs['p_sample'])
    st_a = f32(inputs['state_conv_a'])
    st_p = f32(inputs['state_pool'])
    st_s = f32(inputs['state_short_conv'])
    weights = {n: f32(inputs[n]) for n in WEIGHT_NAMES}

    if 'nc' not in _CACHE:
        _CACHE['nc'] = build_program()
    nc = _CACHE['nc']
    in_maps = []
    for c in range(NCORES):
        sl = slice(c * 16, (c + 1) * 16)
        m = {
            'xp': x_prompt[c],
            'xs': x_sample[sl].reshape(128, 1024),
            'pp': np.ascontiguousarray(p_prompt[:, c]),
            'psm': np.ascontiguousarray(p_sample[:, sl].reshape(2, 128, 256)),
            'sta': np.ascontiguousarray(st_a[:, sl].reshape(2, 480, 256)),
            'stp': np.ascontiguousarray(st_p[:, sl].reshape(2, 240, 256)),
            'sts': np.ascontiguousarray(st_s[:, sl].reshape(2, 32, 256)),
        }
        m.update(weights)
        in_maps.append(m)
    res = run_bass_kernel_spmd(nc, in_maps, core_ids=list(range(NCORES)))
    R = res.results
    y_prompt = np.stack([R[c]['y_p'] for c in range(NCORES)], axis=0)
    y_sample = np.concatenate([R[c]['y_s'].reshape(16, 8, 1024) for c in range(NCORES)], axis=0)
    ca_p = np.stack([R[c]['ca_p'] for c in range(NCORES)], axis=1)
    ca_s = np.concatenate([R[c]['ca_s'].reshape(2, 16, 30, 256) for c in range(NCORES)], axis=1)
    po_p = np.stack([R[c]['po_p'] for c in range(NCORES)], axis=1)
    po_s = np.concatenate([R[c]['po_s'].reshape(2, 16, 15, 256) for c in range(NCORES)], axis=1)
    sc_p = np.stack([R[c]['sc_p'] for c in range(NCORES)], axis=1)
    sc_s = np.concatenate([R[c]['sc_s'].reshape(2, 16, 2, 256) for c in range(NCORES)], axis=1)
    cv_s = np.concatenate([R[c]['cv_s'].reshape(2, 16, 8, 256) for c in range(NCORES)], axis=1)
    outs = (y_prompt, y_sample, ca_p, ca_s, po_p, po_s, sc_p, sc_s, cv_s)
    return tuple(np.ascontiguousarray(o, dtype=np.float32) for o in outs)
```

```python
import numpy as np
from contextlib import ExitStack
import concourse.bass as bass
import concourse.mybir as mybir
from concourse.bass_utils import run_bass_kernel_spmd

F32 = mybir.dt.float32
BF16 = mybir.dt.bfloat16
AF = mybir.ActivationFunctionType
ALU = mybir.AluOpType

NCORES = 8
D = 1024
KC = 8
FF = 2816
FC = 22
HALF = 11
TMAX = 1152
EPS = 1e-6
L = 2

WEIGHT_NAMES = ['norm_ffn1', 'w_ffn1_gate', 'w_ffn1_up', 'w_ffn1_down', 'norm_mix', 'w_mix_in',
                'conv_a_w', 'conv_a_b', 'norm_a_g', 'norm_a_b', 'pool_w', 'pool_scale',
                'sgu_norm_g', 'sgu_norm_b', 'sgu_w', 'sgu_b', 'short_conv_w', 'w_mix_out',
                'norm_ffn2', 'w_ffn2_gate', 'w_ffn2_up', 'w_ffn2_down',
                'norm_ple', 'w_ple_gate', 'w_ple_proj', 'norm_final']
WEIGHT_SHAPES = {
    'norm_ffn1': [2, 1024], 'w_ffn1_gate': [2, 1024, 2816], 'w_ffn1_up': [2, 1024, 2816],
    'w_ffn1_down': [2, 2816, 1024], 'norm_mix': [2, 1024], 'w_mix_in': [2, 1024, 2048],
    'conv_a_w': [2, 31, 256], 'conv_a_b': [2, 256], 'norm_a_g': [2, 256], 'norm_a_b': [2, 256],
    'pool_w': [2, 4, 64, 64], 'pool_scale': [2, 256], 'sgu_norm_g': [2, 256], 'sgu_norm_b': [2, 256],
    'sgu_w': [2, 4, 128, 128], 'sgu_b': [2, 4, 128], 'short_conv_w': [2, 3, 256],
    'w_mix_out': [2, 1024, 1024], 'norm_ffn2': [2, 1024], 'w_ffn2_gate': [2, 1024, 2816],
    'w_ffn2_up': [2, 1024, 2816], 'w_ffn2_down': [2, 2816, 1024], 'norm_ple': [2, 1024],
    'w_ple_gate': [2, 1024, 1024], 'w_ple_proj': [2, 256, 1024], 'norm_final': [1024],
}
CORE_IN_SHAPES = {
    'xp': [2048, 1024], 'xs': [128, 1024], 'pp': [2, 2048, 256], 'psm': [2, 128, 256],
    'sta': [2, 480, 256], 'stp': [2, 240, 256], 'sts': [2, 32, 256],
}
CORE_OUT_SHAPES = {
    'y_p': [2048, 1024], 'y_s': [128, 1024], 'ca_p': [2, 30, 256], 'ca_s': [2, 480, 256],
    'po_p': [2, 15, 256], 'po_s': [2, 240, 256], 'sc_p': [2, 2, 256], 'sc_s': [2, 32, 256],
    'cv_s': [2, 128, 256],
}


class Prod:
    def __init__(self, h, name):
        self.h = h
        self.count = 0
        self.name = name


class EngW:
    def __init__(self, name, eng, prod):
        self.name = name
        self.eng = eng
        self.prod = prod
        self.waited = {}


class Res:
    __slots__ = ('name', 'w', 'r', 'gen')

    def __init__(self, name):
        self.name = name
        self.w = None
        self.r = {}


class GenRes:
    __slots__ = ('base', 'gen')

    def __init__(self, base):
        self.base = base
        base.gen = getattr(base, 'gen', 0) + 1
        self.gen = base.gen

    def _chk(self):
        assert self.gen == self.base.gen, f"stale ring buffer use: {self.base.name}"
        return self.base

    @property
    def w(self):
        return self._chk().w

    @w.setter
    def w(self, v):
        self._chk().w = v

    @property
    def r(self):
        return self._chk().r

    @r.setter
    def r(self, v):
        self._chk().r = v


class Sched:
    def __init__(self, nc, es):
        self.nc = nc
        self.es = es
        self.nsem = 0
        mk = lambda n, e: EngW(n, e, self.new_prod(n))
        self.pe = mk('pe', nc.tensor)
        self.act = mk('act', nc.scalar)
        self.dve = mk('dve', nc.vector)
        self.pool = mk('pool', nc.gpsimd)
        self.sp = mk('sp', nc.sync)
        self.out_marks = []

    def new_prod(self, name):
        h = self.es.enter_context(self.nc.semaphore(name + str(self.nsem)))
        self.nsem += 1
        return Prod(h, name)

    def _deps(self, ew, reads, writes):
        deps = {}

        def add(p, c):
            if deps.get(p, 0) < c:
                deps[p] = c
        for r in reads:
            if r.w is not None:
                add(*r.w)
        for w in writes:
            if w.w is not None:
                add(*w.w)
            for p, c in w.r.items():
                if p is ew.prod and ew.name == 'pe':
                    continue
                add(p, c)
        for p, c in deps.items():
            if p is ew.prod and ew.name == 'pe':
                continue
            if ew.waited.get(p, 0) < c:
                ew.eng.wait_ge(p.h, c)
                ew.waited[p] = c
                self.nwaits = getattr(self, 'nwaits', 0) + 1
                if p is ew.prod:
                    self.nself = getattr(self, 'nself', 0) + 1

    def _mark(self, mark, reads, writes):
        p, c = mark
        for r in reads:
            if r.r.get(p, 0) < c:
                r.r[p] = c
        for w in writes:
            w.w = mark
            w.r = {}

    def op(self, ew, reads, writes, fn):
        self._deps(ew, reads, writes)
        inst = fn(ew.eng)
        ew.prod.count += 1
        inst.then_inc(ew.prod.h, 1)
        self._mark((ew.prod, ew.prod.count), reads, writes)

    def dma(self, qew, sem, items, is_output=False):
        allr, allw = [], []
        for reads, writes, fn in items:
            allr += reads
            allw += writes
        self._deps(qew, allr, allw)
        if sem.count and qew.waited.get(sem, 0) < sem.count:
            qew.eng.wait_ge(sem.h, sem.count)
            qew.waited[sem] = sem.count
        for reads, writes, fn in items:
            inst = fn(qew.eng)
            sem.count += 16
            inst.then_inc(sem.h, 16)
        self._mark((sem, sem.count), allr, allw)
        if is_output:
            self.out_marks.append(sem)


def build_program():
    nc = bass.Bass("TRN2", target_bir_lowering=False)
    din = {}
    for n, s in CORE_IN_SHAPES.items():
        din[n] = nc.dram_tensor(n, s, F32, kind="ExternalInput").ap()
    for n in WEIGHT_NAMES:
        din[n] = nc.dram_tensor(n, WEIGHT_SHAPES[n], F32, kind="ExternalInput").ap()
    dout = {}
    for n, s in CORE_OUT_SHAPES.items():
        dout[n] = nc.dram_tensor(n, s, F32, kind="ExternalOutput").ap()

    with ExitStack() as es:
        S = Sched(nc, es)
        pe, act, dve, pool, sp = S.pe, S.act, S.dve, S.pool, S.sp

        def sb(name, shape, dt):
            return es.enter_context(nc.sbuf_tensor(name, shape, dt))

        X = sb("X", [128, KC, TMAX], F32)
        Hf = sb("Hf", [128, KC * TMAX // 2], F32)
        H = Hf[:].bitcast(BF16).rearrange("p (c t) -> p c t", c=KC)
        YFB = Hf[:, 0:KC * 512].rearrange("p (c t) -> p c t", c=KC)
        RAf = sb("RAf", [128, 8 * TMAX // 2], F32)
        RA = RAf[:].bitcast(BF16).rearrange("p (s t) -> p s t", s=8)
        RBf = sb("RBf", [128, 3 * TMAX // 2], F32)
        RB = RBf[:].bitcast(BF16).rearrange("p (s t) -> p s t", s=3)
        ZW = 15 + 1024
        ZB = RBf[:, 0:ZW]
        ZBS = RBf[:, ZW:ZW + 16 * 23].rearrange("p (s t) -> p s t", s=16)
        P1 = sb("P1", [128, ZW + 16 * 23], F32)
        P2 = sb("P2", [128, ZW + 16 * 23], F32)
        HA = sb("HA", [128, 2, 30 + 1024], BF16)
        HAS = sb("HAS", [128, 2, 16, 38], BF16)
        Q = sb("Q", [128, 2, 2 + 1024], BF16)
        QS = sb("QS", [128, 2, 16, 10], BF16)
        DB = sb("DB", [128, TMAX], BF16)
        VTf = sb("VTf", [128, 9 * 128], F32)
        VT = VTf[:].bitcast(BF16).rearrange("p (a c) -> p a c", a=9)
        PT = sb("PT", [128, 2, TMAX], BF16)
        WA = [sb(f"WA{i}", [128, 2048], BF16) for i in range(4)]
        WD = [sb(f"WD{i}", [128, HALF * 256], BF16) for i in range(2)]
        XS = [sb(f"XS{i}", [128, 1024], F32) for i in range(2)]
        OST = [sb(f"OST{i}", [128, 256], F32) for i in range(3)]
        NTMP = 8
        TMP = [sb(f"TMP{i}", [128, 512], F32) for i in range(NTMP)]
        SQf = sb("SQf", [128, KC * 256], F32)
        SQ = SQf[:].bitcast(BF16).rearrange("p (c t) -> p c t", c=KC)
        TMPX = TMP + [SQf[:, i * 512:(i + 1) * 512] for i in range(4)]
        DIAGf = sb("DIAGf", [128, 31 * 64], F32)
        DIAG = DIAGf[:].bitcast(BF16).rearrange("p (k c) -> p k c", k=31)
        DIAG2f = sb("DIAG2f", [128, 31 * 64], F32)
        DIAG2 = DIAG2f[:].bitcast(BF16).rearrange("p (k c) -> p k c", k=31)
        DIAGc = [DIAG, DIAG2]
        DIAGQ = sb("DIAGQ", [128, 3, 128], BF16)
        IDENT = sb("IDENT", [128, 128], F32)
        ONES = sb("ONES", [128, 128], BF16)
        BD = sb("BD", [128, 128], F32)
        IMBD = sb("IMBD", [128, 128], F32)
        TRIL = sb("TRIL", [128, 128], F32)
        MASKS = sb("MASKS", [128, 128], F32)
        EPSV = sb("EPSV", [128, 1], F32)
        DUMV = sb("DUMV", [128, 1], F32)
        G1024 = sb("G1024", [128, KC, 16], F32)
        P256 = sb("P256", [128, 2, 80], F32)
        WT = sb("WT", [128, L, 4, 128], BF16)
        WTS = sb("WTS", [128, L, 4, 128], BF16)
        BSP = sb("BSP", [128, L, 2, 128], F32)
        BSS = sb("BSS", [128, L, 2, 128], F32)
        BDP = sb("BDP", [128, L, 2, 128], BF16)
        CORR = sb("CORR", [128, 2, 16], F32)
        RS8 = sb("RS8", [128, L, 4, 8], F32)
        HALOA = sb("HALOA", [128, L, 2, 30], BF16)
        HALOB = sb("HALOB", [128, L, 2, 15], F32)
        HALOQ = sb("HALOQ", [128, L, 2, 2], BF16)
        STA = sb("STA", [128, 2, 32], F32)
        STAS = sb("STAS", [128, 2, 128], F32)
        STB = sb("STB", [128, 2, 16], F32)
        STBS = sb("STBS", [128, 2, 128], F32)
        STQ = sb("STQ", [128, 2, 2], F32)
        STQS = sb("STQS", [128, 2, 32], F32)
        CORT = sb("CORT", [128, 16], F32)
        PS = es.enter_context(nc.psum_tensor("PS", [128, 8, 512], F32))

        NB = 3
        XR = [[Res(f"X{c}_{b}") for b in range(NB)] for c in range(KC)]
        HR = [[Res(f"H{c}_{b}") for b in range(NB)] for c in range(KC)]
        RR = [[Res(f"R{j}_{b}") for b in range(NB)] for j in range(HALF)]
        RBall = [RR[j][b] for j in (8, 9, 10) for b in range(NB)]
        PSR = [Res(f"PS{i}") for i in range(8)]
        YFB2 = RAf[:, 0:KC * 512].rearrange("p (c t) -> p c t", c=KC)
        YFBs = [(YFB, [HR[c][b_] for c in range(KC) for b_ in range(NB)]),
                (YFB2, [RR[j][b_] for j in range(8) for b_ in range(NB)])]
        TMPR = [Res(f"TMP{i}") for i in range(NTMP + 4)]
        SQR = TMPR[NTMP:NTMP + 4]
        WAR_ = [Res(f"WA{i}") for i in range(4)]
        WDR = [Res(f"WD{i}") for i in range(2)]
        XSR = [Res(f"XS{i}") for i in range(2)]
        OSTR = [Res(f"OST{i}") for i in range(3)]
        rSQ, rDIAG, rDIAGQ = Res("SQ"), Res("DIAG"), Res("DIAGQ")
        rDK = [Res(f"DK{k}") for k in range(31)]
        rDK2 = [Res(f"DK2_{k}") for k in range(31)]
        rDKc = [rDK, rDK2]
        rDQK = [Res(f"DQK{k}") for k in range(3)]
        rCONST = Res("CONST")
        rC2 = Res("CONST2")
        rBDP, rBS = Res("BDP"), Res("BS")
        rDUM = Res("DUM")
        rP1, rP2, rDB, rVT, rPT = Res("P1"), Res("P2"), Res("DB"), Res("VT"), Res("PT")
        rHA = [[Res(f"HA{c}_{b}") for b in range(NB)] for c in range(2)]
        rHAh = [Res("HAh0"), Res("HAh1")]
        rHASh = Res("HASh")
        rQ = [[Res(f"Q{c}_{b}") for b in range(NB)] for c in range(2)]
        rQh = [Res("Qh0"), Res("Qh1")]
        rQSh = Res("QSh")
        rZBh, rZBSh = Res("ZBh"), Res("ZBSh")
        rVTc = [[Res(f"VT{c}_{b}") for b in range(NB)] for c in range(2)]
        rHALOA, rHALOB, rHALOQ = Res("HALOA"), Res("HALOB"), Res("HALOQ")
        rSTA, rSTAS, rSTB, rSTBS, rSTQ, rSTQS = (Res("STA"), Res("STAS"), Res("STB"), Res("STBS"),
                                                 Res("STQ"), Res("STQS"))
        rCORT = Res("CORT")

        state = {'bank': 0, 'tmp': 0, 'ost': 0, 'xs': 0}

        pinned = set()

        def bank(pin=False):
            i = state['bank']
            while i in pinned:
                i = (i + 1) % 8
            state['bank'] = (i + 1) % 8
            if pin:
                pinned.add(i)
            return PS[:, i, :], GenRes(PSR[i])

        def unpin(ps_res):
            pinned.discard(PSR.index(ps_res.base))

        tpinned = set()

        def tmp(pin=False, ext=False):
            nring = NTMP + 4 if ext else NTMP
            i = state['tmp'] % nring
            while i in tpinned:
                i = (i + 1) % nring
            state['tmp'] = (i + 1) % nring
            if pin:
                tpinned.add(i)
            return TMPX[i], GenRes(TMPR[i])

        def unpin_t(t_res):
            tpinned.discard(TMPR.index(t_res.base))

        sem_w = {('A', i): S.new_prod(f"wA{i}") for i in range(4)}
        sem_w.update({('D', i): S.new_prod(f"wD{i}") for i in range(2)})
        sem_xs = [S.new_prod(f"xs{i}") for i in range(2)]
        sem_ost = [S.new_prod(f"ost{i}") for i in range(3)]
        sem_misc = S.new_prod("misc")
        sem_d2d = S.new_prod("d2d")
        sem_setup = S.new_prod("setup")
        sem_setup2 = S.new_prod("setup2")
        sem_misc2 = S.new_prod("misc2")

        def pool_op(reads, writes, fn):
            S.op(pool, reads, writes, fn)

        def dve_op(reads, writes, fn):
            S.op(dve, reads, writes, fn)

        pool_op([], [rCONST], lambda e: e.memset(IDENT[:], 0.0))
        pool_op([rCONST], [rCONST], lambda e: e.affine_select(
            out=IDENT[:], in_=IDENT[:], pattern=[[-1, 128]], compare_op=ALU.not_equal, fill=1.0,
            base=0, channel_multiplier=1))
        pool_op([], [rCONST], lambda e: e.memset(ONES[:], 1.0 / 1024.0))
        pool_op([], [rCONST], lambda e: e.memset(EPSV[:], EPS))

        items = []
        PSTG2 = OST[0]
        for l in range(L):
            for i, n in enumerate(['norm_ffn1', 'norm_mix', 'norm_ffn2', 'norm_ple']):
                r = l * 4 + i
                for hf in range(2):
                    items.append(([], [TMPR[hf]], lambda e, n=n, l=l, r=r, hf=hf: e.dma_start(
                        out=TMP[hf][r:r + 1, :], in_=din[n][l:l + 1, hf * 512:(hf + 1) * 512])))
        for hf in range(2):
            items.append(([], [TMPR[hf]], lambda e, hf=hf: e.dma_start(
                out=TMP[hf][8:9, :], in_=din['norm_final'].rearrange("(a n) -> a n", a=1)[:, hf * 512:(hf + 1) * 512])))
        for l in range(L):
            b = l * 40
            items.append(([], [OSTR[0]], lambda e, l=l, b=b: e.dma_start(out=PSTG2[b:b + 31, 0:256], in_=din['conv_a_w'][l])))
            for i, n in enumerate(['conv_a_b', 'norm_a_g', 'norm_a_b', 'pool_scale', 'sgu_norm_g', 'sgu_norm_b']):
                items.append(([], [OSTR[0]], lambda e, n=n, l=l, r=b + 31 + i: e.dma_start(
                    out=PSTG2[r:r + 1, 0:256], in_=din[n][l:l + 1, :])))
            items.append(([], [OSTR[0]], lambda e, l=l, b=b: e.dma_start(out=PSTG2[b + 37:b + 40, 0:256], in_=din['short_conv_w'][l])))
        S.dma(act, sem_setup, items)
        for half in range(2):
            pb, pr = bank()
            S.op(pe, [TMPR[half], rCONST], [pr], lambda e, half=half, pb=pb: [
                e.transpose(out=pb[:, j * 16:j * 16 + 9], in_=TMP[half][0:9, j * 128:(j + 1) * 128],
                            identity=IDENT[0:9, 0:9]) for j in range(4)][-1])
            S.op(dve, [pr], [rCONST], lambda e, half=half, pb=pb: e.tensor_copy(
                out=G1024[:, half * 4:half * 4 + 4, 0:9], in_=pb[:, 0:64].rearrange("p (c r) -> p c r", c=4)[:, :, 0:9]))
        pb, pr = bank()
        S.op(pe, [OSTR[0], rCONST], [pr], lambda e, pb=pb: [
            e.transpose(out=pb[:, cc * 80:cc * 80 + 80], in_=PSTG2[0:80, cc * 128:(cc + 1) * 128],
                        identity=IDENT[0:80, 0:80]) for cc in range(2)][-1])
        S.op(dve, [pr], [rCONST], lambda e, pb=pb: e.tensor_copy(
            out=P256[:], in_=pb[:, 0:160].rearrange("p (c r) -> p c r", c=2)))

        def late_setup():
            dve_op([], [rC2], lambda e: e.memset(BD[:], 1.0 / 64.0))
            dve_op([rCONST, rC2, rBDP, rBS], [rC2], lambda e: e.memset(BD[0:64, 64:128], 0.0))
            dve_op([rCONST, rC2, rBDP, rBS], [rC2], lambda e: e.memset(BD[64:128, 0:64], 0.0))
            dve_op([rCONST, rC2, rBDP, rBS], [rC2], lambda e: e.tensor_tensor(out=IMBD[:], in0=IDENT[:], in1=BD[:], op=ALU.subtract))
            dve_op([], [rC2], lambda e: e.memset(TRIL[:], 1.0))
            dve_op([], [rC2], lambda e: e.memset(MASKS[:], 1.0))
            pool_op([rCONST, rC2, rBDP, rBS], [rC2], lambda e: e.affine_select(
                out=TRIL[:], in_=TRIL[:], pattern=[[1, 128]], compare_op=ALU.is_ge, fill=0.0,
                base=0, channel_multiplier=-1))
            MS3 = MASKS[:].rearrange("p (a b) -> p a b", a=16)
            pool_op([rCONST, rC2, rBDP, rBS], [rC2], lambda e: e.affine_select(
                out=MS3, in_=MS3, pattern=[[8, 16], [1, 8]], compare_op=ALU.is_ge, fill=0.0,
                base=0, channel_multiplier=-1))
            pool_op([rCONST, rC2, rBDP, rBS], [rC2], lambda e: e.affine_select(
                out=MS3, in_=MS3, pattern=[[-8, 16], [0, 8]], compare_op=ALU.is_ge, fill=0.0,
                base=0, channel_multiplier=1))
            dve_op([], [rHAh[0], rHAh[1]], lambda e: e.memset(HA[:, :, 0:30], 0.0))
            dve_op([], [rQh[0], rQh[1]], lambda e: e.memset(Q[:, :, 0:2], 0.0))
            wins = {(0, 0): 2, (0, 1): 4, (1, 0): 8, (1, 1): 16}
            for (cc, hf), win in wins.items():
                dve_op([rCONST, rC2, rBDP, rBS], [rC2], lambda e: e.memset(CORR[hf * 64:(hf + 1) * 64, cc, :], 1.0 / win))
                for t in range(win - 1):
                    dve_op([rCONST, rC2, rBDP, rBS], [rC2], lambda e: e.memset(CORR[hf * 64:(hf + 1) * 64, cc, t:t + 1], 1.0 / (t + 1)))

        def setup_dmas():
            dve_op([], [rBDP], lambda e: e.memset(BDP[:], 0.0))
            items = []
            for l in range(L):
                for h in range(4):
                    j = l * 4 + h
                    cc, hh = h // 2, h % 2
                    items.append(([], [rP1], lambda e, l=l, h=h, j=j: e.dma_start(out=P1[:, j * 128:(j + 1) * 128], in_=din['sgu_w'][l, h])))
                    items.append(([], [rP1], lambda e, l=l, h=h, j=j: e.dma_start(
                        out=P1[:, 1024 + j * 8:1024 + j * 8 + 8], in_=din['sgu_w'][l, h, 0:8, 0:8].partition_broadcast(16))))
                    items.append(([], [rBS], lambda e, l=l, h=h, cc=cc, hh=hh: e.dma_start(
                        out=BSP[hh * 64:(hh + 1) * 64, l, cc, :], in_=din['sgu_b'][l, h:h + 1, :].broadcast_to([64, 128]))))
                    items.append(([], [rBS], lambda e, l=l, h=h, cc=cc, hh=hh: e.dma_start(
                        out=BSS[hh * 64:(hh + 1) * 64, l, cc, :].rearrange("p (a t) -> p a t", a=16),
                        in_=din['sgu_b'][l, h, 0:8].partition_broadcast(16).partition_broadcast(64))))
            S.dma(sp, sem_setup2, items)
            items = []
            for l in range(L):
                for g_ in range(4):
                    cc, hh = g_ // 2, g_ % 2
                    items.append(([], [rBDP], lambda e, l=l, g_=g_, cc=cc, hh=hh: e.dma_start(
                        out=BDP[hh * 64:(hh + 1) * 64, l, cc, hh * 64:(hh + 1) * 64], in_=din['pool_w'][l, g_])))
            S.dma(pool, sem_misc, items)
            items = []
            for l in range(L):
                items.append(([], [], lambda e, l=l: e.dma_start(
                    out=dout['ca_s'][l].rearrange("(s j) c -> s j c", s=16)[:, 0:22, :],
                    in_=din['sta'][l].rearrange("(s j) c -> s j c", s=16)[:, 8:30, :])))
                items.append(([], [], lambda e, l=l: e.dma_start(
                    out=dout['po_s'][l].rearrange("(s j) c -> s j c", s=16)[:, 0:7, :],
                    in_=din['stp'][l].rearrange("(s j) c -> s j c", s=16)[:, 8:15, :])))
            S.dma(sp, sem_d2d, items, is_output=True)

        def late_setup_sgu():
            for l in range(L):
                for h in range(4):
                    j = l * 4 + h
                    pb, pr = bank()
                    S.op(pe, [rP1, rCONST], [pr], lambda e: e.transpose(
                        out=pb[:, 0:128], in_=P1[:, j * 128:(j + 1) * 128], identity=IDENT[:]))
                    S.op(dve, [pr, rCONST, rC2, rBDP, rBS], [rC2], lambda e: e.tensor_tensor(
                        out=WT[:, l, h, :], in0=pb[:, 0:128], in1=TRIL[:], op=ALU.mult))
                    S.op(dve, [rP1], [rP2], lambda e: e.tensor_copy(
                        out=P2[:, j * 128:(j + 1) * 128].rearrange("p (a s) -> p a s", a=16),
                        in_=P1[:, 1024 + j * 8:1024 + j * 8 + 8].unsqueeze(1).broadcast_to([128, 16, 8])))
                    pb2, pr2 = bank()
                    S.op(pe, [rP2, rCONST], [pr2], lambda e: e.transpose(
                        out=pb2[:, 0:128], in_=P2[:, j * 128:(j + 1) * 128], identity=IDENT[:]))
                    S.op(dve, [pr2, rCONST, rC2, rBDP, rBS], [rC2], lambda e: e.tensor_tensor(
                        out=WTS[:, l, h, :], in0=pb2[:, 0:128], in1=MASKS[:], op=ALU.mult))

        stream = []

        def a_tile(name, l, c0, ncol, nk=KC):
            def fn(e, buf):
                src = din[name][l].rearrange("(kc p) n -> p kc n", p=128)[:, :, c0:c0 + ncol]
                return e.dma_start(out=buf[:, 0:nk * ncol].rearrange("p (k n) -> p k n", k=nk), in_=src)
            return ('A', fn)

        def d_tile(name, l, half, mp):
            def fn(e, buf):
                src = din[name][l][half * HALF * 128:(half + 1) * HALF * 128, mp * 256:(mp + 1) * 256]
                src = src.rearrange("(j p) n -> p j n", p=128)
                return e.dma_start(out=buf[:, :].rearrange("p (j n) -> p j n", j=HALF), in_=src)
            return ('D', fn)

        def ffn_tiles(l, which):
            pre = f"w_ffn{which}_"
            for half in range(2):
                for tp in range(6):
                    c0 = (half * HALF + tp * 2) * 128
                    ncol = 256 if tp < 5 else 128
                    stream.append(a_tile(pre + 'gate', l, c0, ncol))
                    stream.append(a_tile(pre + 'up', l, c0, ncol))
                for mp in range(4):
                    stream.append(d_tile(pre + 'down', l, half, mp))

        for g in range(2):
            for l in range(L):
                ffn_tiles(l, 1)
                for part in (2, 0, 1, 4, 3, 6, 7, 5):
                    stream.append(a_tile('w_mix_in', l, part * 256, 256))
                for mp in range(4):
                    stream.append(a_tile('w_mix_out', l, mp * 256, 256))
                ffn_tiles(l, 2)
                stream.append(a_tile('w_ple_proj', l, 0, 1024, nk=2))
                for mp in range(4):
                    stream.append(a_tile('w_ple_gate', l, mp * 256, 256))

        wst = {'next_load': 0, 'next_use': 0, 'cnt': {'A': 0, 'D': 0}, 'slot_of': {}, 'free': {}}
        for i in range(4):
            wst['free'][('A', i)] = True
        for i in range(2):
            wst['free'][('D', i)] = True
        nslots = {'A': 4, 'D': 2}
        bufs = {'A': WA, 'D': WD}
        wres = {'A': WAR_, 'D': WDR}

        def pump():
            while wst['next_load'] < len(stream):
                i = wst['next_load']
                kind, fn = stream[i]
                cand = [kk for kk in range(nslots[kind]) if wst['free'][(kind, kk)]]
                if not cand:
                    break
                k = cand[0]
                wst['free'][(kind, k)] = False
                wst['cnt'][kind] += 1
                wst['slot_of'][i] = (kind, k)
                buf = bufs[kind][k]
                S.dma(pool, sem_w[(kind, k)], [(wst.pop('first_reads', []), [wres[kind][k]], lambda e, fn=fn, buf=buf: fn(e, buf))])
                wst['next_load'] += 1

        def getw(expect_kind):
            i = wst['next_use']
            assert i < wst['next_load'], "weight tile not loaded yet (slot starvation)"
            kind, k = wst['slot_of'][i]
            assert kind == expect_kind
            wst['next_use'] += 1
            return bufs[kind][k], wres[kind][k], (kind, k)

        def release(slot):
            wst['free'][slot] = True
            pump()

        def mm_group(reads, writes, mms):
            def fn(e):
                inst = None
                for (o, a, b, st, sp_) in mms:
                    inst = e.matmul(o, lhsT=a, rhs=b, start=st, stop=sp_)
                return inst
            S.op(pe, reads, writes, fn)

        def wavefront(items, stages, order=None):
            n_, m_ = len(items), len(stages)
            for step in range(n_ + m_ - 1):
                for s_ in (order or range(m_)):
                    i_ = step - s_
                    if 0 <= i_ < n_:
                        stages[s_](items[i_])

        def wavefront2(itemsA_, stagesA_, itemsB_, stagesB_, lag=0):
            nA, mA, nB, mB = len(itemsA_), len(stagesA_), len(itemsB_), len(stagesB_)
            for step in range(max(nA + mA - 1, lag + nB + mB - 1)):
                for s_ in range(mA):
                    i_ = step - s_
                    if 0 <= i_ < nA:
                        stagesA_[s_](itemsA_[i_])
                for s_ in range(mB):
                    i_ = step - lag - s_
                    if 0 <= i_ < nB:
                        stagesB_[s_](itemsB_[i_])

        def m_order(last, nb):
            if last:
                return [(mm, bi) for bi in range(nb) for mm in range(2)]
            return [(mm, bi) for mm in range(2) for bi in range(nb)]

        def blocks_of(g):
            return [(0, 512), (512, 512), (1024, 128)] if g == 0 else [(0, 512), (512, 512)]

        def xr_all(bi):
            return [XR[c][bi] for c in range(KC)]

        def hr_all(bi):
            return [HR[c][bi] for c in range(KC)]

        nst = {'key': None, 'done': set(), 'stat': {}}

        def norm_stats(g, gi, bi, final):
            b0, n = blocks_of(g)[bi]
            pb, pr = bank(pin=final)
            for hf in range(2):
                S.op(act, [XR[c][bi] for c in range(hf * 4, hf * 4 + 4)], SQR[hf * 2:hf * 2 + 2], lambda e, hf=hf: e.activation(
                    out=SQ[:, hf * 4:hf * 4 + 4, 0:n], in_=X[:, hf * 4:hf * 4 + 4, b0:b0 + n], func=AF.Square))
            S.op(act, [rCONST], [rDUM], lambda e: e.activation(out=DUMV[:, 0:1], in_=EPSV[:, 0:1], func=AF.Ln))
            for hf in range(2):
                mm_group(SQR[hf * 2:hf * 2 + 2] + [rCONST], [pr],
                         [(pb[:, 0:n], ONES[:], SQ[:, c, 0:n], c == 0, c == KC - 1) for c in range(hf * 4, hf * 4 + 4)])
            S.op(act, [pr, rCONST], [pr], lambda e: e.activation(
                out=pb[:, 0:n], in_=pb[:, 0:n], func=AF.Ln, bias=EPSV[:, 0:1], scale=1.0))
            S.op(act, [pr], [pr], lambda e: e.activation(out=pb[:, 0:n], in_=pb[:, 0:n], func=AF.Exp, scale=-0.5))
            nst['stat'][bi] = (pb, pr)

        def norm_apply(g, gi, bi, final, tile_cb=None):
            b0, n = blocks_of(g)[bi]
            pb, pr = nst['stat'][bi]
            for c in range(KC):
                if not final:
                    S.op(dve, [XR[c][bi], pr, rCONST], [HR[c][bi]], lambda e, c=c: e.scalar_tensor_tensor(
                        out=H[:, c, b0:b0 + n], in0=X[:, c, b0:b0 + n], scalar=G1024[:, c, gi:gi + 1],
                        in1=pb[:, 0:n], op0=ALU.mult, op1=ALU.mult))
                else:
                    S.op(dve, [XR[c][bi], pr, rCONST], YFBs[bi % 2][1], lambda e, c=c: e.scalar_tensor_tensor(
                        out=YFBs[bi % 2][0][:, c, 0:n], in0=X[:, c, b0:b0 + n], scalar=G1024[:, c, gi:gi + 1],
                        in1=pb[:, 0:n], op0=ALU.mult, op1=ALU.mult))
            if final:
                unpin(pr)
                tile_cb(bi, b0, n)

        def norm_hook(g, nxt):
            if nxt is None:
                return lambda bi: None
            gi, final = nxt
            key = (g, gi)

            def hook(bi):
                if nst['key'] != key:
                    nst['key'], nst['done'], nst['stat'] = key, set(), {}
                norm_stats(g, gi, bi, final)
                if not final:
                    norm_apply(g, gi, bi, final)
                nst['done'].add(bi)
            return hook

        def rmsnorm(g, gi, final=False, tile_cb=None):
            key = (g, gi)
            if nst['key'] != key:
                nst['key'], nst['done'], nst['stat'] = key, set(), {}
            nb = len(blocks_of(g))
            if final:
                for bi in range(nb):
                    if bi not in nst['done']:
                        norm_stats(g, gi, bi, final)
                for bi in range(nb):
                    norm_apply(g, gi, bi, final, tile_cb)
            else:
                for bi in range(nb):
                    if bi not in nst['done']:
                        norm_stats(g, gi, bi, final)
                        norm_apply(g, gi, bi, final)
            nst['key'] = None

        def rslot(j):
            return RA[:, j, :] if j < 8 else RB[:, j - 8, :]

        def ffn(g, l, which, mid_cb=None, nxt=None):
            blocks = blocks_of(g)
            rmsnorm(g, l * 4 + (0 if which == 1 else 2))
            for half in range(2):
                for tp in range(6):
                    ncol = 256 if tp < 5 else 128
                    wg, wgr, sg = getw('A')
                    wu, wur, su = getw('A')
                    wg3 = wg[:, 0:KC * ncol].rearrange("p (k n) -> p k n", k=KC)
                    wu3 = wu[:, 0:KC * ncol].rearrange("p (k n) -> p k n", k=KC)
                    njj = ncol // 128
                    if half == 0 and tp == 0:
                        jb_order = [(jj, bi) for bi in range(len(blocks)) for jj in range(njj)]
                    else:
                        jb_order = [(jj, bi) for jj in range(njj) for bi in range(len(blocks))]
                    for jj, bi in jb_order:
                        slot = tp * 2 + jj
                        for (b0, n) in [blocks[bi]]:
                            pg, pgr = bank()
                            pu, pur = bank()
                            if half == 0 and tp == 0 and jj == 0:
                                for kc in range(KC):
                                    mm_group([wgr, wur, HR[kc][bi]], [pgr, pur],
                                             [(pg[:, 0:n], wg3[:, kc, jj * 128:(jj + 1) * 128], H[:, kc, b0:b0 + n], kc == 0, kc == KC - 1),
                                              (pu[:, 0:n], wu3[:, kc, jj * 128:(jj + 1) * 128], H[:, kc, b0:b0 + n], kc == 0, kc == KC - 1)])
                            else:
                                mm_group([wgr, wur] + hr_all(bi), [pgr, pur],
                                         [(pg[:, 0:n], wg3[:, kc, jj * 128:(jj + 1) * 128], H[:, kc, b0:b0 + n], kc == 0, kc == KC - 1) for kc in range(KC)] +
                                         [(pu[:, 0:n], wu3[:, kc, jj * 128:(jj + 1) * 128], H[:, kc, b0:b0 + n], kc == 0, kc == KC - 1) for kc in range(KC)])
                            t1, t1r = tmp()
                            S.op(act, [pgr], [t1r], lambda e: e.activation(out=t1[:, 0:n], in_=pg[:, 0:n], func=AF.Silu))
                            S.op(dve, [t1r, pur], [RR[slot][bi]], lambda e: e.tensor_tensor(
                                out=rslot(slot)[:, b0:b0 + n], in0=pu[:, 0:n], in1=t1[:, 0:n], op=ALU.mult))
                    release(sg)
                    release(su)
                for mp in range(4):
                    wd, wdr, sd = getw('D')
                    wd3 = wd[:, :].rearrange("p (j n) -> p j n", j=HALF)
                    lastmp = (half == 1 and mp == 3)
                    hook = norm_hook(g, nxt)
                    for mm, bi in m_order(lastmp, len(blocks)):
                        m = mp * 2 + mm
                        for (b0, n) in [blocks[bi]]:
                            pd, pdr = bank()
                            mm_group([wdr] + [RR[s][bi] for s in range(HALF)], [pdr],
                                     [(pd[:, 0:n], wd3[:, s, mm * 128:(mm + 1) * 128], rslot(s)[:, b0:b0 + n], s == 0, s == HALF - 1)
                                      for s in range(HALF)])
                            S.op(dve, [pdr, XR[m][bi]], [XR[m][bi]], lambda e: e.scalar_tensor_tensor(
                                out=X[:, m, b0:b0 + n], in0=pd[:, 0:n], scalar=0.5, in1=X[:, m, b0:b0 + n],
                                op0=ALU.mult, op1=ALU.add))
                        if lastmp and mm == 1 and bi > 0:
                            hook(bi - 1)
                    if lastmp:
                        hook(len(blocks) - 1)
                    release(sd)
                if half == 0 and mid_cb is not None:
                    mid_cb()

        def z_mms(ps_ap, w3, cc, bi, b0, n):
            return [(ps_ap[:, 0:n], w3[:, kc, cc * 128:(cc + 1) * 128], H[:, kc, b0:b0 + n], kc == 0, kc == KC - 1)
                    for kc in range(KC)]

        def a3(w):
            return w[:, 0:KC * 256].rearrange("p (k n) -> p k n", k=KC)

        def out_T(srcs, src_res, nrows, dst_fn):
            pb, pr = bank()
            S.op(pe, src_res + [rCONST], [pr], lambda e: [
                e.transpose(out=pb[0:nrows, cc * 128:(cc + 1) * 128], in_=srcs[cc], identity=IDENT[:]) for cc in range(2)][-1])
            k = state['ost']
            state['ost'] = (k + 1) % 3
            S.op(act, [pr], [OSTR[k]], lambda e: e.activation(out=OST[k][0:nrows, :], in_=pb[0:nrows, 0:256], func=AF.Copy))
            S.dma(sp, sem_ost[k], [([OSTR[k]], [], lambda e: e.dma_start(out=dst_fn(), in_=OST[k][0:nrows, :]))], is_output=True)

        def load_T(src_ap, nrows, evac):
            k = state['ost']
            state['ost'] = (k + 1) % 3
            S.dma(sp, sem_ost[k], [([], [OSTR[k]], lambda e: e.dma_start(out=OST[k][0:nrows, :], in_=src_ap))])
            pb, pr = bank()
            S.op(pe, [OSTR[k], rCONST], [pr], lambda e: [
                e.transpose(out=pb[:, cc * 128:cc * 128 + nrows], in_=OST[k][0:nrows, cc * 128:(cc + 1) * 128],
                            identity=IDENT[0:nrows, 0:nrows]) for cc in range(2)][-1])
            for cc in range(2):
                ew, reads, writes, fn = evac(cc, pb[:, cc * 128:cc * 128 + nrows])
                S.op(ew, [pr] + reads, writes, fn)

        def mixer_prep(g, l):
            pbase = l * 40
            if g == 0:
                for i in range(4):
                    halo_T(DIAGf[:, i * 256:(i + 1) * 256], [rDIAG], 120, lambda cc, ps_ap, i=i: (
                        [], [rHASh], lambda e: e.activation(
                            out=HAS[:, cc, 4 * i:4 * i + 4, 0:30], in_=ps_ap.rearrange("p (s j) -> p s j", s=4), func=AF.Copy)))
                halo_T(DIAGf[:, 1024:1280], [rDIAG], 32, lambda cc, ps_ap: (
                    [], [rQSh], lambda e: e.activation(
                        out=QS[:, cc, :, 0:2], in_=ps_ap.rearrange("p (s j) -> p s j", s=16), func=AF.Copy)))
            else:
                S.op(dve, [rHALOA], [rHAh[0], rHAh[1]], lambda e: e.tensor_copy(out=HA[:, :, 0:30], in_=HALOA[:, l, :, :]))
                S.op(dve, [rHALOQ], [rQh[0], rQh[1]], lambda e: e.tensor_copy(out=Q[:, :, 0:2], in_=HALOQ[:, l, :, :]))

            for cc in (1, 0):
                for k in range(31):
                    wr = [rDKc[cc][k]] + ([rDIAG] if (k == 0 and cc == 0) else [])
                    S.op(dve, [rCONST] + ([rDIAG] if cc == 0 else []), wr, lambda e, k=k, cc=cc: e.tensor_scalar(
                        out=DIAGc[cc][:, k, :], in0=IDENT[:], scalar1=P256[:, cc, pbase + k:pbase + k + 1], scalar2=None,
                        op0=ALU.mult))


        def mixers(g, l):
            blocks = blocks_of(g)
            nbk = len(blocks)
            pbase = l * 40
            rmsnorm(g, l * 4 + 1)
            last_p = 1

            wb, wbr, sbk = getw('A')
            wb3 = a3(wb)
            inv = {(0, 0): 0.5, (0, 1): 0.25, (1, 0): 0.125, (1, 1): 0.0625}
            P1S = P1[:, ZW:].rearrange("p (s t) -> p s t", s=16)
            P2S = P2[:, ZW:].rearrange("p (s t) -> p s t", s=16)
            ZBc = [ZB, RAf[:, 2304:2304 + ZW]]
            ZBSc = [ZBS, RAf[:, 2304 + ZW:2304 + ZW + 16 * 23].rearrange("p (s t) -> p s t", s=16)]
            ZRc = [RBall, [RR[j][b_] for j in (4, 5, 6) for b_ in range(NB)]]
            DBc = [DB[:, :], RA[:, 7, :]]
            DBRc = [[rDB], [RR[7][b_] for b_ in range(NB)]]
            if g == 0:
                for i in range(2):
                    halo_T(VTf[:, i * 256:(i + 1) * 256], [rVT], 120, lambda c2, ps_ap, i=i: (
                        [], ZRc[c2], lambda e: e.activation(
                            out=ZBSc[c2][:, 8 * i:8 * i + 8, 0:15], in_=ps_ap.rearrange("p (s j) -> p s j", s=8), func=AF.Copy)))
            for cc in range(2):
                zb, zr_ = ZBc[cc], ZRc[cc]
                if g == 0:
                    S.op(dve, [], zr_, lambda e: e.memset(zb[:, 0:15], 0.0))
                else:
                    S.op(dve, [rHALOB], zr_, lambda e: e.tensor_copy(out=zb[:, 0:15], in_=HALOB[:, l, cc, :]))
            for bi, (b0, n) in enumerate(blocks):
                for cc in range(2):
                    zb, zbs, zr_ = ZBc[cc], ZBSc[cc], ZRc[cc]
                    pz, pzr = bank()
                    mm_group([wbr] + hr_all(bi), [pzr], z_mms(pz, wb3, cc, bi, b0, n))
                    if b0 < 1024:
                        S.op(act, [pzr], zr_, lambda e: e.activation(out=zb[:, 15 + b0:15 + b0 + n], in_=pz[:, 0:n], func=AF.Copy))
                    else:
                        S.op(act, [pzr], zr_, lambda e: e.activation(
                            out=zbs[:, :, 15:23], in_=pz[:, 0:128].rearrange("p (s t) -> p s t", s=16), func=AF.Copy))
            for cc in range(2):
                zb, zbs, zr_, db, dbr = ZBc[cc], ZBSc[cc], ZRc[cc], DBc[cc], DBRc[cc]
                zr = zr_

                def lvl(dst, dstS, src, srcS, shift, lo, plo, dres, sres):
                    S.op(pool, sres, dres, lambda e: e.tensor_tensor(
                        out=dst[plo:128, lo:ZW], in0=src[plo:128, lo:ZW], in1=src[plo:128, lo - shift:ZW - shift], op=ALU.add))
                    if g == 0:
                        S.op(pool, sres, dres, lambda e: e.tensor_tensor(
                            out=dstS[plo:128, :, lo:23], in0=srcS[plo:128, :, lo:23], in1=srcS[plo:128, :, lo - shift:23 - shift], op=ALU.add))
                if cc == 0:
                    lvl(P1, P1S, zb, zbs, 1, 1, 0, [rP1], zr)
                    lvl(P2, P2S, P1, P1S, 2, 3, 64, [rP2], [rP1])
                else:
                    lvl(P1, P1S, zb, zbs, 1, 1, 0, [rP1], zr)
                    lvl(P2, P2S, P1, P1S, 2, 3, 0, [rP2], [rP1])
                    lvl(P1, P1S, P2, P2S, 4, 7, 0, [rP1], [rP2])
                    lvl(P2, P2S, P1, P1S, 8, 15, 64, [rP2], [rP1])
                for hf in range(2):
                    Sb, SbS = (P1, P1S) if hf == 0 else (P2, P2S)
                    pl, ph = hf * 64, hf * 64 + 64
                    iv = inv[(cc, hf)]
                    if g == 0:
                        S.op(pool, [rP1, rP2, rCONST, rC2, rBDP, rBS], [rCORT], lambda e: e.tensor_tensor(
                            out=CORT[pl:ph, 0:15], in0=Sb[pl:ph, 15:30], in1=CORR[pl:ph, cc, 0:15], op=ALU.mult))
                    S.op(pool, [rP1, rP2], [rP1, rP2], lambda e: e.tensor_scalar(
                        out=Sb[pl:ph, 15:ZW], in0=Sb[pl:ph, 15:ZW], scalar1=iv, scalar2=0.0, op0=ALU.mult, op1=ALU.add))
                    S.op(pool, [rP1, rP2] + zr, dbr, lambda e: e.tensor_tensor(
                        out=db[pl:ph, 0:1024], in0=Sb[pl:ph, 15:ZW], in1=zb[pl:ph, 15:ZW], op=ALU.subtract))
                    if g == 0:
                        S.op(pool, [rP1, rP2], [rP1, rP2], lambda e: e.tensor_scalar(
                            out=SbS[pl:ph, :, 15:23], in0=SbS[pl:ph, :, 15:23], scalar1=iv, scalar2=0.0, op0=ALU.mult, op1=ALU.add))
                        S.op(pool, [rP1, rP2] + zr, dbr, lambda e: e.tensor_tensor(
                            out=db[pl:ph, 1024:1152].rearrange("p (s t) -> p s t", s=16), in0=SbS[pl:ph, :, 15:23],
                            in1=zbs[pl:ph, :, 15:23], op=ALU.subtract))
                        S.op(pool, [rCORT] + zr, dbr, lambda e: e.tensor_tensor(
                            out=db[pl:ph, 0:15], in0=CORT[pl:ph, 0:15], in1=zb[pl:ph, 15:30], op=ALU.subtract))
                if g == 0:
                    S.op(dve, zr, [rHALOB], lambda e: e.tensor_copy(out=HALOB[:, l, cc, :], in_=zb[:, ZW - 15:ZW]))
                    S.op(dve, zr, [rSTBS], lambda e: e.tensor_copy(
                        out=STBS[:, cc, :].rearrange("p (s t) -> p s t", s=16), in_=zbs[:, :, 15:23]))
                else:
                    S.op(dve, zr, [rSTB], lambda e: e.tensor_copy(out=STB[:, cc, 0:15], in_=zb[:, ZW - 15:ZW]))
            release(sbk)
            if g == 1:
                out_T([STB[:, 0, 0:15], STB[:, 1, 0:15]], [rSTB], 15, lambda: dout['po_p'][l])
            else:
                out_T([STBS[:, 0, :], STBS[:, 1, :]], [rSTBS], 128,
                      lambda: dout['po_s'][l].rearrange("(s j) c -> s j c", s=16)[:, 7:15, :])

            wv, wvr, sv = getw('A')
            wgt, wgtr, sgt = getw('A')
            wv3, wgt3 = a3(wv), a3(wgt)
            itemsA = [dict(cc=cc, bi=bi, b0=b0, n=n) for bi, (b0, n) in enumerate(blocks) for cc in range(2)]

            def A1(d):
                cc, bi, b0, n = d['cc'], d['bi'], d['b0'], d['n']
                d['pv'], d['pvr'] = bank(pin=True)
                d['pg'], d['pgr'] = bank()
                mm_group([wvr, wgtr] + hr_all(bi), [d['pvr'], d['pgr']],
                         z_mms(d['pv'], wv3, cc, bi, b0, n) + z_mms(d['pg'], wgt3, cc, bi, b0, n))
                d['t1'], d['t1r'] = tmp(pin=True, ext=True)
                S.op(act, [d['pgr']], [d['t1r']], lambda e: e.activation(out=d['t1'][:, 0:n], in_=d['pg'][:, 0:n], func=AF.Sigmoid))
                if d is itemsA[-1]:
                    release(sv)
                    release(sgt)

            def A2(d):
                n = d['n']
                d['t2'], d['t2r'] = tmp(pin=True, ext=True)
                S.op(dve, [d['pvr'], d['t1r']], [d['t2r']], lambda e: e.tensor_tensor(
                    out=d['t2'][:, 0:n], in0=d['pv'][:, 0:n], in1=d['t1'][:, 0:n], op=ALU.mult))
                unpin(d['pvr'])
                unpin_t(d['t1r'])

            def A3(d):
                cc, bi, b0, n = d['cc'], d['bi'], d['b0'], d['n']
                t2, t2r = d['t2'], d['t2r']
                if b0 < 1024:
                    S.op(act, [t2r], [rHA[cc][bi]], lambda e: e.activation(
                        out=HA[:, cc, 30 + b0:30 + b0 + n], in_=t2[:, 0:n], func=AF.Copy))
                    if g == 1 and bi == last_p:
                        S.op(dve, [t2r], [rSTA], lambda e: e.tensor_copy(out=STA[:, cc, 0:30], in_=t2[:, n - 30:n]))
                    if g == 0 and bi == last_p:
                        S.op(dve, [rHA[cc][bi]], [rHALOA], lambda e: e.tensor_copy(
                            out=HALOA[:, l, cc, :], in_=HA[:, cc, 1024:1054]))
                else:
                    S.op(act, [t2r], [rHA[cc][bi]], lambda e: e.activation(
                        out=HAS[:, cc, :, 30:38], in_=t2[:, 0:128].rearrange("p (s t) -> p s t", s=16), func=AF.Copy))
                    S.op(dve, [t2r], [rSTAS], lambda e: e.tensor_copy(out=STAS[:, cc, :], in_=t2[:, 0:128]))
                unpin_t(t2r)

            def A4(d):
                cc, bi, b0, n = d['cc'], d['bi'], d['b0'], d['n']
                pc, pcr = bank()
                DG, rdk = DIAGc[cc], rDKc[cc]
                if b0 < 1024:
                    rd = rdk + [rHA[cc][bi], rHAh[cc]] + ([rHA[cc][bi - 1]] if bi > 0 else [])
                    mm_group(rd, [pcr], [(pc[:, 0:n], DG[:, k, :], HA[:, cc, b0 + k:b0 + k + n], k == 0, k == 30) for k in range(31)])
                else:
                    mm_group(rdk + [rHA[cc][bi], rHASh], [pcr],
                             [(pc[:, 0:128].rearrange("p (s t) -> p s t", s=16), DG[:, k, :], HAS[:, cc, :, k:k + 8], k == 0, k == 30)
                              for k in range(31)])
                d['t3'], d['t3r'] = tmp(pin=True, ext=True)
                S.op(act, [pcr, rCONST, rC2, rBDP, rBS], [d['t3r']], lambda e: e.activation(
                    out=d['t3'][:, 0:n], in_=pc[:, 0:n], func=AF.Identity, bias=P256[:, cc, pbase + 31:pbase + 32], scale=1.0))

            def A5(d):
                n = d['n']
                d['pd'], d['pdr'] = bank(pin=True)
                mm_group([d['t3r'], rCONST, rC2, rBDP, rBS], [d['pdr']], [(d['pd'][:, 0:n], IMBD[:], d['t3'][:, 0:n], True, True)])
                d['t4'], d['t4r'] = tmp(pin=True, ext=True)
                S.op(act, [d['pdr']], [d['t4r']], lambda e: e.activation(out=d['t4'][:, 0:n], in_=d['pd'][:, 0:n], func=AF.Square))
                unpin_t(d['t3r'])

            def A6(d):
                cc, bi, b0, n = d['cc'], d['bi'], d['b0'], d['n']
                pw, pwr = bank()
                mm_group([d['t4r'], rCONST, rC2, rBDP, rBS], [pwr], [(pw[:, 0:n], BD[:], d['t4'][:, 0:n], True, True)])
                S.op(act, [pwr, rCONST, rC2, rBDP, rBS], [pwr], lambda e: e.activation(
                    out=pw[:, 0:n], in_=pw[:, 0:n], func=AF.Ln, bias=EPSV[:, 0:1], scale=1.0))
                t5, t5r = tmp(ext=True)
                S.op(act, [pwr], [t5r], lambda e: e.activation(out=t5[:, 0:n], in_=pw[:, 0:n], func=AF.Exp, scale=-0.5))
                t6, t6r = tmp(ext=True)
                S.op(dve, [d['pdr'], t5r], [t6r], lambda e: e.tensor_tensor(out=t6[:, 0:n], in0=d['pd'][:, 0:n], in1=t5[:, 0:n], op=ALU.mult))
                S.op(act, [t6r, rCONST, rC2, rBDP, rBS], [RR[cc][bi]], lambda e: e.activation(
                    out=RA[:, cc, b0:b0 + n], in_=t6[:, 0:n], func=AF.Silu,
                    bias=P256[:, cc, pbase + 33:pbase + 34], scale=P256[:, cc, pbase + 32:pbase + 33]))
                unpin(d['pdr'])
                unpin_t(d['t4r'])

            wavefront(itemsA, [A1, A2, A3, A4, A5, A6])
            wvv, wvvr, svv = getw('A')
            wuu, wuur, suu = getw('A')
            wvv3, wuu3 = a3(wvv), a3(wuu)
            pairsC = [[dict(cc=cc, bi=bi, b0=b0, n=n) for cc in range(2)] for bi, (b0, n) in enumerate(blocks)]

            def PC1(P):
                for d in P:
                    cc, bi, b0, n = d['cc'], d['bi'], d['b0'], d['n']
                    d['pv'], d['pvr'] = bank()
                    mm_group([wvvr] + hr_all(bi), [d['pvr']], z_mms(d['pv'], wvv3, cc, bi, b0, n))
                for d in P:
                    n = d['n']
                    d['t1'], d['t1r'] = tmp(pin=True, ext=True)
                    S.op(act, [d['pvr']], [d['t1r']], lambda e: e.activation(out=d['t1'][:, 0:n], in_=d['pv'][:, 0:n], func=AF.Gelu_apprx_tanh))
                if P is pairsC[-1]:
                    release(svv)

            def PC2(P):
                for d in P:
                    n = d['n']
                    d['pd'], d['pdr'] = bank(pin=True)
                    mm_group([d['t1r'], rCONST, rC2, rBDP, rBS], [d['pdr']], [(d['pd'][:, 0:n], IMBD[:], d['t1'][:, 0:n], True, True)])
                for d in P:
                    n = d['n']
                    d['t2'], d['t2r'] = tmp(pin=True, ext=True)
                    S.op(act, [d['pdr']], [d['t2r']], lambda e: e.activation(out=d['t2'][:, 0:n], in_=d['pd'][:, 0:n], func=AF.Square))
                    unpin_t(d['t1r'])

            def PC3(P):
                for d in P:
                    n = d['n']
                    d['pw'], d['pwr'] = bank()
                    mm_group([d['t2r'], rCONST, rC2, rBDP, rBS], [d['pwr']], [(d['pw'][:, 0:n], BD[:], d['t2'][:, 0:n], True, True)])
                for d in P:
                    n = d['n']
                    S.op(act, [d['pwr'], rCONST, rC2, rBDP, rBS], [d['pwr']], lambda e: e.activation(
                        out=d['pw'][:, 0:n], in_=d['pw'][:, 0:n], func=AF.Ln, bias=EPSV[:, 0:1], scale=1.0))
                for d in P:
                    n = d['n']
                    d['t3'], d['t3r'] = tmp(ext=True)
                    S.op(act, [d['pwr']], [d['t3r']], lambda e: e.activation(
                        out=d['t3'][:, 0:n], in_=d['pw'][:, 0:n], func=AF.Exp, scale=-0.5))
                for d in P:
                    cc, n = d['cc'], d['n']
                    d['t4'], d['t4r'] = tmp(ext=True)
                    S.op(dve, [d['pdr'], d['t3r'], rCONST], [d['t4r']], lambda e: e.scalar_tensor_tensor(
                        out=d['t4'][:, 0:n], in0=d['pd'][:, 0:n], scalar=P256[:, cc, pbase + 35:pbase + 36], in1=d['t3'][:, 0:n],
                        op0=ALU.mult, op1=ALU.mult))
                for d in P:
                    cc, n = d['cc'], d['n']
                    d['t5'], d['t5r'] = tmp(pin=True, ext=True)
                    S.op(act, [d['t4r'], rCONST], [d['t5r']], lambda e: e.activation(
                        out=d['t5'][:, 0:n], in_=d['t4'][:, 0:n], func=AF.Identity, bias=P256[:, cc, pbase + 36:pbase + 37], scale=1.0))
                    unpin(d['pdr'])
                    unpin_t(d['t2r'])

            def PC4(P):
                for d in P:
                    n = d['n']
                    nt = n // 128
                    t5 = d['t5']
                    d['pt'], d['ptr'] = bank()
                    pt = d['pt']
                    S.op(pe, [d['t5r'], rCONST, rC2, rBDP, rBS], [d['ptr']], lambda e: [
                        e.transpose(out=pt[:, ti * 128:(ti + 1) * 128], in_=t5[:, ti * 128:(ti + 1) * 128], identity=IDENT[:])
                        for ti in range(nt)][-1])
                for d in P:
                    cc, bi, b0, n = d['cc'], d['bi'], d['b0'], d['n']
                    nt = n // 128
                    tt0 = b0 // 128
                    pt = d['pt']
                    S.op(dve, [d['ptr']], [rVTc[cc][bi], rVT], lambda e: e.tensor_copy(
                        out=VT[:, tt0:tt0 + nt, cc * 128:(cc + 1) * 128], in_=pt[:, 0:n].rearrange("p (a c) -> p a c", a=nt)))
                    if b0 >= 1024:
                        k = state['ost']
                        state['ost'] = (k + 1) % 3
                        S.op(act, [d['ptr']], [OSTR[k]], lambda e: e.activation(
                            out=OST[k][:, 0:128], in_=pt[:, 0:128], func=AF.Copy))
                        S.dma(sp, sem_ost[k], [([OSTR[k]], [], lambda e: e.dma_start(
                            out=dout['cv_s'][l][:, cc * 128:(cc + 1) * 128], in_=OST[k][:, 0:128]))], is_output=True)
                    unpin_t(d['t5r'])

            def PC5(P):
                for d in P:
                    cc, bi, b0, n = d['cc'], d['bi'], d['b0'], d['n']
                    nt = n // 128
                    tt0 = b0 // 128
                    d['ps'], d['psr'] = bank()
                    mms = []
                    for ti in range(nt):
                        for hh in range(2):
                            h = 2 * cc + hh
                            wt = WTS[:, l, h, :] if b0 >= 1024 else WT[:, l, h, :]
                            mms.append((d['ps'][hh * 64:(hh + 1) * 64, ti * 128:(ti + 1) * 128],
                                        VT[:, tt0 + ti, h * 64:(h + 1) * 64], wt, True, True))
                    mm_group([rVTc[cc][bi], rCONST, rC2, rBDP, rBS], [d['psr']], mms)
                    d['pu'], d['pur'] = bank()
                    mm_group([wuur] + hr_all(bi), [d['pur']], z_mms(d['pu'], wuu3, cc, bi, b0, n))
                for d in P:
                    n = d['n']
                    d['t6'], d['t6r'] = tmp(ext=True)
                    S.op(act, [d['pur']], [d['t6r']], lambda e: e.activation(out=d['t6'][:, 0:n], in_=d['pu'][:, 0:n], func=AF.Gelu_apprx_tanh))
                for d in P:
                    cc, b0, n = d['cc'], d['b0'], d['n']
                    nt = n // 128
                    d['t7'], d['t7r'] = tmp(ext=True)
                    if b0 < 1024:
                        S.op(dve, [d['psr'], rCONST, rC2, rBDP, rBS], [d['t7r']], lambda e: e.tensor_tensor(
                            out=d['t7'][:, 0:n].rearrange("p (a t) -> p a t", a=nt), in0=d['ps'][:, 0:n].rearrange("p (a t) -> p a t", a=nt),
                            in1=BSP[:, l, cc, :].unsqueeze(1).broadcast_to([128, nt, 128]), op=ALU.add))
                    else:
                        S.op(dve, [d['psr'], rCONST, rC2, rBDP, rBS], [d['t7r']], lambda e: e.tensor_tensor(
                            out=d['t7'][:, 0:n], in0=d['ps'][:, 0:n], in1=BSS[:, l, cc, :], op=ALU.add))
                for d in P:
                    cc, bi, b0, n = d['cc'], d['bi'], d['b0'], d['n']
                    S.op(dve, [d['t6r'], d['t7r']], [RR[4 + cc][bi]], lambda e: e.tensor_tensor(
                        out=RA[:, 4 + cc, b0:b0 + n], in0=d['t6'][:, 0:n], in1=d['t7'][:, 0:n], op=ALU.mult))

            wavefront(pairsC, [PC1, PC2, PC3, PC4, PC5])
            release(suu)
            if g == 1:
                out_T([STA[:, 0, 0:30], STA[:, 1, 0:30]], [rSTA], 30, lambda: dout['ca_p'][l])
            else:
                out_T([STAS[:, 0, :], STAS[:, 1, :]], [rSTAS], 128,
                      lambda: dout['ca_s'][l].rearrange("(s j) c -> s j c", s=16)[:, 22:30, :])

            for cc in range(2):
                db, dbr = DBc[cc], DBRc[cc]
                for bi, (b0, n) in enumerate(blocks):
                    pp_, ppr = bank()
                    mm_group(dbr + [rCONST, rC2, rBDP, rBS], [ppr], [(pp_[:, 0:n], BDP[:, l, cc, :], db[:, b0:b0 + n], True, True)])
                    S.op(act, [ppr, rCONST, rC2, rBDP, rBS], [RR[2 + cc][bi]], lambda e: e.activation(
                        out=RA[:, 2 + cc, b0:b0 + n], in_=pp_[:, 0:n], func=AF.Identity,
                        scale=P256[:, cc, pbase + 34:pbase + 35]))


            wc, wcr, sc_ = getw('A')
            wdi, wdir, sdi = getw('A')
            wc3, wdi3 = a3(wc), a3(wdi)
            itemsD = [dict(cc=cc, bi=bi, b0=b0, n=n) for cc in range(2) for bi, (b0, n) in enumerate(blocks)]

            def D1(d):
                cc, bi, b0, n = d['cc'], d['bi'], d['b0'], d['n']
                pc, pcr = bank()
                d['pdn'], d['pdnr'] = bank()
                mm_group([wcr, wdir] + hr_all(bi), [pcr, d['pdnr']], z_mms(pc, wc3, cc, bi, b0, n) + z_mms(d['pdn'], wdi3, cc, bi, b0, n))
                d['t1'], d['t1r'] = tmp()
                S.op(act, [pcr], [d['t1r']], lambda e: e.activation(out=d['t1'][:, 0:n], in_=pc[:, 0:n], func=AF.Copy))

            def D2(d):
                n = d['n']
                d['t2'], d['t2r'] = tmp()
                S.op(dve, [d['pdnr'], d['t1r']], [d['t2r']], lambda e: e.tensor_tensor(
                    out=d['t2'][:, 0:n], in0=d['pdn'][:, 0:n], in1=d['t1'][:, 0:n], op=ALU.mult))

            def D3(d):
                cc, bi, b0, n = d['cc'], d['bi'], d['b0'], d['n']
                t2, t2r = d['t2'], d['t2r']
                if b0 < 1024:
                    S.op(act, [t2r], [rQ[cc][bi]], lambda e: e.activation(out=Q[:, cc, 2 + b0:2 + b0 + n], in_=t2[:, 0:n], func=AF.Copy))
                    if g == 1 and bi == last_p:
                        S.op(dve, [t2r], [rSTQ], lambda e: e.tensor_copy(out=STQ[:, cc, 0:2], in_=t2[:, n - 2:n]))
                    if g == 0 and bi == last_p:
                        S.op(dve, [rQ[cc][bi]], [rHALOQ], lambda e: e.tensor_copy(out=HALOQ[:, l, cc, :], in_=Q[:, cc, 1024:1026]))
                else:
                    S.op(act, [t2r], [rQ[cc][bi]], lambda e: e.activation(
                        out=QS[:, cc, :, 2:10], in_=t2[:, 0:128].rearrange("p (s t) -> p s t", s=16), func=AF.Copy))
                    S.op(dve, [t2r], [rSTQS], lambda e: e.tensor_copy(
                        out=STQS[:, cc, :].rearrange("p (s j) -> p s j", s=16),
                        in_=t2[:, 0:128].rearrange("p (s t) -> p s t", s=16)[:, :, 6:8]))

            wavefront(itemsD, [D1, D2, D3])
            release(sc_)
            release(sdi)
            wbg, wbgr, sbg = getw('A')
            wbg3 = a3(wbg)
            itemsD2 = [dict(cc=cc, bi=bi, b0=b0, n=n) for cc in range(2) for bi, (b0, n) in enumerate(blocks)]

            def E1(d):
                cc, bi, b0, n = d['cc'], d['bi'], d['b0'], d['n']
                if bi == 0:
                    for k in range(3):
                        S.op(dve, [rCONST], [rDQK[k]], lambda e, k=k: e.tensor_scalar(
                            out=DIAGQ[:, k, :], in0=IDENT[:], scalar1=P256[:, cc, pbase + 37 + k:pbase + 38 + k], scalar2=None,
                            op0=ALU.mult))
                pb_, pbr = bank()
                d['py'], d['pyr'] = bank()
                py = d['py']
                if b0 < 1024:
                    rd = rDQK + [rQ[cc][bi], rQh[cc]] + ([rQ[cc][bi - 1]] if bi > 0 else [])
                    cm = [(py[:, 0:n], DIAGQ[:, k, :], Q[:, cc, b0 + k:b0 + k + n], k == 0, k == 2) for k in range(3)]
                else:
                    rd = rDQK + [rQ[cc][bi], rQSh]
                    cm = [(py[:, 0:128].rearrange("p (s t) -> p s t", s=16), DIAGQ[:, k, :], QS[:, cc, :, k:k + 8], k == 0, k == 2)
                          for k in range(3)]
                mm_group([wbgr] + hr_all(bi) + rd, [pbr, d['pyr']], z_mms(pb_, wbg3, cc, bi, b0, n) + cm)
                d['t1'], d['t1r'] = tmp()
                S.op(act, [pbr], [d['t1r']], lambda e: e.activation(out=d['t1'][:, 0:n], in_=pb_[:, 0:n], func=AF.Copy))

            def E2(d):
                cc, bi, b0, n = d['cc'], d['bi'], d['b0'], d['n']
                S.op(dve, [d['pyr'], d['t1r']], [RR[6 + cc][bi]], lambda e: e.tensor_tensor(
                    out=RA[:, 6 + cc, b0:b0 + n], in0=d['py'][:, 0:n], in1=d['t1'][:, 0:n], op=ALU.mult))

            wavefront(itemsD2, [E1, E2])
            release(sbg)
            if g == 1:
                out_T([STQ[:, 0, 0:2], STQ[:, 1, 0:2]], [rSTQ], 2, lambda: dout['sc_p'][l])
            else:
                out_T([STQS[:, 0, :], STQS[:, 1, :]], [rSTQS], 32, lambda: dout['sc_s'][l])

            for mp in range(4):
                wo, wor, so = getw('A')
                wo3 = a3(wo)
                hook = norm_hook(g, (l * 4 + 2, False))
                for mm, bi in m_order(mp == 3, len(blocks)):
                    m = mp * 2 + mm
                    for (b0, n) in [blocks[bi]]:
                        po, por = bank()
                        mm_group([wor] + [RR[kc][bi] for kc in range(KC)], [por],
                                 [(po[:, 0:n], wo3[:, kc, mm * 128:(mm + 1) * 128], RA[:, kc, b0:b0 + n], kc == 0, kc == KC - 1)
                                  for kc in range(KC)])
                        S.op(dve, [por, XR[m][bi]], [XR[m][bi]], lambda e: e.tensor_tensor(
                            out=X[:, m, b0:b0 + n], in0=po[:, 0:n], in1=X[:, m, b0:b0 + n], op=ALU.add))
                    if mp == 3 and mm == 1 and bi > 0:
                        hook(bi - 1)
                if mp == 3:
                    hook(len(blocks) - 1)
                release(so)

        RAall = [RR[j][b_] for j in range(8) for b_ in range(NB)]
        xstage = [(P1[:, 0:1024], [rP1], S.new_prod("xst0")),
                  (P2[:, 0:1024], [rP2], S.new_prod("xst1")),
                  (RBf[:, 0:1024], RBall, S.new_prod("xst2")),
                  (VTf[:, 0:1024], [rVT] + rVTc[0] + rVTc[1], S.new_prod("xst3")),
                  (DIAGf[:, 0:1024], [rDIAG] + rDK, S.new_prod("xst4")),
                  (DIAG2f[:, 0:1024], rDK2, S.new_prod("xst5"))]
        xseq = [0, 1, 2, 3, 4, 5, 0, 1, 2]

        def x_src(g, tt):
            p0 = g * 1024
            if tt < 8:
                return din['xp'][p0 + tt * 128:p0 + (tt + 1) * 128, :]
            return din['xs'][:, :]

        def x_dma(g, tt, after=None):
            ap_, res_, sem_ = xstage[xseq[tt]]
            S.dma(sp, sem_, [(after or [], res_, lambda e: e.dma_start(out=ap_, in_=x_src(g, tt)))])

        def x_load(g, pre_done):
            ntile = 9 if g == 0 else 8
            if not pre_done:
                for tt in range(4):
                    x_dma(g, tt)
                pump()
                for tt in range(4, 6):
                    x_dma(g, tt, after=list(xstage[0][1]))
            for tt in range(ntile):
                ap_, res_, sem_ = xstage[xseq[tt]]
                for q in range(2):
                    pb, pr = bank()
                    S.op(pe, res_ + [rCONST], [pr], lambda e: [
                        e.transpose(out=pb[:, j * 128:(j + 1) * 128], in_=ap_[:, (q * 4 + j) * 128:(q * 4 + j + 1) * 128],
                                    identity=IDENT[:]) for j in range(4)][-1])
                    dres = [XR[c][min(tt // 4, 2)] for c in range(q * 4, q * 4 + 4)]
                    dst = X[:, q * 4:q * 4 + 4, tt * 128:(tt + 1) * 128]
                    if (tt + q) % 2 == 0:
                        S.op(act, [pr], dres, lambda e: e.activation(
                            out=dst, in_=pb[:, 0:512].rearrange("p (c t) -> p c t", c=4), func=AF.Copy))
                    else:
                        S.op(dve, [pr], dres, lambda e: e.tensor_copy(
                            out=dst, in_=pb[:, 0:512].rearrange("p (c t) -> p c t", c=4)))
                if tt + 6 < ntile:
                    x_dma(g, tt + 6)

        pt_st = {}

        def pt_dma(g, l):
            p0 = g * 1024
            for k in range(2):
                S.dma(sp, sem_xs[k], [([], [XSR[k]], lambda e: e.dma_start(
                    out=XS[k][:, :].rearrange("p (t c) -> p t c", t=4),
                    in_=din['pp'][l, p0 + k * 512:p0 + (k + 1) * 512, :].rearrange("(t p) c -> p t c", p=128)))])
            if g == 0:
                k = state['ost']
                state['ost'] = (k + 1) % 3
                pt_st['k'] = k
                S.dma(sp, sem_ost[k], [([], [OSTR[k]], lambda e: e.dma_start(out=OST[k][:, :], in_=din['psm'][l]))])

        def pt_transposes(g, l):
            ntile = 9 if g == 0 else 8
            for tt in range(ntile):
                if tt < 8:
                    src, sres = XS[tt // 4][:, (tt % 4) * 256:(tt % 4 + 1) * 256], [XSR[tt // 4]]
                else:
                    src, sres = OST[pt_st['k']][:, :], [OSTR[pt_st['k']]]
                pb, pr = bank()
                S.op(pe, sres + [rCONST], [pr], lambda e: [
                    e.transpose(out=pb[:, j * 128:(j + 1) * 128], in_=src[:, j * 128:(j + 1) * 128], identity=IDENT[:])
                    for j in range(2)][-1])
                if tt % 2 == 0:
                    S.op(act, [pr], [rPT], lambda e: e.activation(
                        out=PT[:, 0:2, tt * 128:(tt + 1) * 128], in_=pb[:, 0:256].rearrange("p (c t) -> p c t", c=2), func=AF.Copy))
                else:
                    S.op(dve, [pr], [rPT], lambda e: e.tensor_copy(
                        out=PT[:, 0:2, tt * 128:(tt + 1) * 128], in_=pb[:, 0:256].rearrange("p (c t) -> p c t", c=2)))

        def halo_dma(l):
            items = []
            for i in range(4):
                items.append(([], [rDIAG] + rDK, lambda e, i=i: e.dma_start(out=DIAGf[0:120, i * 256:(i + 1) * 256], in_=din['sta'][l, i * 120:(i + 1) * 120, :])))
            items.append(([], [rDIAG] + rDK, lambda e: e.dma_start(out=DIAGf[0:32, 1024:1280], in_=din['sts'][l])))
            for i in range(2):
                items.append(([], [rVT] + rVTc[0] + rVTc[1], lambda e, i=i: e.dma_start(out=VTf[0:120, i * 256:(i + 1) * 256], in_=din['stp'][l, i * 120:(i + 1) * 120, :])))
            S.dma(sp, sem_misc2, items)

        def halo_T(src, sres, nrows, evac):
            pb, pr = bank()
            S.op(pe, sres + [rCONST], [pr], lambda e: [
                e.transpose(out=pb[:, cc * 128:cc * 128 + nrows], in_=src[0:nrows, cc * 128:(cc + 1) * 128],
                            identity=IDENT[0:nrows, 0:nrows]) for cc in range(2)][-1])
            for cc in range(2):
                reads, writes, fn = evac(cc, pb[:, cc * 128:cc * 128 + nrows])
                S.op(act, [pr] + reads, writes, fn)

        def ple(g, l, nxt=None):
            blocks = blocks_of(g)
            rmsnorm(g, l * 4 + 3)
            wp, wpr, spp = getw('A')
            wp3 = wp[:, 0:2048].rearrange("p (k n) -> p k n", k=2)
            for mp in range(4):
                wgt, wgtr, sgt = getw('A')
                wgt3 = a3(wgt)
                hook = norm_hook(g, nxt)
                for mm, bi in m_order(mp in (0, 3), len(blocks)):
                    m = mp * 2 + mm
                    for (b0, n) in [blocks[bi]]:
                        pg, pgr = bank()
                        pq, pqr = bank()
                        mm_group([wgtr, wpr, rPT] + hr_all(bi), [pgr, pqr],
                                 [(pg[:, 0:n], wgt3[:, kc, mm * 128:(mm + 1) * 128], H[:, kc, b0:b0 + n], kc == 0, kc == KC - 1) for kc in range(KC)] +
                                 [(pq[:, 0:n], wp3[:, k2, m * 128:(m + 1) * 128], PT[:, k2, b0:b0 + n], k2 == 0, k2 == 1) for k2 in range(2)])
                        t1, t1r = tmp()
                        S.op(act, [pgr], [t1r], lambda e: e.activation(out=t1[:, 0:n], in_=pg[:, 0:n], func=AF.Sigmoid))
                        t2, t2r = tmp()
                        S.op(dve, [pqr, t1r], [t2r], lambda e: e.tensor_tensor(out=t2[:, 0:n], in0=pq[:, 0:n], in1=t1[:, 0:n], op=ALU.mult))
                        S.op(dve, [t2r, XR[m][bi]], [XR[m][bi]], lambda e: e.tensor_tensor(
                            out=X[:, m, b0:b0 + n], in0=t2[:, 0:n], in1=X[:, m, b0:b0 + n], op=ALU.add))
                    if mp == 3 and mm == 1 and bi > 0:
                        hook(bi - 1)
                if mp == 3:
                    hook(len(blocks) - 1)
                release(sgt)
            release(spp)

        def final_store(g):
            p0 = g * 1024

            def store_tiles(bi, b0, n):
                for ti in range(n // 128):
                    tt = b0 // 128 + ti
                    k = state['xs']
                    state['xs'] = (k + 1) % 2
                    for q in range(2):
                        pb, pr = bank()
                        S.op(pe, YFBs[bi % 2][1] + [rCONST], [pr], lambda e: [
                            e.transpose(out=pb[:, j * 128:(j + 1) * 128], in_=YFBs[bi % 2][0][:, q * 4 + j, ti * 128:(ti + 1) * 128],
                                        identity=IDENT[:]) for j in range(4)][-1])
                        if q == 0:
                            S.op(act, [pr], [XSR[k]], lambda e: e.activation(out=XS[k][:, 0:512], in_=pb[:, 0:512], func=AF.Copy))
                        else:
                            S.op(dve, [pr], [XSR[k]], lambda e: e.tensor_copy(out=XS[k][:, 512:1024], in_=pb[:, 0:512]))
                    if tt < 8:
                        dst = dout['y_p'][p0 + tt * 128:p0 + (tt + 1) * 128, :]
                    else:
                        dst = dout['y_s'][:, :]
                    S.dma(sp, sem_xs[k], [([XSR[k]], [], lambda e: e.dma_start(out=dst, in_=XS[k][:, :]))], is_output=True)
            rmsnorm(g, 8, final=True, tile_cb=store_tiles)

        for g in range(2):
            x_load(g, pre_done=(g == 1))
            if g == 0:
                setup_dmas()
            for l in range(L):
                pt_dma(g, l)
                if g == 0:
                    halo_dma(l)
                def mid(g=g, l=l):
                    if g == 0 and l == 0:
                        late_setup()
                        late_setup_sgu()
                    pt_transposes(g, l)
                    mixer_prep(g, l)
                ffn(g, l, 1, mid_cb=mid, nxt=(l * 4 + 1, False))
                mixers(g, l)
                ffn(g, l, 2, nxt=(l * 4 + 3, False))
                if g == 0 and l == L - 1:
                    for tt in range(6):
                        x_dma(1, tt)
                ple(g, l, nxt=((l + 1) * 4, False) if l < L - 1 else (8, True))
            final_store(g)

        assert wst['next_use'] == len(stream), (wst['next_use'], len(stream))
        done = set()
        for sem in S.out_marks:
            if id(sem) in done:
                continue
            done.add(id(sem))
            sp.eng.wait_ge(sem.h, sem.count)
    return nc


_CACHE = {}


def kernel(**inputs):
    f32 = lambda a: np.ascontiguousarray(np.asarray(a, dtype=np.float32))
    x_prompt = f32(inputs['x_prompt'])
    x_sample = f32(inputs['x_sample'])
    p_prompt = f32(inputs['p_prompt'])
    p_sample = f32(inputs['p_sample'])
    st_a = f32(inputs['state_conv_a'])
    st_p = f32(inputs['state_pool'])
    st_s = f32(inputs['state_short_conv'])
    weights = {n: f32(inputs[n]) for n in WEIGHT_NAMES}

    if 'nc' not in _CACHE:
        _CACHE['nc'] = build_program()
    nc = _CACHE['nc']
    in_maps = []
    for c in range(NCORES):
        sl = slice(c * 16, (c + 1) * 16)
        m = {
            'xp': x_prompt[c],
            'xs': x_sample[sl].reshape(128, 1024),
            'pp': np.ascontiguousarray(p_prompt[:, c]),
            'psm': np.ascontiguousarray(p_sample[:, sl].reshape(2, 128, 256)),
            'sta': np.ascontiguousarray(st_a[:, sl].reshape(2, 480, 256)),
            'stp': np.ascontiguousarray(st_p[:, sl].reshape(2, 240, 256)),
            'sts': np.ascontiguousarray(st_s[:, sl].reshape(2, 32, 256)),
        }
        m.update(weights)
        in_maps.append(m)
    res = run_bass_kernel_spmd(nc, in_maps, core_ids=list(range(NCORES)))
    R = res.results
    y_prompt = np.stack([R[c]['y_p'] for c in range(NCORES)], axis=0)
    y_sample = np.concatenate([R[c]['y_s'].reshape(16, 8, 1024) for c in range(NCORES)], axis=0)
    ca_p = np.stack([R[c]['ca_p'] for c in range(NCORES)], axis=1)
    ca_s = np.concatenate([R[c]['ca_s'].reshape(2, 16, 30, 256) for c in range(NCORES)], axis=1)
    po_p = np.stack([R[c]['po_p'] for c in range(NCORES)], axis=1)
    po_s = np.concatenate([R[c]['po_s'].reshape(2, 16, 15, 256) for c in range(NCORES)], axis=1)
    sc_p = np.stack([R[c]['sc_p'] for c in range(NCORES)], axis=1)
    sc_s = np.concatenate([R[c]['sc_s'].reshape(2, 16, 2, 256) for c in range(NCORES)], axis=1)
    cv_s = np.concatenate([R[c]['cv_s'].reshape(2, 16, 8, 256) for c in range(NCORES)], axis=1)
    outs = (y_prompt, y_sample, ca_p, ca_s, po_p, po_s, sc_p, sc_s, cv_s)
    return tuple(np.ascontiguousarray(o, dtype=np.float32) for o in outs)
```

```python
import numpy as np
from contextlib import ExitStack
import concourse.bass as bass
import concourse.mybir as mybir
from concourse.bass_utils import run_bass_kernel_spmd

F32 = mybir.dt.float32
BF16 = mybir.dt.bfloat16
AF = mybir.ActivationFunctionType
ALU = mybir.AluOpType

NCORES = 8
D = 1024
KC = 8
FF = 2816
FC = 22
HALF = 11
TMAX = 1152
EPS = 1e-6
L = 2

WEIGHT_NAMES = ['norm_ffn1', 'w_ffn1_gate', 'w_ffn1_up', 'w_ffn1_down', 'norm_mix', 'w_mix_in',
                'conv_a_w', 'conv_a_b', 'norm_a_g', 'norm_a_b', 'pool_w', 'pool_scale',
                'sgu_norm_g', 'sgu_norm_b', 'sgu_w', 'sgu_b', 'short_conv_w', 'w_mix_out',
                'norm_ffn2', 'w_ffn2_gate', 'w_ffn2_up', 'w_ffn2_down',
                'norm_ple', 'w_ple_gate', 'w_ple_proj', 'norm_final']
WEIGHT_SHAPES = {
    'norm_ffn1': [2, 1024], 'w_ffn1_gate': [2, 1024, 2816], 'w_ffn1_up': [2, 1024, 2816],
    'w_ffn1_down': [2, 2816, 1024], 'norm_mix': [2, 1024], 'w_mix_in': [2, 1024, 2048],
    'conv_a_w': [2, 31, 256], 'conv_a_b': [2, 256], 'norm_a_g': [2, 256], 'norm_a_b': [2, 256],
    'pool_w': [2, 4, 64, 64], 'pool_scale': [2, 256], 'sgu_norm_g': [2, 256], 'sgu_norm_b': [2, 256],
    'sgu_w': [2, 4, 128, 128], 'sgu_b': [2, 4, 128], 'short_conv_w': [2, 3, 256],
    'w_mix_out': [2, 1024, 1024], 'norm_ffn2': [2, 1024], 'w_ffn2_gate': [2, 1024, 2816],
    'w_ffn2_up': [2, 1024, 2816], 'w_ffn2_down': [2, 2816, 1024], 'norm_ple': [2, 1024],
    'w_ple_gate': [2, 1024, 1024], 'w_ple_proj': [2, 256, 1024], 'norm_final': [1024],
}
CORE_IN_SHAPES = {
    'xp': [2048, 1024], 'xs': [128, 1024], 'pp': [2, 2048, 256], 'psm': [2, 128, 256],
    'sta': [2, 480, 256], 'stp': [2, 240, 256], 'sts': [2, 32, 256],
}
CORE_OUT_SHAPES = {
    'y_p': [2048, 1024], 'y_s': [128, 1024], 'ca_p': [2, 30, 256], 'ca_s': [2, 480, 256],
    'po_p': [2, 15, 256], 'po_s': [2, 240, 256], 'sc_p': [2, 2, 256], 'sc_s': [2, 32, 256],
    'cv_s': [2, 128, 256],
}


class Prod:
    def __init__(self, h, name):
        self.h = h
        self.count = 0
        self.name = name


class EngW:
    def __init__(self, name, eng, prod):
        self.name = name
        self.eng = eng
        self.prod = prod
        self.waited = {}


class Res:
    __slots__ = ('name', 'w', 'r', 'gen')

    def __init__(self, name):
        self.name = name
        self.w = None
        self.r = {}


class GenRes:
    __slots__ = ('base', 'gen')

    def __init__(self, base):
        self.base = base
        base.gen = getattr(base, 'gen', 0) + 1
        self.gen = base.gen

    def _chk(self):
        assert self.gen == self.base.gen, f"stale ring buffer use: {self.base.name}"
        return self.base

    @property
    def w(self):
        return self._chk().w

    @w.setter
    def w(self, v):
        self._chk().w = v

    @property
    def r(self):
        return self._chk().r

    @r.setter
    def r(self, v):
        self._chk().r = v


class Sched:
    def __init__(self, nc, es):
        self.nc = nc
        self.es = es
        self.nsem = 0
        mk = lambda n, e: EngW(n, e, self.new_prod(n))
        self.pe = mk('pe', nc.tensor)
        self.act = mk('act', nc.scalar)
        self.dve = mk('dve', nc.vector)
        self.pool = mk('pool', nc.gpsimd)
        self.sp = mk('sp', nc.sync)
        self.out_marks = []

    def new_prod(self, name):
        h = self.es.enter_context(self.nc.semaphore(name + str(self.nsem)))
        self.nsem += 1
        return Prod(h, name)

    def _deps(self, ew, reads, writes):
        deps = {}

        def add(p, c):
            if deps.get(p, 0) < c:
                deps[p] = c
        for r in reads:
            if r.w is not None:
                add(*r.w)
        for w in writes:
            if w.w is not None:
                add(*w.w)
            for p, c in w.r.items():
                if p is ew.prod and ew.name == 'pe':
                    continue
                add(p, c)
        for p, c in deps.items():
            if p is ew.prod and ew.name == 'pe':
                continue
            if ew.waited.get(p, 0) < c:
                ew.eng.wait_ge(p.h, c)
                ew.waited[p] = c
                self.nwaits = getattr(self, 'nwaits', 0) + 1
                if p is ew.prod:
                    self.nself = getattr(self, 'nself', 0) + 1

    def _mark(self, mark, reads, writes):
        p, c = mark
        for r in reads:
            if r.r.get(p, 0) < c:
                r.r[p] = c
        for w in writes:
            w.w = mark
            w.r = {}

    def op(self, ew, reads, writes, fn):
        self._deps(ew, reads, writes)
        inst = fn(ew.eng)
        ew.prod.count += 1
        inst.then_inc(ew.prod.h, 1)
        self._mark((ew.prod, ew.prod.count), reads, writes)

    def dma(self, qew, sem, items, is_output=False):
        allr, allw = [], []
        for reads, writes, fn in items:
            allr += reads
            allw += writes
        self._deps(qew, allr, allw)
        if sem.count and qew.waited.get(sem, 0) < sem.count:
            qew.eng.wait_ge(sem.h, sem.count)
            qew.waited[sem] = sem.count
        for reads, writes, fn in items:
            inst = fn(qew.eng)
            sem.count += 16
            inst.then_inc(sem.h, 16)
        self._mark((sem, sem.count), allr, allw)
        if is_output:
            self.out_marks.append(sem)


def build_program():
    nc = bass.Bass("TRN2", target_bir_lowering=False)
    din = {}
    for n, s in CORE_IN_SHAPES.items():
        din[n] = nc.dram_tensor(n, s, F32, kind="ExternalInput").ap()
    for n in WEIGHT_NAMES:
        din[n] = nc.dram_tensor(n, WEIGHT_SHAPES[n], F32, kind="ExternalInput").ap()
    dout = {}
    for n, s in CORE_OUT_SHAPES.items():
        dout[n] = nc.dram_tensor(n, s, F32, kind="ExternalOutput").ap()

    with ExitStack() as es:
        S = Sched(nc, es)
        pe, act, dve, pool, sp = S.pe, S.act, S.dve, S.pool, S.sp

        def sb(name, shape, dt):
            return es.enter_context(nc.sbuf_tensor(name, shape, dt))

        X = sb("X", [128, KC, TMAX], F32)
        Hf = sb("Hf", [128, KC * TMAX // 2], F32)
        H = Hf[:].bitcast(BF16).rearrange("p (c t) -> p c t", c=KC)
        YFB = Hf[:, 0:KC * 512].rearrange("p (c t) -> p c t", c=KC)
        RAf = sb("RAf", [128, 8 * TMAX // 2], F32)
        RA = RAf[:].bitcast(BF16).rearrange("p (s t) -> p s t", s=8)
        RBf = sb("RBf", [128, 3 * TMAX // 2], F32)
        RB = RBf[:].bitcast(BF16).rearrange("p (s t) -> p s t", s=3)
        ZW = 15 + 1024
        ZB = RBf[:, 0:ZW]
        ZBS = RBf[:, ZW:ZW + 16 * 23].rearrange("p (s t) -> p s t", s=16)
        P1 = sb("P1", [128, ZW + 16 * 23], F32)
        P2 = sb("P2", [128, ZW + 16 * 23], F32)
        HA = sb("HA", [128, 2, 30 + 1024], BF16)
        HAS = sb("HAS", [128, 2, 16, 38], BF16)
        Q = sb("Q", [128, 2, 2 + 1024], BF16)
        QS = sb("QS", [128, 2, 16, 10], BF16)
        DB = sb("DB", [128, TMAX], BF16)
        VTf = sb("VTf", [128, 9 * 128], F32)
        VT = VTf[:].bitcast(BF16).rearrange("p (a c) -> p a c", a=9)
        PT = sb("PT", [128, 2, TMAX], BF16)
        WA = [sb(f"WA{i}", [128, 2048], BF16) for i in range(4)]
        WD = [sb(f"WD{i}", [128, HALF * 256], BF16) for i in range(2)]
        XS = [sb(f"XS{i}", [128, 1024], F32) for i in range(2)]
        OST = [sb(f"OST{i}", [128, 256], F32) for i in range(3)]
        NTMP = 8
        TMP = [sb(f"TMP{i}", [128, 512], F32) for i in range(NTMP)]
        SQf = sb("SQf", [128, KC * 256], F32)
        SQ = SQf[:].bitcast(BF16).rearrange("p (c t) -> p c t", c=KC)
        TMPX = TMP + [SQf[:, i * 512:(i + 1) * 512] for i in range(4)]
        DIAGf = sb("DIAGf", [128, 31 * 64], F32)
        DIAG = DIAGf[:].bitcast(BF16).rearrange("p (k c) -> p k c", k=31)
        DIAG2f = sb("DIAG2f", [128, 31 * 64], F32)
        DIAG2 = DIAG2f[:].bitcast(BF16).rearrange("p (k c) -> p k c", k=31)
        DIAGc = [DIAG, DIAG2]
        DIAGQ = sb("DIAGQ", [128, 3, 128], BF16)
        IDENT = sb("IDENT", [128, 128], F32)
        ONES = sb("ONES", [128, 128], BF16)
        BD = sb("BD", [128, 128], F32)
        IMBD = sb("IMBD", [128, 128], F32)
        TRIL = sb("TRIL", [128, 128], F32)
        MASKS = sb("MASKS", [128, 128], F32)
        EPSV = sb("EPSV", [128, 1], F32)
        DUMV = sb("DUMV", [128, 1], F32)
        G1024 = sb("G1024", [128, KC, 16], F32)
        P256 = sb("P256", [128, 2, 80], F32)
        WT = sb("WT", [128, L, 4, 128], BF16)
        WTS = sb("WTS", [128, L, 4, 128], BF16)
        BSP = sb("BSP", [128, L, 2, 128], F32)
        BSS = sb("BSS", [128, L, 2, 128], F32)
        BDP = sb("BDP", [128, L, 2, 128], BF16)
        CORR = sb("CORR", [128, 2, 16], F32)
        RS8 = sb("RS8", [128, L, 4, 8], F32)
        HALOA = sb("HALOA", [128, L, 2, 30], BF16)
        HALOB = sb("HALOB", [128, L, 2, 15], F32)
        HALOQ = sb("HALOQ", [128, L, 2, 2], BF16)
        STA = sb("STA", [128, 2, 32], F32)
        STAS = sb("STAS", [128, 2, 128], F32)
        STB = sb("STB", [128, 2, 16], F32)
        STBS = sb("STBS", [128, 2, 128], F32)
        STQ = sb("STQ", [128, 2, 2], F32)
        STQS = sb("STQS", [128, 2, 32], F32)
        CORT = sb("CORT", [128, 16], F32)
        PS = es.enter_context(nc.psum_tensor("PS", [128, 8, 512], F32))

        NB = 3
        XR = [[Res(f"X{c}_{b}") for b in range(NB)] for c in range(KC)]
        HR = [[Res(f"H{c}_{b}") for b in range(NB)] for c in range(KC)]
        RR = [[Res(f"R{j}_{b}") for b in range(NB)] for j in range(HALF)]
        RBall = [RR[j][b] for j in (8, 9, 10) for b in range(NB)]
        PSR = [Res(f"PS{i}") for i in range(8)]
        YFB2 = RAf[:, 0:KC * 512].rearrange("p (c t) -> p c t", c=KC)
        YFBs = [(YFB, [HR[c][b_] for c in range(KC) for b_ in range(NB)]),
                (YFB2, [RR[j][b_] for j in range(8) for b_ in range(NB)])]
        TMPR = [Res(f"TMP{i}") for i in range(NTMP + 4)]
        SQR = TMPR[NTMP:NTMP + 4]
        WAR_ = [Res(f"WA{i}") for i in range(4)]
        WDR = [Res(f"WD{i}") for i in range(2)]
        XSR = [Res(f"XS{i}") for i in range(2)]
        OSTR = [Res(f"OST{i}") for i in range(3)]
        rSQ, rDIAG, rDIAGQ = Res("SQ"), Res("DIAG"), Res("DIAGQ")
        rDK = [Res(f"DK{k}") for k in range(31)]
        rDK2 = [Res(f"DK2_{k}") for k in range(31)]
        rDKc = [rDK, rDK2]
        rDQK = [Res(f"DQK{k}") for k in range(3)]
        rCONST = Res("CONST")
        rC2 = Res("CONST2")
        rBDP, rBS = Res("BDP"), Res("BS")
        rDUM = Res("DUM")
        rP1, rP2, rDB, rVT, rPT = Res("P1"), Res("P2"), Res("DB"), Res("VT"), Res("PT")
        rHA = [[Res(f"HA{c}_{b}") for b in range(NB)] for c in range(2)]
        rHAh = [Res("HAh0"), Res("HAh1")]
        rHASh = Res("HASh")
        rQ = [[Res(f"Q{c}_{b}") for b in range(NB)] for c in range(2)]
        rQh = [Res("Qh0"), Res("Qh1")]
        rQSh = Res("QSh")
        rZBh, rZBSh = Res("ZBh"), Res("ZBSh")
        rVTc = [[Res(f"VT{c}_{b}") for b in range(NB)] for c in range(2)]
        rHALOA, rHALOB, rHALOQ = Res("HALOA"), Res("HALOB"), Res("HALOQ")
        rSTA, rSTAS, rSTB, rSTBS, rSTQ, rSTQS = (Res("STA"), Res("STAS"), Res("STB"), Res("STBS"),
                                                 Res("STQ"), Res("STQS"))
        rCORT = Res("CORT")

        state = {'bank': 0, 'tmp': 0, 'ost': 0, 'xs': 0}

        pinned = set()

        def bank(pin=False):
            i = state['bank']
            while i in pinned:
                i = (i + 1) % 8
            state['bank'] = (i + 1) % 8
            if pin:
                pinned.add(i)
            return PS[:, i, :], GenRes(PSR[i])

        def unpin(ps_res):
            pinned.discard(PSR.index(ps_res.base))

        tpinned = set()

        def tmp(pin=False, ext=False):
            nring = NTMP + 4 if ext else NTMP
            i = state['tmp'] % nring
            while i in tpinned:
                i = (i + 1) % nring
            state['tmp'] = (i + 1) % nring
            if pin:
                tpinned.add(i)
            return TMPX[i], GenRes(TMPR[i])

        def unpin_t(t_res):
            tpinned.discard(TMPR.index(t_res.base))

        sem_w = {('A', i): S.new_prod(f"wA{i}") for i in range(4)}
        sem_w.update({('D', i): S.new_prod(f"wD{i}") for i in range(2)})
        sem_xs = [S.new_prod(f"xs{i}") for i in range(2)]
        sem_ost = [S.new_prod(f"ost{i}") for i in range(3)]
        sem_misc = S.new_prod("misc")
        sem_d2d = S.new_prod("d2d")
        sem_setup = S.new_prod("setup")
        sem_setup2 = S.new_prod("setup2")
        sem_misc2 = S.new_prod("misc2")

        def pool_op(reads, writes, fn):
            S.op(pool, reads, writes, fn)

        def dve_op(reads, writes, fn):
            S.op(dve, reads, writes, fn)

        pool_op([], [rCONST], lambda e: e.memset(IDENT[:], 0.0))
        pool_op([rCONST], [rCONST], lambda e: e.affine_select(
            out=IDENT[:], in_=IDENT[:], pattern=[[-1, 128]], compare_op=ALU.not_equal, fill=1.0,
            base=0, channel_multiplier=1))
        pool_op([], [rCONST], lambda e: e.memset(ONES[:], 1.0 / 1024.0))
        pool_op([], [rCONST], lambda e: e.memset(EPSV[:], EPS))

        items = []
        PSTG2 = OST[0]
        for l in range(L):
            for i, n in enumerate(['norm_ffn1', 'norm_mix', 'norm_ffn2', 'norm_ple']):
                r = l * 4 + i
                for hf in range(2):
                    items.append(([], [TMPR[hf]], lambda e, n=n, l=l, r=r, hf=hf: e.dma_start(
                        out=TMP[hf][r:r + 1, :], in_=din[n][l:l + 1, hf * 512:(hf + 1) * 512])))
        for hf in range(2):
            items.append(([], [TMPR[hf]], lambda e, hf=hf: e.dma_start(
                out=TMP[hf][8:9, :], in_=din['norm_final'].rearrange("(a n) -> a n", a=1)[:, hf * 512:(hf + 1) * 512])))
        for l in range(L):
            b = l * 40
            items.append(([], [OSTR[0]], lambda e, l=l, b=b: e.dma_start(out=PSTG2[b:b + 31, 0:256], in_=din['conv_a_w'][l])))
            for i, n in enumerate(['conv_a_b', 'norm_a_g', 'norm_a_b', 'pool_scale', 'sgu_norm_g', 'sgu_norm_b']):
                items.append(([], [OSTR[0]], lambda e, n=n, l=l, r=b + 31 + i: e.dma_start(
                    out=PSTG2[r:r + 1, 0:256], in_=din[n][l:l + 1, :])))
            items.append(([], [OSTR[0]], lambda e, l=l, b=b: e.dma_start(out=PSTG2[b + 37:b + 40, 0:256], in_=din['short_conv_w'][l])))
        S.dma(act, sem_setup, items)
        for half in range(2):
            pb, pr = bank()
            S.op(pe, [TMPR[half], rCONST], [pr], lambda e, half=half, pb=pb: [
                e.transpose(out=pb[:, j * 16:j * 16 + 9], in_=TMP[half][0:9, j * 128:(j + 1) * 128],
                            identity=IDENT[0:9, 0:9]) for j in range(4)][-1])
            S.op(dve, [pr], [rCONST], lambda e, half=half, pb=pb: e.tensor_copy(
                out=G1024[:, half * 4:half * 4 + 4, 0:9], in_=pb[:, 0:64].rearrange("p (c r) -> p c r", c=4)[:, :, 0:9]))
        pb, pr = bank()
        S.op(pe, [OSTR[0], rCONST], [pr], lambda e, pb=pb: [
            e.transpose(out=pb[:, cc * 80:cc * 80 + 80], in_=PSTG2[0:80, cc * 128:(cc + 1) * 128],
                        identity=IDENT[0:80, 0:80]) for cc in range(2)][-1])
        S.op(dve, [pr], [rCONST], lambda e, pb=pb: e.tensor_copy(
            out=P256[:], in_=pb[:, 0:160].rearrange("p (c r) -> p c r", c=2)))

        def late_setup():
            dve_op([], [rC2], lambda e: e.memset(BD[:], 1.0 / 64.0))
            dve_op([rCONST, rC2, rBDP, rBS], [rC2], lambda e: e.memset(BD[0:64, 64:128], 0.0))
            dve_op([rCONST, rC2, rBDP, rBS], [rC2], lambda e: e.memset(BD[64:128, 0:64], 0.0))
            dve_op([rCONST, rC2, rBDP, rBS], [rC2], lambda e: e.tensor_tensor(out=IMBD[:], in0=IDENT[:], in1=BD[:], op=ALU.subtract))
            dve_op([], [rC2], lambda e: e.memset(TRIL[:], 1.0))
            dve_op([], [rC2], lambda e: e.memset(MASKS[:], 1.0))
            pool_op([rCONST, rC2, rBDP, rBS], [rC2], lambda e: e.affine_select(
                out=TRIL[:], in_=TRIL[:], pattern=[[1, 128]], compare_op=ALU.is_ge, fill=0.0,
                base=0, channel_multiplier=-1))
            MS3 = MASKS[:].rearrange("p (a b) -> p a b", a=16)
            pool_op([rCONST, rC2, rBDP, rBS], [rC2], lambda e: e.affine_select(
                out=MS3, in_=MS3, pattern=[[8, 16], [1, 8]], compare_op=ALU.is_ge, fill=0.0,
                base=0, channel_multiplier=-1))
            pool_op([rCONST, rC2, rBDP, rBS], [rC2], lambda e: e.affine_select(
                out=MS3, in_=MS3, pattern=[[-8, 16], [0, 8]], compare_op=ALU.is_ge, fill=0.0,
                base=0, channel_multiplier=1))
            dve_op([], [rHAh[0], rHAh[1]], lambda e: e.memset(HA[:, :, 0:30], 0.0))
            dve_op([], [rQh[0], rQh[1]], lambda e: e.memset(Q[:, :, 0:2], 0.0))
            wins = {(0, 0): 2, (0, 1): 4, (1, 0): 8, (1, 1): 16}
            for (cc, hf), win in wins.items():
                dve_op([rCONST, rC2, rBDP, rBS], [rC2], lambda e: e.memset(CORR[hf * 64:(hf + 1) * 64, cc, :], 1.0 / win))
                for t in range(win - 1):
                    dve_op([rCONST, rC2, rBDP, rBS], [rC2], lambda e: e.memset(CORR[hf * 64:(hf + 1) * 64, cc, t:t + 1], 1.0 / (t + 1)))

        def setup_dmas():
            dve_op([], [rBDP], lambda e: e.memset(BDP[:], 0.0))
            items = []
            for l in range(L):
                for h in range(4):
                    j = l * 4 + h
                    cc, hh = h // 2, h % 2
                    items.append(([], [rP1], lambda e, l=l, h=h, j=j: e.dma_start(out=P1[:, j * 128:(j + 1) * 128], in_=din['sgu_w'][l, h])))
                    items.append(([], [rP1], lambda e, l=l, h=h, j=j: e.dma_start(
                        out=P1[:, 1024 + j * 8:1024 + j * 8 + 8], in_=din['sgu_w'][l, h, 0:8, 0:8].partition_broadcast(16))))
                    items.append(([], [rBS], lambda e, l=l, h=h, cc=cc, hh=hh: e.dma_start(
                        out=BSP[hh * 64:(hh + 1) * 64, l, cc, :], in_=din['sgu_b'][l, h:h + 1, :].broadcast_to([64, 128]))))
                    items.append(([], [rBS], lambda e, l=l, h=h, cc=cc, hh=hh: e.dma_start(
                        out=BSS[hh * 64:(hh + 1) * 64, l, cc, :].rearrange("p (a t) -> p a t", a=16),
                        in_=din['sgu_b'][l, h, 0:8].partition_broadcast(16).partition_broadcast(64))))
            S.dma(sp, sem_setup2, items)
            items = []
            for l in range(L):
                for g_ in range(4):
                    cc, hh = g_ // 2, g_ % 2
                    items.append(([], [rBDP], lambda e, l=l, g_=g_, cc=cc, hh=hh: e.dma_start(
                        out=BDP[hh * 64:(hh + 1) * 64, l, cc, hh * 64:(hh + 1) * 64], in_=din['pool_w'][l, g_])))
            S.dma(pool, sem_misc, items)
            items = []
            for l in range(L):
                items.append(([], [], lambda e, l=l: e.dma_start(
                    out=dout['ca_s'][l].rearrange("(s j) c -> s j c", s=16)[:, 0:22, :],
                    in_=din['sta'][l].rearrange("(s j) c -> s j c", s=16)[:, 8:30, :])))
                items.append(([], [], lambda e, l=l: e.dma_start(
                    out=dout['po_s'][l].rearrange("(s j) c -> s j c", s=16)[:, 0:7, :],
                    in_=din['stp'][l].rearrange("(s j) c -> s j c", s=16)[:, 8:15, :])))
            S.dma(sp, sem_d2d, items, is_output=True)

        def late_setup_sgu():
            for l in range(L):
                for h in range(4):
                    j = l * 4 + h
                    pb, pr = bank()
                    S.op(pe, [rP1, rCONST], [pr], lambda e: e.transpose(
                        out=pb[:, 0:128], in_=P1[:, j * 128:(j + 1) * 128], identity=IDENT[:]))
                    S.op(dve, [pr, rCONST, rC2, rBDP, rBS], [rC2], lambda e: e.tensor_tensor(
                        out=WT[:, l, h, :], in0=pb[:, 0:128], in1=TRIL[:], op=ALU.mult))
                    S.op(dve, [rP1], [rP2], lambda e: e.tensor_copy(
                        out=P2[:, j * 128:(j + 1) * 128].rearrange("p (a s) -> p a s", a=16),
                        in_=P1[:, 1024 + j * 8:1024 + j * 8 + 8].unsqueeze(1).broadcast_to([128, 16, 8])))
                    pb2, pr2 = bank()
                    S.op(pe, [rP2, rCONST], [pr2], lambda e: e.transpose(
                        out=pb2[:, 0:128], in_=P2[:, j * 128:(j + 1) * 128], identity=IDENT[:]))
                    S.op(dve, [pr2, rCONST, rC2, rBDP, rBS], [rC2], lambda e: e.tensor_tensor(
                        out=WTS[:, l, h, :], in0=pb2[:, 0:128], in1=MASKS[:], op=ALU.mult))

        stream = []

        def a_tile(name, l, c0, ncol, nk=KC):
            def fn(e, buf):
                src = din[name][l].rearrange("(kc p) n -> p kc n", p=128)[:, :, c0:c0 + ncol]
                return e.dma_start(out=buf[:, 0:nk * ncol].rearrange("p (k n) -> p k n", k=nk), in_=src)
            return ('A', fn)

        def d_tile(name, l, half, mp):
            def fn(e, buf):
                src = din[name][l][half * HALF * 128:(half + 1) * HALF * 128, mp * 256:(mp + 1) * 256]
                src = src.rearrange("(j p) n -> p j n", p=128)
                return e.dma_start(out=buf[:, :].rearrange("p (j n) -> p j n", j=HALF), in_=src)
            return ('D', fn)

        def ffn_tiles(l, which):
            pre = f"w_ffn{which}_"
            for half in range(2):
                for tp in range(6):
                    c0 = (half * HALF + tp * 2) * 128
                    ncol = 256 if tp < 5 else 128
                    stream.append(a_tile(pre + 'gate', l, c0, ncol))
                    stream.append(a_tile(pre + 'up', l, c0, ncol))
                for mp in range(4):
                    stream.append(d_tile(pre + 'down', l, half, mp))

        for g in range(2):
            for l in range(L):
                ffn_tiles(l, 1)
                for part in (2, 0, 1, 4, 3, 6, 7, 5):
                    stream.append(a_tile('w_mix_in', l, part * 256, 256))
                for mp in range(4):
                    stream.append(a_tile('w_mix_out', l, mp * 256, 256))
                ffn_tiles(l, 2)
                stream.append(a_tile('w_ple_proj', l, 0, 1024, nk=2))
                for mp in range(4):
                    stream.append(a_tile('w_ple_gate', l, mp * 256, 256))

        wst = {'next_load': 0, 'next_use': 0, 'cnt': {'A': 0, 'D': 0}, 'slot_of': {}, 'free': {}}
        for i in range(4):
            wst['free'][('A', i)] = True
        for i in range(2):
            wst['free'][('D', i)] = True
        nslots = {'A': 4, 'D': 2}
        bufs = {'A': WA, 'D': WD}
        wres = {'A': WAR_, 'D': WDR}

        def pump():
            while wst['next_load'] < len(stream):
                i = wst['next_load']
                kind, fn = stream[i]
                cand = [kk for kk in range(nslots[kind]) if wst['free'][(kind, kk)]]
                if not cand:
                    break
                k = cand[0]
                wst['free'][(kind, k)] = False
                wst['cnt'][kind] += 1
                wst['slot_of'][i] = (kind, k)
                buf = bufs[kind][k]
                S.dma(pool, sem_w[(kind, k)], [(wst.pop('first_reads', []), [wres[kind][k]], lambda e, fn=fn, buf=buf: fn(e, buf))])
                wst['next_load'] += 1

        def getw(expect_kind):
            i = wst['next_use']
            assert i < wst['next_load'], "weight tile not loaded yet (slot starvation)"
            kind, k = wst['slot_of'][i]
            assert kind == expect_kind
            wst['next_use'] += 1
            return bufs[kind][k], wres[kind][k], (kind, k)

        def release(slot):
            wst['free'][slot] = True
            pump()

        def mm_group(reads, writes, mms):
            def fn(e):
                inst = None
                for (o, a, b, st, sp_) in mms:
                    inst = e.matmul(o, lhsT=a, rhs=b, start=st, stop=sp_)
                return inst
            S.op(pe, reads, writes, fn)

        def wavefront(items, stages, order=None):
            n_, m_ = len(items), len(stages)
            for step in range(n_ + m_ - 1):
                for s_ in (order or range(m_)):
                    i_ = step - s_
                    if 0 <= i_ < n_:
                        stages[s_](items[i_])

        def wavefront2(itemsA_, stagesA_, itemsB_, stagesB_, lag=0):
            nA, mA, nB, mB = len(itemsA_), len(stagesA_), len(itemsB_), len(stagesB_)
            for step in range(max(nA + mA - 1, lag + nB + mB - 1)):
                for s_ in range(mA):
                    i_ = step - s_
                    if 0 <= i_ < nA:
                        stagesA_[s_](itemsA_[i_])
                for s_ in range(mB):
                    i_ = step - lag - s_
                    if 0 <= i_ < nB:
                        stagesB_[s_](itemsB_[i_])

        def m_order(last, nb):
            if last:
                return [(mm, bi) for bi in range(nb) for mm in range(2)]
            return [(mm, bi) for mm in range(2) for bi in range(nb)]

        def blocks_of(g):
            return [(0, 512), (512, 512), (1024, 128)] if g == 0 else [(0, 512), (512, 512)]

        def xr_all(bi):
            return [XR[c][bi] for c in range(KC)]

        def hr_all(bi):
            return [HR[c][bi] for c in range(KC)]

        nst = {'key': None, 'done': set(), 'stat': {}}

        def norm_stats(g, gi, bi, final):
            b0, n = blocks_of(g)[bi]
            pb, pr = bank(pin=final)
            for qt in range(4):
                S.op(act, [XR[c][bi] for c in range(qt * 2, qt * 2 + 2)], [SQR[qt]], lambda e, qt=qt: e.activation(
                    out=SQ[:, qt * 2:qt * 2 + 2, 0:n], in_=X[:, qt * 2:qt * 2 + 2, b0:b0 + n], func=AF.Square))
            S.op(act, [rCONST], [rDUM], lambda e: e.activation(out=DUMV[:, 0:1], in_=EPSV[:, 0:1], func=AF.Ln))
            for qt in range(4):
                mm_group([SQR[qt], rCONST], [pr],
                         [(pb[:, 0:n], ONES[:], SQ[:, c, 0:n], c == 0, c == KC - 1) for c in range(qt * 2, qt * 2 + 2)])
            S.op(act, [pr, rCONST], [pr], lambda e: e.activation(
                out=pb[:, 0:n], in_=pb[:, 0:n], func=AF.Ln, bias=EPSV[:, 0:1], scale=1.0))
            S.op(act, [pr], [pr], lambda e: e.activation(out=pb[:, 0:n], in_=pb[:, 0:n], func=AF.Exp, scale=-0.5))
            nst['stat'][bi] = (pb, pr)

        def norm_apply(g, gi, bi, final, tile_cb=None):
            b0, n = blocks_of(g)[bi]
            pb, pr = nst['stat'][bi]
            for c in range(KC):
                if not final:
                    S.op(dve, [XR[c][bi], pr, rCONST], [HR[c][bi]], lambda e, c=c: e.scalar_tensor_tensor(
                        out=H[:, c, b0:b0 + n], in0=X[:, c, b0:b0 + n], scalar=G1024[:, c, gi:gi + 1],
                        in1=pb[:, 0:n], op0=ALU.mult, op1=ALU.mult))
                else:
                    S.op(dve, [XR[c][bi], pr, rCONST], YFBs[bi % 2][1], lambda e, c=c: e.scalar_tensor_tensor(
                        out=YFBs[bi % 2][0][:, c, 0:n], in0=X[:, c, b0:b0 + n], scalar=G1024[:, c, gi:gi + 1],
                        in1=pb[:, 0:n], op0=ALU.mult, op1=ALU.mult))
            if final:
                unpin(pr)
                tile_cb(bi, b0, n)

        def norm_hook(g, nxt):
            if nxt is None:
                return lambda bi: None
            gi, final = nxt
            key = (g, gi)

            def hook(bi):
                if nst['key'] != key:
                    nst['key'], nst['done'], nst['stat'] = key, set(), {}
                norm_stats(g, gi, bi, final)
                if not final:
                    norm_apply(g, gi, bi, final)
                nst['done'].add(bi)
            return hook

        def rmsnorm(g, gi, final=False, tile_cb=None):
            key = (g, gi)
            if nst['key'] != key:
                nst['key'], nst['done'], nst['stat'] = key, set(), {}
            nb = len(blocks_of(g))
            if final:
                for bi in range(nb):
                    if bi not in nst['done']:
                        norm_stats(g, gi, bi, final)
                for bi in range(nb):
                    norm_apply(g, gi, bi, final, tile_cb)
            else:
                for bi in range(nb):
                    if bi not in nst['done']:
                        norm_stats(g, gi, bi, final)
                        norm_apply(g, gi, bi, final)
            nst['key'] = None

        def rslot(j):
            return RA[:, j, :] if j < 8 else RB[:, j - 8, :]

        def ffn(g, l, which, mid_cb=None, nxt=None):
            blocks = blocks_of(g)
            rmsnorm(g, l * 4 + (0 if which == 1 else 2))
            for half in range(2):
                for tp in range(6):
                    ncol = 256 if tp < 5 else 128
                    wg, wgr, sg = getw('A')
                    wu, wur, su = getw('A')
                    wg3 = wg[:, 0:KC * ncol].rearrange("p (k n) -> p k n", k=KC)
                    wu3 = wu[:, 0:KC * ncol].rearrange("p (k n) -> p k n", k=KC)
                    njj = ncol // 128
                    if half == 0 and tp == 0:
                        jb_order = [(jj, bi) for bi in range(len(blocks)) for jj in range(njj)]
                    else:
                        jb_order = [(jj, bi) for jj in range(njj) for bi in range(len(blocks))]
                    for jj, bi in jb_order:
                        slot = tp * 2 + jj
                        for (b0, n) in [blocks[bi]]:
                            pg, pgr = bank()
                            pu, pur = bank()
                            if half == 0 and tp == 0 and jj == 0:
                                for kc in range(KC):
                                    mm_group([wgr, wur, HR[kc][bi]], [pgr, pur],
                                             [(pg[:, 0:n], wg3[:, kc, jj * 128:(jj + 1) * 128], H[:, kc, b0:b0 + n], kc == 0, kc == KC - 1),
                                              (pu[:, 0:n], wu3[:, kc, jj * 128:(jj + 1) * 128], H[:, kc, b0:b0 + n], kc == 0, kc == KC - 1)])
                            else:
                                mm_group([wgr, wur] + hr_all(bi), [pgr, pur],
                                         [(pg[:, 0:n], wg3[:, kc, jj * 128:(jj + 1) * 128], H[:, kc, b0:b0 + n], kc == 0, kc == KC - 1) for kc in range(KC)] +
                                         [(pu[:, 0:n], wu3[:, kc, jj * 128:(jj + 1) * 128], H[:, kc, b0:b0 + n], kc == 0, kc == KC - 1) for kc in range(KC)])
                            t1, t1r = tmp()
                            S.op(act, [pgr], [t1r], lambda e: e.activation(out=t1[:, 0:n], in_=pg[:, 0:n], func=AF.Silu))
                            S.op(dve, [t1r, pur], [RR[slot][bi]], lambda e: e.tensor_tensor(
                                out=rslot(slot)[:, b0:b0 + n], in0=pu[:, 0:n], in1=t1[:, 0:n], op=ALU.mult))
                    release(sg)
                    release(su)
                for mp in range(4):
                    wd, wdr, sd = getw('D')
                    wd3 = wd[:, :].rearrange("p (j n) -> p j n", j=HALF)
                    lastmp = (half == 1 and mp == 3)
                    hook = norm_hook(g, nxt)
                    for mm, bi in m_order(lastmp, len(blocks)):
                        m = mp * 2 + mm
                        for (b0, n) in [blocks[bi]]:
                            pd, pdr = bank()
                            mm_group([wdr] + [RR[s][bi] for s in range(HALF)], [pdr],
                                     [(pd[:, 0:n], wd3[:, s, mm * 128:(mm + 1) * 128], rslot(s)[:, b0:b0 + n], s == 0, s == HALF - 1)
                                      for s in range(HALF)])
                            S.op(dve, [pdr, XR[m][bi]], [XR[m][bi]], lambda e: e.scalar_tensor_tensor(
                                out=X[:, m, b0:b0 + n], in0=pd[:, 0:n], scalar=0.5, in1=X[:, m, b0:b0 + n],
                                op0=ALU.mult, op1=ALU.add))
                        if lastmp and mm == 1 and bi > 0:
                            hook(bi - 1)
                    if lastmp:
                        hook(len(blocks) - 1)
                    release(sd)
                if half == 0 and mid_cb is not None:
                    mid_cb()

        def z_mms(ps_ap, w3, cc, bi, b0, n):
            return [(ps_ap[:, 0:n], w3[:, kc, cc * 128:(cc + 1) * 128], H[:, kc, b0:b0 + n], kc == 0, kc == KC - 1)
                    for kc in range(KC)]

        def a3(w):
            return w[:, 0:KC * 256].rearrange("p (k n) -> p k n", k=KC)

        def out_T(srcs, src_res, nrows, dst_fn):
            pb, pr = bank()
            S.op(pe, src_res + [rCONST], [pr], lambda e: [
                e.transpose(out=pb[0:nrows, cc * 128:(cc + 1) * 128], in_=srcs[cc], identity=IDENT[:]) for cc in range(2)][-1])
            k = state['ost']
            state['ost'] = (k + 1) % 3
            S.op(act, [pr], [OSTR[k]], lambda e: e.activation(out=OST[k][0:nrows, :], in_=pb[0:nrows, 0:256], func=AF.Copy))
            S.dma(sp, sem_ost[k], [([OSTR[k]], [], lambda e: e.dma_start(out=dst_fn(), in_=OST[k][0:nrows, :]))], is_output=True)

        def load_T(src_ap, nrows, evac):
            k = state['ost']
            state['ost'] = (k + 1) % 3
            S.dma(sp, sem_ost[k], [([], [OSTR[k]], lambda e: e.dma_start(out=OST[k][0:nrows, :], in_=src_ap))])
            pb, pr = bank()
            S.op(pe, [OSTR[k], rCONST], [pr], lambda e: [
                e.transpose(out=pb[:, cc * 128:cc * 128 + nrows], in_=OST[k][0:nrows, cc * 128:(cc + 1) * 128],
                            identity=IDENT[0:nrows, 0:nrows]) for cc in range(2)][-1])
            for cc in range(2):
                ew, reads, writes, fn = evac(cc, pb[:, cc * 128:cc * 128 + nrows])
                S.op(ew, [pr] + reads, writes, fn)

        def mixer_prep(g, l):
            pbase = l * 40
            if g == 0:
                for i in range(4):
                    halo_T(DIAGf[:, i * 256:(i + 1) * 256], [rDIAG], 120, lambda cc, ps_ap, i=i: (
                        [], [rHASh], lambda e: e.activation(
                            out=HAS[:, cc, 4 * i:4 * i + 4, 0:30], in_=ps_ap.rearrange("p (s j) -> p s j", s=4), func=AF.Copy)))
                halo_T(DIAGf[:, 1024:1280], [rDIAG], 32, lambda cc, ps_ap: (
                    [], [rQSh], lambda e: e.activation(
                        out=QS[:, cc, :, 0:2], in_=ps_ap.rearrange("p (s j) -> p s j", s=16), func=AF.Copy)))
            else:
                S.op(dve, [rHALOA], [rHAh[0], rHAh[1]], lambda e: e.tensor_copy(out=HA[:, :, 0:30], in_=HALOA[:, l, :, :]))
                S.op(dve, [rHALOQ], [rQh[0], rQh[1]], lambda e: e.tensor_copy(out=Q[:, :, 0:2], in_=HALOQ[:, l, :, :]))

            for cc in (1, 0):
                for k in range(31):
                    wr = [rDKc[cc][k]] + ([rDIAG] if (k == 0 and cc == 0) else [])
                    S.op(dve, [rCONST] + ([rDIAG] if cc == 0 else []), wr, lambda e, k=k, cc=cc: e.tensor_scalar(
                        out=DIAGc[cc][:, k, :], in0=IDENT[:], scalar1=P256[:, cc, pbase + k:pbase + k + 1], scalar2=None,
                        op0=ALU.mult))


        def mixers(g, l):
            blocks = blocks_of(g)
            nbk = len(blocks)
            pbase = l * 40
            rmsnorm(g, l * 4 + 1)
            last_p = 1

            wb, wbr, sbk = getw('A')
            wb3 = a3(wb)
            inv = {(0, 0): 0.5, (0, 1): 0.25, (1, 0): 0.125, (1, 1): 0.0625}
            P1S = P1[:, ZW:].rearrange("p (s t) -> p s t", s=16)
            P2S = P2[:, ZW:].rearrange("p (s t) -> p s t", s=16)
            ZBc = [ZB, RAf[:, 2304:2304 + ZW]]
            ZBSc = [ZBS, RAf[:, 2304 + ZW:2304 + ZW + 16 * 23].rearrange("p (s t) -> p s t", s=16)]
            ZRc = [RBall, [RR[j][b_] for j in (4, 5, 6) for b_ in range(NB)]]
            DBc = [DB[:, :], RA[:, 7, :]]
            DBRc = [[rDB], [RR[7][b_] for b_ in range(NB)]]
            if g == 0:
                for i in range(2):
                    halo_T(VTf[:, i * 256:(i + 1) * 256], [rVT], 120, lambda c2, ps_ap, i=i: (
                        [], ZRc[c2], lambda e: e.activation(
                            out=ZBSc[c2][:, 8 * i:8 * i + 8, 0:15], in_=ps_ap.rearrange("p (s j) -> p s j", s=8), func=AF.Copy)))
            for cc in range(2):
                zb, zr_ = ZBc[cc], ZRc[cc]
                if g == 0:
                    S.op(dve, [], zr_, lambda e: e.memset(zb[:, 0:15], 0.0))
                else:
                    S.op(dve, [rHALOB], zr_, lambda e: e.tensor_copy(out=zb[:, 0:15], in_=HALOB[:, l, cc, :]))
            for bi, (b0, n) in enumerate(blocks):
                for cc in range(2):
                    zb, zbs, zr_ = ZBc[cc], ZBSc[cc], ZRc[cc]
                    pz, pzr = bank()
                    mm_group([wbr] + hr_all(bi), [pzr], z_mms(pz, wb3, cc, bi, b0, n))
                    if b0 < 1024:
                        S.op(act, [pzr], zr_, lambda e: e.activation(out=zb[:, 15 + b0:15 + b0 + n], in_=pz[:, 0:n], func=AF.Copy))
                    else:
                        S.op(act, [pzr], zr_, lambda e: e.activation(
                            out=zbs[:, :, 15:23], in_=pz[:, 0:128].rearrange("p (s t) -> p s t", s=16), func=AF.Copy))
            for cc in range(2):
                zb, zbs, zr_, db, dbr = ZBc[cc], ZBSc[cc], ZRc[cc], DBc[cc], DBRc[cc]
                zr = zr_

                def lvl(dst, dstS, src, srcS, shift, lo, plo, dres, sres):
                    S.op(pool, sres, dres, lambda e: e.tensor_tensor(
                        out=dst[plo:128, lo:ZW], in0=src[plo:128, lo:ZW], in1=src[plo:128, lo - shift:ZW - shift], op=ALU.add))
                    if g == 0:
                        S.op(pool, sres, dres, lambda e: e.tensor_tensor(
                            out=dstS[plo:128, :, lo:23], in0=srcS[plo:128, :, lo:23], in1=srcS[plo:128, :, lo - shift:23 - shift], op=ALU.add))
                if cc == 0:
                    lvl(P1, P1S, zb, zbs, 1, 1, 0, [rP1], zr)
                    lvl(P2, P2S, P1, P1S, 2, 3, 64, [rP2], [rP1])
                else:
                    lvl(P1, P1S, zb, zbs, 1, 1, 0, [rP1], zr)
                    lvl(P2, P2S, P1, P1S, 2, 3, 0, [rP2], [rP1])
                    lvl(P1, P1S, P2, P2S, 4, 7, 0, [rP1], [rP2])
                    lvl(P2, P2S, P1, P1S, 8, 15, 64, [rP2], [rP1])
                for hf in range(2):
                    Sb, SbS = (P1, P1S) if hf == 0 else (P2, P2S)
                    pl, ph = hf * 64, hf * 64 + 64
                    iv = inv[(cc, hf)]
                    if g == 0:
                        S.op(pool, [rP1, rP2, rCONST, rC2, rBDP, rBS], [rCORT], lambda e: e.tensor_tensor(
                            out=CORT[pl:ph, 0:15], in0=Sb[pl:ph, 15:30], in1=CORR[pl:ph, cc, 0:15], op=ALU.mult))
                    S.op(pool, [rP1, rP2], [rP1, rP2], lambda e: e.tensor_scalar(
                        out=Sb[pl:ph, 15:ZW], in0=Sb[pl:ph, 15:ZW], scalar1=iv, scalar2=0.0, op0=ALU.mult, op1=ALU.add))
                    S.op(pool, [rP1, rP2] + zr, dbr, lambda e: e.tensor_tensor(
                        out=db[pl:ph, 0:1024], in0=Sb[pl:ph, 15:ZW], in1=zb[pl:ph, 15:ZW], op=ALU.subtract))
                    if g == 0:
                        S.op(pool, [rP1, rP2], [rP1, rP2], lambda e: e.tensor_scalar(
                            out=SbS[pl:ph, :, 15:23], in0=SbS[pl:ph, :, 15:23], scalar1=iv, scalar2=0.0, op0=ALU.mult, op1=ALU.add))
                        S.op(pool, [rP1, rP2] + zr, dbr, lambda e: e.tensor_tensor(
                            out=db[pl:ph, 1024:1152].rearrange("p (s t) -> p s t", s=16), in0=SbS[pl:ph, :, 15:23],
                            in1=zbs[pl:ph, :, 15:23], op=ALU.subtract))
                        S.op(pool, [rCORT] + zr, dbr, lambda e: e.tensor_tensor(
                            out=db[pl:ph, 0:15], in0=CORT[pl:ph, 0:15], in1=zb[pl:ph, 15:30], op=ALU.subtract))
                if g == 0:
                    S.op(dve, zr, [rHALOB], lambda e: e.tensor_copy(out=HALOB[:, l, cc, :], in_=zb[:, ZW - 15:ZW]))
                    S.op(dve, zr, [rSTBS], lambda e: e.tensor_copy(
                        out=STBS[:, cc, :].rearrange("p (s t) -> p s t", s=16), in_=zbs[:, :, 15:23]))
                else:
                    S.op(dve, zr, [rSTB], lambda e: e.tensor_copy(out=STB[:, cc, 0:15], in_=zb[:, ZW - 15:ZW]))
            release(sbk)
            if g == 1:
                out_T([STB[:, 0, 0:15], STB[:, 1, 0:15]], [rSTB], 15, lambda: dout['po_p'][l])
            else:
                out_T([STBS[:, 0, :], STBS[:, 1, :]], [rSTBS], 128,
                      lambda: dout['po_s'][l].rearrange("(s j) c -> s j c", s=16)[:, 7:15, :])

            wv, wvr, sv = getw('A')
            wgt, wgtr, sgt = getw('A')
            wv3, wgt3 = a3(wv), a3(wgt)
            itemsA = [dict(cc=cc, bi=bi, b0=b0, n=n) for bi, (b0, n) in enumerate(blocks) for cc in range(2)]

            def A1(d):
                cc, bi, b0, n = d['cc'], d['bi'], d['b0'], d['n']
                d['pv'], d['pvr'] = bank(pin=True)
                d['pg'], d['pgr'] = bank()
                mm_group([wvr, wgtr] + hr_all(bi), [d['pvr'], d['pgr']],
                         z_mms(d['pv'], wv3, cc, bi, b0, n) + z_mms(d['pg'], wgt3, cc, bi, b0, n))
                d['t1'], d['t1r'] = tmp(pin=True, ext=True)
                S.op(act, [d['pgr']], [d['t1r']], lambda e: e.activation(out=d['t1'][:, 0:n], in_=d['pg'][:, 0:n], func=AF.Sigmoid))
                if d is itemsA[-1]:
                    release(sv)
                    release(sgt)

            def A2(d):
                n = d['n']
                d['t2'], d['t2r'] = tmp(pin=True, ext=True)
                S.op(dve, [d['pvr'], d['t1r']], [d['t2r']], lambda e: e.tensor_tensor(
                    out=d['t2'][:, 0:n], in0=d['pv'][:, 0:n], in1=d['t1'][:, 0:n], op=ALU.mult))
                unpin(d['pvr'])
                unpin_t(d['t1r'])

            def A3(d):
                cc, bi, b0, n = d['cc'], d['bi'], d['b0'], d['n']
                t2, t2r = d['t2'], d['t2r']
                if b0 < 1024:
                    S.op(act, [t2r], [rHA[cc][bi]], lambda e: e.activation(
                        out=HA[:, cc, 30 + b0:30 + b0 + n], in_=t2[:, 0:n], func=AF.Copy))
                    if g == 1 and bi == last_p:
                        S.op(dve, [t2r], [rSTA], lambda e: e.tensor_copy(out=STA[:, cc, 0:30], in_=t2[:, n - 30:n]))
                    if g == 0 and bi == last_p:
                        S.op(dve, [rHA[cc][bi]], [rHALOA], lambda e: e.tensor_copy(
                            out=HALOA[:, l, cc, :], in_=HA[:, cc, 1024:1054]))
                else:
                    S.op(act, [t2r], [rHA[cc][bi]], lambda e: e.activation(
                        out=HAS[:, cc, :, 30:38], in_=t2[:, 0:128].rearrange("p (s t) -> p s t", s=16), func=AF.Copy))
                    S.op(dve, [t2r], [rSTAS], lambda e: e.tensor_copy(out=STAS[:, cc, :], in_=t2[:, 0:128]))
                unpin_t(t2r)

            def A4(d):
                cc, bi, b0, n = d['cc'], d['bi'], d['b0'], d['n']
                pc, pcr = bank()
                DG, rdk = DIAGc[cc], rDKc[cc]
                if b0 < 1024:
                    rd = rdk + [rHA[cc][bi], rHAh[cc]] + ([rHA[cc][bi - 1]] if bi > 0 else [])
                    mm_group(rd, [pcr], [(pc[:, 0:n], DG[:, k, :], HA[:, cc, b0 + k:b0 + k + n], k == 0, k == 30) for k in range(31)])
                else:
                    mm_group(rdk + [rHA[cc][bi], rHASh], [pcr],
                             [(pc[:, 0:128].rearrange("p (s t) -> p s t", s=16), DG[:, k, :], HAS[:, cc, :, k:k + 8], k == 0, k == 30)
                              for k in range(31)])
                d['t3'], d['t3r'] = tmp(pin=True, ext=True)
                S.op(act, [pcr, rCONST, rC2, rBDP, rBS], [d['t3r']], lambda e: e.activation(
                    out=d['t3'][:, 0:n], in_=pc[:, 0:n], func=AF.Identity, bias=P256[:, cc, pbase + 31:pbase + 32], scale=1.0))

            def A5(d):
                n = d['n']
                d['pd'], d['pdr'] = bank(pin=True)
                mm_group([d['t3r'], rCONST, rC2, rBDP, rBS], [d['pdr']], [(d['pd'][:, 0:n], IMBD[:], d['t3'][:, 0:n], True, True)])
                d['t4'], d['t4r'] = tmp(pin=True, ext=True)
                S.op(act, [d['pdr']], [d['t4r']], lambda e: e.activation(out=d['t4'][:, 0:n], in_=d['pd'][:, 0:n], func=AF.Square))
                unpin_t(d['t3r'])

            def A6(d):
                cc, bi, b0, n = d['cc'], d['bi'], d['b0'], d['n']
                pw, pwr = bank()
                mm_group([d['t4r'], rCONST, rC2, rBDP, rBS], [pwr], [(pw[:, 0:n], BD[:], d['t4'][:, 0:n], True, True)])
                S.op(act, [pwr, rCONST, rC2, rBDP, rBS], [pwr], lambda e: e.activation(
                    out=pw[:, 0:n], in_=pw[:, 0:n], func=AF.Ln, bias=EPSV[:, 0:1], scale=1.0))
                t5, t5r = tmp(ext=True)
                S.op(act, [pwr], [t5r], lambda e: e.activation(out=t5[:, 0:n], in_=pw[:, 0:n], func=AF.Exp, scale=-0.5))
                t6, t6r = tmp(ext=True)
                S.op(dve, [d['pdr'], t5r], [t6r], lambda e: e.tensor_tensor(out=t6[:, 0:n], in0=d['pd'][:, 0:n], in1=t5[:, 0:n], op=ALU.mult))
                S.op(act, [t6r, rCONST, rC2, rBDP, rBS], [RR[cc][bi]], lambda e: e.activation(
                    out=RA[:, cc, b0:b0 + n], in_=t6[:, 0:n], func=AF.Silu,
                    bias=P256[:, cc, pbase + 33:pbase + 34], scale=P256[:, cc, pbase + 32:pbase + 33]))
                unpin(d['pdr'])
                unpin_t(d['t4r'])

            wavefront(itemsA, [A1, A2, A3, A4, A5, A6])
            wvv, wvvr, svv = getw('A')
            wuu, wuur, suu = getw('A')
            wvv3, wuu3 = a3(wvv), a3(wuu)
            pairsC = [[dict(cc=cc, bi=bi, b0=b0, n=n) for cc in range(2)] for bi, (b0, n) in enumerate(blocks)]

            def PC1(P):
                for d in P:
                    cc, bi, b0, n = d['cc'], d['bi'], d['b0'], d['n']
                    d['pv'], d['pvr'] = bank()
                    mm_group([wvvr] + hr_all(bi), [d['pvr']], z_mms(d['pv'], wvv3, cc, bi, b0, n))
                for d in P:
                    n = d['n']
                    d['t1'], d['t1r'] = tmp(pin=True, ext=True)
                    S.op(act, [d['pvr']], [d['t1r']], lambda e: e.activation(out=d['t1'][:, 0:n], in_=d['pv'][:, 0:n], func=AF.Gelu_apprx_tanh))
                if P is pairsC[-1]:
                    release(svv)

            def PC2(P):
                for d in P:
                    n = d['n']
                    d['pd'], d['pdr'] = bank(pin=True)
                    mm_group([d['t1r'], rCONST, rC2, rBDP, rBS], [d['pdr']], [(d['pd'][:, 0:n], IMBD[:], d['t1'][:, 0:n], True, True)])
                for d in P:
                    n = d['n']
                    d['t2'], d['t2r'] = tmp(pin=True, ext=True)
                    S.op(act, [d['pdr']], [d['t2r']], lambda e: e.activation(out=d['t2'][:, 0:n], in_=d['pd'][:, 0:n], func=AF.Square))
                    unpin_t(d['t1r'])

            def PC3(P):
                for d in P:
                    n = d['n']
                    d['pw'], d['pwr'] = bank()
                    mm_group([d['t2r'], rCONST, rC2, rBDP, rBS], [d['pwr']], [(d['pw'][:, 0:n], BD[:], d['t2'][:, 0:n], True, True)])
                for d in P:
                    n = d['n']
                    S.op(act, [d['pwr'], rCONST, rC2, rBDP, rBS], [d['pwr']], lambda e: e.activation(
                        out=d['pw'][:, 0:n], in_=d['pw'][:, 0:n], func=AF.Ln, bias=EPSV[:, 0:1], scale=1.0))
                for d in P:
                    n = d['n']
                    d['t3'], d['t3r'] = tmp(ext=True)
                    S.op(act, [d['pwr']], [d['t3r']], lambda e: e.activation(
                        out=d['t3'][:, 0:n], in_=d['pw'][:, 0:n], func=AF.Exp, scale=-0.5))
                for d in P:
                    cc, n = d['cc'], d['n']
                    d['t4'], d['t4r'] = tmp(ext=True)
                    S.op(dve, [d['pdr'], d['t3r'], rCONST], [d['t4r']], lambda e: e.scalar_tensor_tensor(
                        out=d['t4'][:, 0:n], in0=d['pd'][:, 0:n], scalar=P256[:, cc, pbase + 35:pbase + 36], in1=d['t3'][:, 0:n],
                        op0=ALU.mult, op1=ALU.mult))
                for d in P:
                    cc, n = d['cc'], d['n']
                    d['t5'], d['t5r'] = tmp(pin=True, ext=True)
                    S.op(act, [d['t4r'], rCONST], [d['t5r']], lambda e: e.activation(
                        out=d['t5'][:, 0:n], in_=d['t4'][:, 0:n], func=AF.Identity, bias=P256[:, cc, pbase + 36:pbase + 37], scale=1.0))
                    unpin(d['pdr'])
                    unpin_t(d['t2r'])

            def PC4(P):
                for d in P:
                    n = d['n']
                    nt = n // 128
                    t5 = d['t5']
                    d['pt'], d['ptr'] = bank()
                    pt = d['pt']
                    S.op(pe, [d['t5r'], rCONST, rC2, rBDP, rBS], [d['ptr']], lambda e: [
                        e.transpose(out=pt[:, ti * 128:(ti + 1) * 128], in_=t5[:, ti * 128:(ti + 1) * 128], identity=IDENT[:])
                        for ti in range(nt)][-1])
                for d in P:
                    cc, bi, b0, n = d['cc'], d['bi'], d['b0'], d['n']
                    nt = n // 128
                    tt0 = b0 // 128
                    pt = d['pt']
                    S.op(dve, [d['ptr']], [rVTc[cc][bi], rVT], lambda e: e.tensor_copy(
                        out=VT[:, tt0:tt0 + nt, cc * 128:(cc + 1) * 128], in_=pt[:, 0:n].rearrange("p (a c) -> p a c", a=nt)))
                    if b0 >= 1024:
                        k = state['ost']
                        state['ost'] = (k + 1) % 3
                        S.op(act, [d['ptr']], [OSTR[k]], lambda e: e.activation(
                            out=OST[k][:, 0:128], in_=pt[:, 0:128], func=AF.Copy))
                        S.dma(sp, sem_ost[k], [([OSTR[k]], [], lambda e: e.dma_start(
                            out=dout['cv_s'][l][:, cc * 128:(cc + 1) * 128], in_=OST[k][:, 0:128]))], is_output=True)
                    unpin_t(d['t5r'])

            def PC5(P):
                for d in P:
                    cc, bi, b0, n = d['cc'], d['bi'], d['b0'], d['n']
                    nt = n // 128
                    tt0 = b0 // 128
                    d['ps'], d['psr'] = bank()
                    mms = []
                    for ti in range(nt):
                        for hh in range(2):
                            h = 2 * cc + hh
                            wt = WTS[:, l, h, :] if b0 >= 1024 else WT[:, l, h, :]
                            mms.append((d['ps'][hh * 64:(hh + 1) * 64, ti * 128:(ti + 1) * 128],
                                        VT[:, tt0 + ti, h * 64:(h + 1) * 64], wt, True, True))
                    mm_group([rVTc[cc][bi], rCONST, rC2, rBDP, rBS], [d['psr']], mms)
                    d['pu'], d['pur'] = bank()
                    mm_group([wuur] + hr_all(bi), [d['pur']], z_mms(d['pu'], wuu3, cc, bi, b0, n))
                for d in P:
                    n = d['n']
                    d['t6'], d['t6r'] = tmp(ext=True)
                    S.op(act, [d['pur']], [d['t6r']], lambda e: e.activation(out=d['t6'][:, 0:n], in_=d['pu'][:, 0:n], func=AF.Gelu_apprx_tanh))
                for d in P:
                    cc, b0, n = d['cc'], d['b0'], d['n']
                    nt = n // 128
                    d['t7'], d['t7r'] = tmp(ext=True)
                    if b0 < 1024:
                        S.op(dve, [d['psr'], rCONST, rC2, rBDP, rBS], [d['t7r']], lambda e: e.tensor_tensor(
                            out=d['t7'][:, 0:n].rearrange("p (a t) -> p a t", a=nt), in0=d['ps'][:, 0:n].rearrange("p (a t) -> p a t", a=nt),
                            in1=BSP[:, l, cc, :].unsqueeze(1).broadcast_to([128, nt, 128]), op=ALU.add))
                    else:
                        S.op(dve, [d['psr'], rCONST, rC2, rBDP, rBS], [d['t7r']], lambda e: e.tensor_tensor(
                            out=d['t7'][:, 0:n], in0=d['ps'][:, 0:n], in1=BSS[:, l, cc, :], op=ALU.add))
                for d in P:
                    cc, bi, b0, n = d['cc'], d['bi'], d['b0'], d['n']
                    S.op(dve, [d['t6r'], d['t7r']], [RR[4 + cc][bi]], lambda e: e.tensor_tensor(
                        out=RA[:, 4 + cc, b0:b0 + n], in0=d['t6'][:, 0:n], in1=d['t7'][:, 0:n], op=ALU.mult))

            wavefront(pairsC, [PC1, PC2, PC3, PC4, PC5])
            release(suu)
            if g == 1:
                out_T([STA[:, 0, 0:30], STA[:, 1, 0:30]], [rSTA], 30, lambda: dout['ca_p'][l])
            else:
                out_T([STAS[:, 0, :], STAS[:, 1, :]], [rSTAS], 128,
                      lambda: dout['ca_s'][l].rearrange("(s j) c -> s j c", s=16)[:, 22:30, :])

            for cc in range(2):
                db, dbr = DBc[cc], DBRc[cc]
                for bi, (b0, n) in enumerate(blocks):
                    pp_, ppr = bank()
                    mm_group(dbr + [rCONST, rC2, rBDP, rBS], [ppr], [(pp_[:, 0:n], BDP[:, l, cc, :], db[:, b0:b0 + n], True, True)])
                    S.op(act, [ppr, rCONST, rC2, rBDP, rBS], [RR[2 + cc][bi]], lambda e: e.activation(
                        out=RA[:, 2 + cc, b0:b0 + n], in_=pp_[:, 0:n], func=AF.Identity,
                        scale=P256[:, cc, pbase + 34:pbase + 35]))


            wc, wcr, sc_ = getw('A')
            wdi, wdir, sdi = getw('A')
            wc3, wdi3 = a3(wc), a3(wdi)
            itemsD = [dict(cc=cc, bi=bi, b0=b0, n=n) for cc in range(2) for bi, (b0, n) in enumerate(blocks)]

            def D1(d):
                cc, bi, b0, n = d['cc'], d['bi'], d['b0'], d['n']
                pc, pcr = bank()
                d['pdn'], d['pdnr'] = bank()
                mm_group([wcr, wdir] + hr_all(bi), [pcr, d['pdnr']], z_mms(pc, wc3, cc, bi, b0, n) + z_mms(d['pdn'], wdi3, cc, bi, b0, n))
                d['t1'], d['t1r'] = tmp()
                S.op(act, [pcr], [d['t1r']], lambda e: e.activation(out=d['t1'][:, 0:n], in_=pc[:, 0:n], func=AF.Copy))

            def D2(d):
                n = d['n']
                d['t2'], d['t2r'] = tmp()
                S.op(dve, [d['pdnr'], d['t1r']], [d['t2r']], lambda e: e.tensor_tensor(
                    out=d['t2'][:, 0:n], in0=d['pdn'][:, 0:n], in1=d['t1'][:, 0:n], op=ALU.mult))

            def D3(d):
                cc, bi, b0, n = d['cc'], d['bi'], d['b0'], d['n']
                t2, t2r = d['t2'], d['t2r']
                if b0 < 1024:
                    S.op(act, [t2r], [rQ[cc][bi]], lambda e: e.activation(out=Q[:, cc, 2 + b0:2 + b0 + n], in_=t2[:, 0:n], func=AF.Copy))
                    if g == 1 and bi == last_p:
                        S.op(dve, [t2r], [rSTQ], lambda e: e.tensor_copy(out=STQ[:, cc, 0:2], in_=t2[:, n - 2:n]))
                    if g == 0 and bi == last_p:
                        S.op(dve, [rQ[cc][bi]], [rHALOQ], lambda e: e.tensor_copy(out=HALOQ[:, l, cc, :], in_=Q[:, cc, 1024:1026]))
                else:
                    S.op(act, [t2r], [rQ[cc][bi]], lambda e: e.activation(
                        out=QS[:, cc, :, 2:10], in_=t2[:, 0:128].rearrange("p (s t) -> p s t", s=16), func=AF.Copy))
                    S.op(dve, [t2r], [rSTQS], lambda e: e.tensor_copy(
                        out=STQS[:, cc, :].rearrange("p (s j) -> p s j", s=16),
                        in_=t2[:, 0:128].rearrange("p (s t) -> p s t", s=16)[:, :, 6:8]))

            wavefront(itemsD, [D1, D2, D3])
            release(sc_)
            release(sdi)
            wbg, wbgr, sbg = getw('A')
            wbg3 = a3(wbg)
            itemsD2 = [dict(cc=cc, bi=bi, b0=b0, n=n) for cc in range(2) for bi, (b0, n) in enumerate(blocks)]

            def E1(d):
                cc, bi, b0, n = d['cc'], d['bi'], d['b0'], d['n']
                if bi == 0:
                    for k in range(3):
                        S.op(dve, [rCONST], [rDQK[k]], lambda e, k=k: e.tensor_scalar(
                            out=DIAGQ[:, k, :], in0=IDENT[:], scalar1=P256[:, cc, pbase + 37 + k:pbase + 38 + k], scalar2=None,
                            op0=ALU.mult))
                pb_, pbr = bank()
                d['py'], d['pyr'] = bank()
                py = d['py']
                if b0 < 1024:
                    rd = rDQK + [rQ[cc][bi], rQh[cc]] + ([rQ[cc][bi - 1]] if bi > 0 else [])
                    cm = [(py[:, 0:n], DIAGQ[:, k, :], Q[:, cc, b0 + k:b0 + k + n], k == 0, k == 2) for k in range(3)]
                else:
                    rd = rDQK + [rQ[cc][bi], rQSh]
                    cm = [(py[:, 0:128].rearrange("p (s t) -> p s t", s=16), DIAGQ[:, k, :], QS[:, cc, :, k:k + 8], k == 0, k == 2)
                          for k in range(3)]
                mm_group([wbgr] + hr_all(bi) + rd, [pbr, d['pyr']], z_mms(pb_, wbg3, cc, bi, b0, n) + cm)
                d['t1'], d['t1r'] = tmp()
                S.op(act, [pbr], [d['t1r']], lambda e: e.activation(out=d['t1'][:, 0:n], in_=pb_[:, 0:n], func=AF.Copy))

            def E2(d):
                cc, bi, b0, n = d['cc'], d['bi'], d['b0'], d['n']
                S.op(dve, [d['pyr'], d['t1r']], [RR[6 + cc][bi]], lambda e: e.tensor_tensor(
                    out=RA[:, 6 + cc, b0:b0 + n], in0=d['py'][:, 0:n], in1=d['t1'][:, 0:n], op=ALU.mult))

            wavefront(itemsD2, [E1, E2])
            release(sbg)
            if g == 1:
                out_T([STQ[:, 0, 0:2], STQ[:, 1, 0:2]], [rSTQ], 2, lambda: dout['sc_p'][l])
            else:
                out_T([STQS[:, 0, :], STQS[:, 1, :]], [rSTQS], 32, lambda: dout['sc_s'][l])

            for mp in range(4):
                wo, wor, so = getw('A')
                wo3 = a3(wo)
                hook = norm_hook(g, (l * 4 + 2, False))
                for mm, bi in m_order(mp == 3, len(blocks)):
                    m = mp * 2 + mm
                    for (b0, n) in [blocks[bi]]:
                        po, por = bank()
                        mm_group([wor] + [RR[kc][bi] for kc in range(KC)], [por],
                                 [(po[:, 0:n], wo3[:, kc, mm * 128:(mm + 1) * 128], RA[:, kc, b0:b0 + n], kc == 0, kc == KC - 1)
                                  for kc in range(KC)])
                        S.op(dve, [por, XR[m][bi]], [XR[m][bi]], lambda e: e.tensor_tensor(
                            out=X[:, m, b0:b0 + n], in0=po[:, 0:n], in1=X[:, m, b0:b0 + n], op=ALU.add))
                    if mp == 3 and mm == 1 and bi > 0:
                        hook(bi - 1)
                if mp == 3:
                    hook(len(blocks) - 1)
                release(so)

        RAall = [RR[j][b_] for j in range(8) for b_ in range(NB)]
        xstage = [(P1[:, 0:1024], [rP1], S.new_prod("xst0")),
                  (P2[:, 0:1024], [rP2], S.new_prod("xst1")),
                  (RBf[:, 0:1024], RBall, S.new_prod("xst2")),
                  (VTf[:, 0:1024], [rVT] + rVTc[0] + rVTc[1], S.new_prod("xst3")),
                  (DIAGf[:, 0:1024], [rDIAG] + rDK, S.new_prod("xst4")),
                  (DIAG2f[:, 0:1024], rDK2, S.new_prod("xst5"))]
        xseq = [0, 1, 2, 3, 4, 5, 0, 1, 2]

        def x_src(g, tt):
            p0 = g * 1024
            if tt < 8:
                return din['xp'][p0 + tt * 128:p0 + (tt + 1) * 128, :]
            return din['xs'][:, :]

        def x_dma(g, tt, after=None):
            ap_, res_, sem_ = xstage[xseq[tt]]
            S.dma(sp, sem_, [(after or [], res_, lambda e: e.dma_start(out=ap_, in_=x_src(g, tt)))])

        def x_load(g, pre_done):
            ntile = 9 if g == 0 else 8
            if not pre_done:
                for tt in range(4):
                    x_dma(g, tt)
                pump()
                for tt in range(4, 6):
                    x_dma(g, tt, after=list(xstage[0][1]))
            for tt in range(ntile):
                ap_, res_, sem_ = xstage[xseq[tt]]
                for q in range(2):
                    pb, pr = bank()
                    S.op(pe, res_ + [rCONST], [pr], lambda e: [
                        e.transpose(out=pb[:, j * 128:(j + 1) * 128], in_=ap_[:, (q * 4 + j) * 128:(q * 4 + j + 1) * 128],
                                    identity=IDENT[:]) for j in range(4)][-1])
                    dres = [XR[c][min(tt // 4, 2)] for c in range(q * 4, q * 4 + 4)]
                    dst = X[:, q * 4:q * 4 + 4, tt * 128:(tt + 1) * 128]
                    if (tt + q) % 2 == 0:
                        S.op(act, [pr], dres, lambda e: e.activation(
                            out=dst, in_=pb[:, 0:512].rearrange("p (c t) -> p c t", c=4), func=AF.Copy))
                    else:
                        S.op(dve, [pr], dres, lambda e: e.tensor_copy(
                            out=dst, in_=pb[:, 0:512].rearrange("p (c t) -> p c t", c=4)))
                if tt + 6 < ntile:
                    x_dma(g, tt + 6)

        pt_st = {}

        def pt_dma(g, l):
            p0 = g * 1024
            for k in range(2):
                S.dma(sp, sem_xs[k], [([], [XSR[k]], lambda e: e.dma_start(
                    out=XS[k][:, :].rearrange("p (t c) -> p t c", t=4),
                    in_=din['pp'][l, p0 + k * 512:p0 + (k + 1) * 512, :].rearrange("(t p) c -> p t c", p=128)))])
            if g == 0:
                k = state['ost']
                state['ost'] = (k + 1) % 3
                pt_st['k'] = k
                S.dma(sp, sem_ost[k], [([], [OSTR[k]], lambda e: e.dma_start(out=OST[k][:, :], in_=din['psm'][l]))])

        def pt_transposes(g, l):
            ntile = 9 if g == 0 else 8
            for tt in range(ntile):
                if tt < 8:
                    src, sres = XS[tt // 4][:, (tt % 4) * 256:(tt % 4 + 1) * 256], [XSR[tt // 4]]
                else:
                    src, sres = OST[pt_st['k']][:, :], [OSTR[pt_st['k']]]
                pb, pr = bank()
                S.op(pe, sres + [rCONST], [pr], lambda e: [
                    e.transpose(out=pb[:, j * 128:(j + 1) * 128], in_=src[:, j * 128:(j + 1) * 128], identity=IDENT[:])
                    for j in range(2)][-1])
                if tt % 2 == 0:
                    S.op(act, [pr], [rPT], lambda e: e.activation(
                        out=PT[:, 0:2, tt * 128:(tt + 1) * 128], in_=pb[:, 0:256].rearrange("p (c t) -> p c t", c=2), func=AF.Copy))
                else:
                    S.op(dve, [pr], [rPT], lambda e: e.tensor_copy(
                        out=PT[:, 0:2, tt * 128:(tt + 1) * 128], in_=pb[:, 0:256].rearrange("p (c t) -> p c t", c=2)))

        def halo_dma(l):
            items = []
            for i in range(4):
                items.append(([], [rDIAG] + rDK, lambda e, i=i: e.dma_start(out=DIAGf[0:120, i * 256:(i + 1) * 256], in_=din['sta'][l, i * 120:(i + 1) * 120, :])))
            items.append(([], [rDIAG] + rDK, lambda e: e.dma_start(out=DIAGf[0:32, 1024:1280], in_=din['sts'][l])))
            for i in range(2):
                items.append(([], [rVT] + rVTc[0] + rVTc[1], lambda e, i=i: e.dma_start(out=VTf[0:120, i * 256:(i + 1) * 256], in_=din['stp'][l, i * 120:(i + 1) * 120, :])))
            S.dma(sp, sem_misc2, items)

        def halo_T(src, sres, nrows, evac):
            pb, pr = bank()
            S.op(pe, sres + [rCONST], [pr], lambda e: [
                e.transpose(out=pb[:, cc * 128:cc * 128 + nrows], in_=src[0:nrows, cc * 128:(cc + 1) * 128],
                            identity=IDENT[0:nrows, 0:nrows]) for cc in range(2)][-1])
            for cc in range(2):
                reads, writes, fn = evac(cc, pb[:, cc * 128:cc * 128 + nrows])
                S.op(act, [pr] + reads, writes, fn)

        def ple(g, l, nxt=None):
            blocks = blocks_of(g)
            rmsnorm(g, l * 4 + 3)
            wp, wpr, spp = getw('A')
            wp3 = wp[:, 0:2048].rearrange("p (k n) -> p k n", k=2)
            for mp in range(4):
                wgt, wgtr, sgt = getw('A')
                wgt3 = a3(wgt)
                hook = norm_hook(g, nxt)
                for mm, bi in m_order(mp in (0, 3), len(blocks)):
                    m = mp * 2 + mm
                    for (b0, n) in [blocks[bi]]:
                        pg, pgr = bank()
                        pq, pqr = bank()
                        mm_group([wgtr, wpr, rPT] + hr_all(bi), [pgr, pqr],
                                 [(pg[:, 0:n], wgt3[:, kc, mm * 128:(mm + 1) * 128], H[:, kc, b0:b0 + n], kc == 0, kc == KC - 1) for kc in range(KC)] +
                                 [(pq[:, 0:n], wp3[:, k2, m * 128:(m + 1) * 128], PT[:, k2, b0:b0 + n], k2 == 0, k2 == 1) for k2 in range(2)])
                        t1, t1r = tmp()
                        S.op(act, [pgr], [t1r], lambda e: e.activation(out=t1[:, 0:n], in_=pg[:, 0:n], func=AF.Sigmoid))
                        t2, t2r = tmp()
                        S.op(dve, [pqr, t1r], [t2r], lambda e: e.tensor_tensor(out=t2[:, 0:n], in0=pq[:, 0:n], in1=t1[:, 0:n], op=ALU.mult))
                        S.op(dve, [t2r, XR[m][bi]], [XR[m][bi]], lambda e: e.tensor_tensor(
                            out=X[:, m, b0:b0 + n], in0=t2[:, 0:n], in1=X[:, m, b0:b0 + n], op=ALU.add))
                    if mp == 3 and mm == 1 and bi > 0:
                        hook(bi - 1)
                if mp == 3:
                    hook(len(blocks) - 1)
                release(sgt)
            release(spp)

        def final_store(g):
            p0 = g * 1024

            def store_tiles(bi, b0, n):
                for ti in range(n // 128):
                    tt = b0 // 128 + ti
                    k = state['xs']
                    state['xs'] = (k + 1) % 2
                    for q in range(2):
                        pb, pr = bank()
                        S.op(pe, YFBs[bi % 2][1] + [rCONST], [pr], lambda e: [
                            e.transpose(out=pb[:, j * 128:(j + 1) * 128], in_=YFBs[bi % 2][0][:, q * 4 + j, ti * 128:(ti + 1) * 128],
                                        identity=IDENT[:]) for j in range(4)][-1])
                        if q == 0:
                            S.op(act, [pr], [XSR[k]], lambda e: e.activation(out=XS[k][:, 0:512], in_=pb[:, 0:512], func=AF.Copy))
                        else:
                            S.op(dve, [pr], [XSR[k]], lambda e: e.tensor_copy(out=XS[k][:, 512:1024], in_=pb[:, 0:512]))
                    if tt < 8:
                        dst = dout['y_p'][p0 + tt * 128:p0 + (tt + 1) * 128, :]
                    else:
                        dst = dout['y_s'][:, :]
                    S.dma(sp, sem_xs[k], [([XSR[k]], [], lambda e: e.dma_start(out=dst, in_=XS[k][:, :]))], is_output=True)
            rmsnorm(g, 8, final=True, tile_cb=store_tiles)

        for g in range(2):
            x_load(g, pre_done=(g == 1))
            if g == 0:
                setup_dmas()
            for l in range(L):
                pt_dma(g, l)
                if g == 0:
                    halo_dma(l)
                def mid(g=g, l=l):
                    if g == 0 and l == 0:
                        late_setup()
                        late_setup_sgu()
                    pt_transposes(g, l)
                    mixer_prep(g, l)
                ffn(g, l, 1, mid_cb=mid, nxt=(l * 4 + 1, False))
                mixers(g, l)
                ffn(g, l, 2, nxt=(l * 4 + 3, False))
                if g == 0 and l == L - 1:
                    for tt in range(6):
                        x_dma(1, tt)
                ple(g, l, nxt=((l + 1) * 4, False) if l < L - 1 else (8, True))
            final_store(g)

        assert wst['next_use'] == len(stream), (wst['next_use'], len(stream))
        done = set()
        for sem in S.out_marks:
            if id(sem) in done:
                continue
            done.add(id(sem))
            sp.eng.wait_ge(sem.h, sem.count)
    return nc


_CACHE = {}


def kernel(**inputs):
    f32 = lambda a: np.ascontiguousarray(np.asarray(a, dtype=np.float32))
    x_prompt = f32(inputs['x_prompt'])
    x_sample = f32(inputs['x_sample'])
    p_prompt = f32(inputs['p_prompt'])
    p_sample = f32(inputs['p_sample'])
    st_a = f32(inputs['state_conv_a'])
    st_p = f32(inputs['state_pool'])
    st_s = f32(inputs['state_short_conv'])
    weights = {n: f32(inputs[n]) for n in WEIGHT_NAMES}

    if 'nc' not in _CACHE:
        _CACHE['nc'] = build_program()
    nc = _CACHE['nc']
    in_maps = []
    for c in range(NCORES):
        sl = slice(c * 16, (c + 1) * 16)
        m = {
            'xp': x_prompt[c],
            'xs': x_sample[sl].reshape(128, 1024),
            'pp': np.ascontiguousarray(p_prompt[:, c]),
            'psm': np.ascontiguousarray(p_sample[:, sl].reshape(2, 128, 256)),
            'sta': np.ascontiguousarray(st_a[:, sl].reshape(2, 480, 256)),
            'stp': np.ascontiguousarray(st_p[:, sl].reshape(2, 240, 256)),
            'sts': np.ascontiguousarray(st_s[:, sl].reshape(2, 32, 256)),
        }
        m.update(weights)
        in_maps.append(m)
    res = run_bass_kernel_spmd(nc, in_maps, core_ids=list(range(NCORES)))
    R = res.results
    y_prompt = np.stack([R[c]['y_p'] for c in range(NCORES)], axis=0)
    y_sample = np.concatenate([R[c]['y_s'].reshape(16, 8, 1024) for c in range(NCORES)], axis=0)
    ca_p = np.stack([R[c]['ca_p'] for c in range(NCORES)], axis=1)
    ca_s = np.concatenate([R[c]['ca_s'].reshape(2, 16, 30, 256) for c in range(NCORES)], axis=1)
    po_p = np.stack([R[c]['po_p'] for c in range(NCORES)], axis=1)
    po_s = np.concatenate([R[c]['po_s'].reshape(2, 16, 15, 256) for c in range(NCORES)], axis=1)
    sc_p = np.stack([R[c]['sc_p'] for c in range(NCORES)], axis=1)
    sc_s = np.concatenate([R[c]['sc_s'].reshape(2, 16, 2, 256) for c in range(NCORES)], axis=1)
    cv_s = np.concatenate([R[c]['cv_s'].reshape(2, 16, 8, 256) for c in range(NCORES)], axis=1)
    outs = (y_prompt, y_sample, ca_p, ca_s, po_p, po_s, sc_p, sc_s, cv_s)
    return tuple(np.ascontiguousarray(o, dtype=np.float32) for o in outs)
```
